# Optimizing a Trainium2 kernel written in Bass

```python
import functools
import jax, jax.numpy as jnp
from jax import lax
import numpy as np

D_MODEL = 1024
BATCH = 4
SEQ = 8192
DEPTH = 1
DEC_BATCH = 16
DEC_SEQ = 16
PAST_LEN = 2048

CHUNK = 64
D_MIX = D_MODEL
D_ATT = D_MIX // 2
HEAD_DIM = 64
N_HEADS = D_ATT // HEAD_DIM
D_CONV = D_MIX - D_ATT
CONV_WIDTH = 31
CONV_STATE = CONV_WIDTH - 1
D_FF = 2816
Q_BLOCK = 128
N_MOD = 9
EPS = 1e-6
D_IN = 3 * D_ATT + N_HEADS + 2 * D_CONV

kernel_name = "fox_conformer_hybrid_stream_step"


def _rms(x, g):
    xf = x.astype(jnp.float32)
    y = xf * lax.rsqrt(jnp.mean(xf * xf, axis=-1, keepdims=True) + EPS)
    return (y * g.astype(jnp.float32)).astype(x.dtype)


def _layernorm(x, g, b):
    xf = x.astype(jnp.float32)
    mu = jnp.mean(xf, axis=-1, keepdims=True)
    var = jnp.mean(jnp.square(xf - mu), axis=-1, keepdims=True)
    return ((xf - mu) * lax.rsqrt(var + EPS) * g.astype(jnp.float32) + b.astype(jnp.float32)).astype(x.dtype)


def _modulate(x, g, shift, scale):
    return _rms(x, g) * (1 + scale[:, None, :]) + shift[:, None, :]


def _adaln(c, w_ada, b_ada):
    return jnp.split(jax.nn.silu(c) @ w_ada + b_ada, N_MOD, axis=-1)


def _swiglu(h, w_up, w_down):
    a, b = jnp.split(h @ w_up, 2, axis=-1)
    return (jax.nn.silu(a) * b) @ w_down


def _project_in(h, w_in, b_f, g_q, g_k):
    B, T = h.shape[:2]
    z = h @ w_in
    q, k, v, fl, u = jnp.split(z, [D_ATT, 2 * D_ATT, 3 * D_ATT, 3 * D_ATT + N_HEADS], axis=-1)
    q = _rms(q.reshape(B, T, N_HEADS, HEAD_DIM), g_q)
    k = _rms(k.reshape(B, T, N_HEADS, HEAD_DIM), g_k)
    v = v.reshape(B, T, N_HEADS, HEAD_DIM)
    logf = jax.nn.log_sigmoid((fl + b_f).astype(jnp.float32))
    a, gt = jnp.split(u, 2, axis=-1)
    u = a * jax.nn.sigmoid(gt)
    return q, k, v, logf, u


def _fox_attend(q, F_q, k, v, F_k, q_pos, k_pos):
    s = jnp.einsum('bqhd,bkhd->bhqk', q, k, preferred_element_type=jnp.float32) * (HEAD_DIM ** -0.5)
    s = s + (jnp.swapaxes(F_q, 1, 2)[..., :, None] - jnp.swapaxes(F_k, 1, 2)[..., None, :])
    s = jnp.where(k_pos[None, :] <= q_pos[:, None], s, -jnp.inf)
    p = jax.nn.softmax(s, axis=-1)
    return jnp.einsum('bhqk,bkhd->bqhd', p.astype(v.dtype), v)


def _fox_prompt(q, k, v, logf):
    B, T = q.shape[:2]
    F = jnp.cumsum(logf, axis=1)
    nb = T // Q_BLOCK
    qb = q.reshape(B, nb, Q_BLOCK, N_HEADS, HEAD_DIM).swapaxes(0, 1)
    Fb = F.reshape(B, nb, Q_BLOCK, N_HEADS).swapaxes(0, 1)
    k_pos = jnp.arange(T)

    def block(args):
        qi, Fi, i = args
        q_pos = i * Q_BLOCK + jnp.arange(Q_BLOCK)
        return _fox_attend(qi, Fi, k, v, F, q_pos, k_pos)

    o = lax.map(block, (qb, Fb, jnp.arange(nb)))
    return o.swapaxes(0, 1).reshape(B, T, N_HEADS, HEAD_DIM)


def _fox_sample(q, k, v, logf, cache_k, cache_v, cache_logf):
    L, T = cache_k.shape[1], q.shape[1]
    kk = jnp.concatenate([cache_k.astype(k.dtype), k], axis=1)
    vv = jnp.concatenate([cache_v.astype(v.dtype), v], axis=1)
    F = jnp.cumsum(jnp.concatenate([cache_logf.astype(jnp.float32), logf], axis=1), axis=1)
    q_pos = L + jnp.arange(T)
    k_pos = jnp.arange(L + T)
    return _fox_attend(q, F[:, L:], kk, vv, F, q_pos, k_pos)


def _dwconv(u_pad, conv_w, conv_b):
    y = lax.conv_general_dilated(u_pad, conv_w[:, None, :].astype(u_pad.dtype), window_strides=(1,),
                                 padding='VALID', dimension_numbers=('NWC', 'WIO', 'NWC'),
                                 feature_group_count=D_CONV)
    return y + conv_b


def _mix_out(o, u_pad, conv_w, conv_b, conv_ln_g, conv_ln_b, w_out):
    B, T = o.shape[:2]
    y = jax.nn.silu(_layernorm(_dwconv(u_pad, conv_w, conv_b), conv_ln_g, conv_ln_b))
    return jnp.concatenate([o.reshape(B, T, D_ATT), y], axis=-1) @ w_out


def _mixer_prompt(h, w_in, b_f, g_q, g_k, conv_w, conv_b, conv_ln_g, conv_ln_b, w_out):
    q, k, v, logf, u = _project_in(h, w_in, b_f, g_q, g_k)
    o = _fox_prompt(q, k, v, logf)
    u_pad = jnp.concatenate([jnp.zeros((u.shape[0], CONV_STATE, D_CONV), u.dtype), u], axis=1)
    out = _mix_out(o, u_pad, conv_w, conv_b, conv_ln_g, conv_ln_b, w_out)
    return out, (k, v, logf, u_pad[:, -CONV_STATE:])


def _mixer_sample(h, cache_k, cache_v, cache_logf, state_conv, w_in, b_f, g_q, g_k,
                  conv_w, conv_b, conv_ln_g, conv_ln_b, w_out):
    q, k, v, logf, u = _project_in(h, w_in, b_f, g_q, g_k)
    o = _fox_sample(q, k, v, logf, cache_k, cache_v, cache_logf)
    u_pad = jnp.concatenate([state_conv.astype(u.dtype), u], axis=1)
    out = _mix_out(o, u_pad, conv_w, conv_b, conv_ln_g, conv_ln_b, w_out)
    return out, (k, v, logf, u_pad[:, -CONV_STATE:])


def _layer(x, c, mixer_fn, w_ada, b_ada, g_ffn1, w_up1, w_down1, g_mix, g_ffn2, w_up2, w_down2, g_final):
    sh1, sc1, gt1, sh2, sc2, gt2, sh3, sc3, gt3 = _adaln(c, w_ada, b_ada)
    x = x + 0.5 * gt1[:, None, :] * _swiglu(_modulate(x, g_ffn1, sh1, sc1), w_up1, w_down1)
    m, states = mixer_fn(_modulate(x, g_mix, sh2, sc2))
    x = x + gt2[:, None, :] * m
    x = x + 0.5 * gt3[:, None, :] * _swiglu(_modulate(x, g_ffn2, sh3, sc3), w_up2, w_down2)
    return _rms(x, g_final), states


def setup_inputs(seed: int = 0) -> dict:
    key = jax.random.key(seed)
    ks = jax.random.split(key, 32)
    f32 = jnp.float32
    nrm = lambda k, shape, s: jax.random.normal(k, shape, f32) * s
    gain = lambda k, shape: 1.0 + 0.02 * jax.random.normal(k, shape, f32)
    return {
        "x_prompt": nrm(ks[0], (BATCH, SEQ, D_MODEL), 1.0),
        "x_sample": nrm(ks[1], (DEC_BATCH, DEC_SEQ, D_MODEL), 1.0),
        "c_prompt": nrm(ks[2], (BATCH, D_MODEL), 1.0),
        "c_sample": nrm(ks[3], (DEC_BATCH, D_MODEL), 1.0),
        "cache_k": nrm(ks[4], (DEPTH, DEC_BATCH, PAST_LEN, N_HEADS, HEAD_DIM), 1.0),
        "cache_v": nrm(ks[5], (DEPTH, DEC_BATCH, PAST_LEN, N_HEADS, HEAD_DIM), 1.0),
        "cache_logf": jax.nn.log_sigmoid(4.0 + jax.random.normal(ks[6], (DEPTH, DEC_BATCH, PAST_LEN, N_HEADS), f32)),
        "state_conv": nrm(ks[7], (DEPTH, DEC_BATCH, CONV_STATE, D_CONV), 0.5),
        "w_ada": nrm(ks[8], (DEPTH, D_MODEL, N_MOD * D_MODEL), 0.5 * D_MODEL ** -0.5),
        "b_ada": nrm(ks[9], (DEPTH, N_MOD * D_MODEL), 0.02),
        "g_ffn1": gain(ks[10], (DEPTH, D_MODEL)),
        "w_up1": nrm(ks[11], (DEPTH, D_MODEL, 2 * D_FF), D_MODEL ** -0.5),
        "w_down1": nrm(ks[12], (DEPTH, D_FF, D_MODEL), D_FF ** -0.5),
        "g_mix": gain(ks[13], (DEPTH, D_MODEL)),
        "w_in": nrm(ks[14], (DEPTH, D_MODEL, D_IN), D_MODEL ** -0.5),
        "b_f": 2.0 + 4.0 * jax.random.uniform(ks[15], (DEPTH, N_HEADS), f32),
        "g_q": gain(ks[16], (DEPTH, HEAD_DIM)),
        "g_k": gain(ks[17], (DEPTH, HEAD_DIM)),
        "conv_w": nrm(ks[18], (DEPTH, CONV_WIDTH, D_CONV), CONV_WIDTH ** -0.5),
        "conv_b": nrm(ks[19], (DEPTH, D_CONV), 0.02),
        "conv_ln_g": gain(ks[20], (DEPTH, D_CONV)),
        "conv_ln_b": nrm(ks[21], (DEPTH, D_CONV), 0.02),
        "w_out": nrm(ks[22], (DEPTH, D_MIX, D_MODEL), D_MIX ** -0.5),
        "g_ffn2": gain(ks[23], (DEPTH, D_MODEL)),
        "w_up2": nrm(ks[24], (DEPTH, D_MODEL, 2 * D_FF), D_MODEL ** -0.5),
        "w_down2": nrm(ks[25], (DEPTH, D_FF, D_MODEL), D_FF ** -0.5),
        "g_final": gain(ks[26], (DEPTH, D_MODEL)),
    }


def reference(x_prompt, x_sample, c_prompt, c_sample, cache_k, cache_v, cache_logf, state_conv,
              w_ada, b_ada, g_ffn1, w_up1, w_down1, g_mix, w_in, b_f, g_q, g_k,
              conv_w, conv_b, conv_ln_g, conv_ln_b, w_out, g_ffn2, w_up2, w_down2, g_final):
    xp, xs = x_prompt, x_sample
    kp, vp, fp, cp, ksm, vsm, fsm, csm = [], [], [], [], [], [], [], []
    for l in range(DEPTH):
        mix_w = (w_in[l], b_f[l], g_q[l], g_k[l], conv_w[l], conv_b[l], conv_ln_g[l], conv_ln_b[l], w_out[l])
        layer_w = (w_ada[l], b_ada[l], g_ffn1[l], w_up1[l], w_down1[l], g_mix[l],
                   g_ffn2[l], w_up2[l], w_down2[l], g_final[l])
        prompt_mixer = functools.partial(_mixer_prompt, w_in=mix_w[0], b_f=mix_w[1], g_q=mix_w[2], g_k=mix_w[3],
                                         conv_w=mix_w[4], conv_b=mix_w[5], conv_ln_g=mix_w[6],
                                         conv_ln_b=mix_w[7], w_out=mix_w[8])
        sample_mixer = functools.partial(_mixer_sample, cache_k=cache_k[l], cache_v=cache_v[l],
                                         cache_logf=cache_logf[l], state_conv=state_conv[l],
                                         w_in=mix_w[0], b_f=mix_w[1], g_q=mix_w[2], g_k=mix_w[3],
                                         conv_w=mix_w[4], conv_b=mix_w[5], conv_ln_g=mix_w[6],
                                         conv_ln_b=mix_w[7], w_out=mix_w[8])
        xp, (k1, v1, f1, c1) = _layer(xp, c_prompt, prompt_mixer, *layer_w)
        xs, (k2, v2, f2, c2) = _layer(xs, c_sample, sample_mixer, *layer_w)
        kp.append(k1); vp.append(v1); fp.append(f1); cp.append(c1)
        ksm.append(k2); vsm.append(v2); fsm.append(f2); csm.append(c2)
    k_prompt, v_prompt = jnp.stack(kp, 0), jnp.stack(vp, 0)
    logf_prompt, conv_prompt = jnp.stack(fp, 0), jnp.stack(cp, 0)
    k_sample, v_sample = jnp.stack(ksm, 0), jnp.stack(vsm, 0)
    logf_sample, conv_sample = jnp.stack(fsm, 0), jnp.stack(csm, 0)
    return (xp, xs, k_prompt, v_prompt, logf_prompt, conv_prompt, k_sample, v_sample, logf_sample, conv_sample)
```

```python
import contextlib
import numpy as np
import ml_dtypes
import concourse.bass as bass
import concourse.mybir as mybir
from concourse.bass_utils import run_bass_kernel_spmd

F32 = mybir.dt.float32
BF16 = mybir.dt.bfloat16
AF = mybir.ActivationFunctionType
ALU = mybir.AluOpType
AX = mybir.AxisListType

COMPUTE = ("pe", "act", "dve", "pool")
NDMA_SEMS = 32
NSW_SEMS = 8

D = 1024
DFF = 2816
NF = DFF // 128
NH = 8
DH = 64
DIN = 2568
T = 512
CW = 31
EPS = 1e-6
NEGBIG = -30000.0
WUP_GROUPS = ((0, 4), (4, 10), (10, 16), (16, 22))


class Sched:
    def __init__(self, nc, same_engine_sync=True):
        self.nc = nc
        self.same_engine_sync = same_engine_sync
        self.queues = {e: [] for e in ("pe", "act", "dve", "pool", "sp")}
        self.count = {e: 0 for e in COMPUTE}
        self.dma_n = 0
        self.dma_sw = 0
        self.dma_slot_last = [0] * NDMA_SEMS
        self.last_write = {}
        self.reads = {}
        self.waited = {e: {} for e in self.queues}
        self.out_dma = []
        self.needed = {e: set() for e in COMPUTE}
        self.pending = {e: {} for e in self.queues}

    def _deps(self, eng, reads, writes):
        need = dict(self.pending[eng])
        self.pending[eng] = {}

        def add(tok):
            if tok is None:
                return
            k, v = tok
            if need.get(k, 0) < v:
                need[k] = v

        for r in reads:
            add(self.last_write.get(r))
        for w in writes:
            add(self.last_write.get(w))
            for t in self.reads.get(w, ()):
                add(t)
        out = {}
        for k, v in need.items():
            if k == eng and (eng == "pe" or not self.same_engine_sync):
                continue
            if self.waited[eng].get(k, 0) >= v:
                continue
            self.waited[eng][k] = v
            out[k] = v
            if not isinstance(k, tuple):
                self.needed[k].add(v)
        return out

    def _commit(self, tok, reads, writes):
        for r in reads:
            self.reads.setdefault(r, []).append(tok)
        for w in writes:
            self.last_write[w] = tok
            self.reads[w] = []

    def op(self, eng, fn, reads=(), writes=()):
        waits = self._deps(eng, reads, writes)
        self.count[eng] += 1
        tok = (eng, self.count[eng])
        self.queues[eng].append((waits, fn, tok))
        self._commit(tok, reads, writes)
        return tok

    def dma(self, q, fn, reads=(), writes=(), is_output=False):
        if q == "pool":
            slot = NDMA_SEMS - NSW_SEMS + self.dma_sw % NSW_SEMS
            self.dma_sw += 1
        else:
            slot = self.dma_n % (NDMA_SEMS - NSW_SEMS)
            self.dma_n += 1
        key = ("dma", slot)
        waits = self._deps(q, reads, writes)
        prev = self.dma_slot_last[slot]
        if prev and self.waited[q].get(key, 0) < prev:
            waits[key] = max(waits.get(key, 0), prev)
            self.waited[q][key] = prev
        val = prev + 16
        self.dma_slot_last[slot] = val
        tok = (key, val)
        self.queues[q].append((waits, fn, tok))
        self._commit(tok, reads, writes)
        if is_output:
            self.out_dma.append(tok)
        return tok

    def barrier(self):
        allw = {}
        for e in COMPUTE:
            if self.count[e]:
                allw[e] = self.count[e]
        for i in range(NDMA_SEMS):
            if self.dma_slot_last[i]:
                allw[("dma", i)] = self.dma_slot_last[i]
        for e in self.queues:
            p = self.pending[e]
            for k, v in allw.items():
                if p.get(k, 0) < v:
                    p[k] = v
        self.last_write = {}
        self.reads = {}

    def emit(self):
        nc = self.nc
        with contextlib.ExitStack() as es:
            sems = {}
            for e in COMPUTE:
                sems[e] = es.enter_context(nc.semaphore("s_" + e))
            for i in range(NDMA_SEMS):
                sems[("dma", i)] = es.enter_context(nc.semaphore("s_dma%d" % i))
            final_waits = {}
            for i in range(NDMA_SEMS):
                if self.dma_slot_last[i]:
                    final_waits[("dma", i)] = self.dma_slot_last[i]
            for e in COMPUTE:
                if self.count[e]:
                    self.needed[e].add(self.count[e])
                    final_waits[e] = self.count[e]
            rank = {}
            for e in COMPUTE:
                for i, v in enumerate(sorted(self.needed[e])):
                    rank[(e, v)] = i + 1
            block = es.enter_context(nc.Block())
            queues = self.queues

            def semval(k, v):
                return v if isinstance(k, tuple) else rank[(k, v)]

            def run(engobj, q, extra_final=False):
                for waits, fn, tok in queues[q]:
                    for k, v in waits.items():
                        engobj.wait_ge(sems[k], semval(k, v))
                    ins = fn(engobj)
                    k, v = tok
                    if isinstance(k, tuple):
                        ins.then_inc(sems[k], 16)
                    elif (k, v) in rank:
                        ins.then_inc(sems[k], 1)
                if extra_final:
                    for k, v in final_waits.items():
                        engobj.wait_ge(sems[k], semval(k, v))

            @block.sync
            def _(e):
                run(e, "sp", extra_final=True)

            @block.tensor
            def _(e):
                run(e, "pe")

            @block.scalar
            def _(e):
                run(e, "act")

            @block.vector
            def _(e):
                run(e, "dve")

            @block.gpsimd
            def _(e):
                run(e, "pool")


class Arena:
    def __init__(self, ap, nwords):
        self.ap = ap
        self.n = nwords
        self.off = 0
        self.mark = 0

    def set_mark(self):
        self.mark = self.off

    def reset(self):
        self.off = self.mark

    def f32(self, n):
        assert self.off + n <= self.n, ("arena overflow", self.off, n, self.n)
        a = self.ap[:, self.off:self.off + n]
        self.off += n
        return a

    def bf16(self, n):
        w = (n + 1) // 2
        assert self.off + w <= self.n, ("arena overflow", self.off, w, self.n)
        a = self.ap[:, self.off:self.off + w].bitcast(BF16)
        self.off += w
        return a[:, 0:n]


def build(NTO, PAST, phases="0ABCD"):
    NTA = 2 * NTO
    TP = NTA * T
    TO = NTO * T
    NPB = PAST // 128
    nc = bass.Bass("TRN2", target_bir_lowering=False)

    def din(name, shape, dt=F32):
        return nc.dram_tensor(name, list(shape), dt, kind="ExternalInput").ap()

    def dout(name, shape, dt=F32):
        return nc.dram_tensor(name, list(shape), dt, kind="ExternalOutput").ap()

    def dscr(name, shape, dt=F32):
        return nc.dram_tensor(name, list(shape), dt).ap()

    xp = din("xp", [TP, D]); xs = din("xs", [32, D]); cvec = din("cvec", [3, D])
    ck = din("ck", [2, PAST, 512]); cv = din("cv", [2, PAST, 512]); clf = din("clf", [2, PAST, NH])
    sconv = din("sconv", [2, 30, 512]); flagc = din("flagc", [128, 3 * NTO]); lmat = din("lmat", [NTA, NTA])
    w_ada = din("w_ada", [D, 9 * D]); b_ada = din("b_ada", [1, 9 * D])
    g3 = din("g3", [3, D])
    w_up = [din("w_up1", [D, 2 * DFF]), din("w_up2", [D, 2 * DFF])]
    w_dn = [din("w_down1", [DFF, D]), din("w_down2", [DFF, D])]
    w_in = din("w_in", [D, DIN]); b_f = din("b_f", [1, NH]); g_q = din("g_q", [1, DH]); g_k = din("g_k", [1, DH])
    conv_w = din("conv_w", [CW, 512]); cvec3 = din("cvec3", [3, 512])
    w_out = din("w_out", [D, D]); g_final = din("g_final", [1, D])

    yp = dout("yp", [TO, D]); ys = dout("ys", [32, D])
    kp = dout("kp", [TP, 512]); vp = dout("vp", [TP, 512]); lfp = dout("lfp", [TP, NH]); cvp = dout("cvp", [30, 512])
    ks = dout("ks", [32, 512]); vs = dout("vs", [32, 512]); lfs = dout("lfs", [32, NH]); cvs = dout("cvs", [2, 30, 512])

    TPX = TP + 32
    x1d = dscr("x1d", [TPX, D]); x2d = dscr("x2d", [TO + 32, D])
    kTd = dscr("kTd", [512, TPX], BF16); uTd = dscr("uTd", [512, TPX], BF16); qTd = dscr("qTd", [512, TO + 32], BF16)
    vSd = dscr("vSd", [TPX, 512], BF16); cumFd = dscr("cumFd", [TP, NH]); modrows = dscr("modrows", [3, 9 * D])

    S = Sched(nc)
    es = contextlib.ExitStack()
    with es:
        es.enter_context(nc.allow_low_precision("bf16 matmul operands, fp32 accumulate"))
        es.enter_context(nc.allow_non_contiguous_dma("small strided loads"))
        NW = 51200
        arena_t = es.enter_context(nc.sbuf_tensor("arena", [128, NW], F32))
        AR = Arena(arena_t, NW)
        PS = [es.enter_context(nc.psum_tensor("ps%d" % i, [128, 512], F32)) for i in range(8)]
        ps_ctr = [0]

        reserved = set()

        def psb():
            while True:
                i = ps_ctr[0] % 8
                ps_ctr[0] += 1
                if i not in reserved:
                    return PS[i], ("ps", i)

        def dma(q, out, in_, r=(), w=(), is_output=False):
            S.dma(q, lambda e: e.dma_start(out=out, in_=in_), reads=r, writes=w, is_output=is_output)

        def mm(out, lhsT, rhs, start, stop, r, w):
            S.op("pe", lambda e: e.matmul(out, lhsT=lhsT, rhs=rhs, start=start, stop=stop), reads=r, writes=w)

        def act(out, in_, func, r, w, bias=None, scale=None, accum_out=None):
            kw = {}
            if bias is not None:
                kw["bias"] = bias
            if scale is not None:
                kw["scale"] = scale
            if accum_out is not None:
                kw["accum_out"] = accum_out
            S.op("act", lambda e: e.activation(out=out, in_=in_, func=func, **kw), reads=r, writes=w)

        def tt(eng, out, in0, in1, op, r, w):
            S.op(eng, lambda e: e.tensor_tensor(out=out, in0=in0, in1=in1, op=op), reads=r, writes=w)

        def ts(eng, out, in0, s1, s2, op0, op1, r, w):
            if op1 is None:
                S.op(eng, lambda e: e.tensor_scalar(out=out, in0=in0, scalar1=s1, scalar2=s2, op0=op0), reads=r, writes=w)
            else:
                S.op(eng, lambda e: e.tensor_scalar(out=out, in0=in0, scalar1=s1, scalar2=s2, op0=op0, op1=op1), reads=r, writes=w)

        def cp(eng, out, in_, r, w):
            if eng == "act":
                S.op("act", lambda e: e.copy(out=out, in_=in_), reads=r, writes=w)
            else:
                S.op(eng, lambda e: e.tensor_copy(out=out, in_=in_), reads=r, writes=w)

        def memset(eng, ap, val, w):
            S.op(eng, lambda e: e.memset(ap, val), writes=w)

        def recip(out, in_, r, w):
            S.op("dve", lambda e: e.reciprocal(out=out, in_=in_), reads=r, writes=w)

        def rsqrt_small(out, in_, scale, r, w):
            act(out, in_, AF.Sqrt, r, w, bias=epsb[0:out.shape[0], 0:1], scale=scale)
            recip(out, out, w, w)

        ident = AR.bf16(128)
        identf = AR.f32(128)
        UT = AR.f32(128)
        onesf = AR.f32(128)
        masks = AR.bf16(4 * 512).rearrange("p (o t) -> p o t", o=4)
        modcol = AR.f32(72 * 3).rearrange("p (j s) -> p j s", s=3)
        gcol = AR.f32(8 * 3).rearrange("p (k s) -> p k s", s=3)
        gsc = [AR.f32(8 * 3).rearrange("p (k s) -> p k s", s=3) for _ in range(3)]
        ccol = AR.f32(4 * 3).rearrange("p (k s) -> p k s", s=3)
        wcol = AR.f32(4 * CW).rearrange("p (k j) -> p k j", k=4)
        flag_t = AR.f32(3 * NTO)
        bf_t = AR.f32(NH)
        gq_t = AR.f32(512)
        gk_t = AR.f32(512)
        cshift = AR.f32(1)
        epsb = AR.f32(1)
        accF = AR.f32(NH)
        AR.set_mark()

        memset("pool", identf, 0.0, ["identf"])
        S.op("pool", lambda e: e.affine_select(out=identf, in_=identf, pattern=[[-1, 128]], compare_op=ALU.not_equal,
                                               fill=1.0, base=0, channel_multiplier=1), reads=["identf"], writes=["identf"])
        cp("dve", ident, identf, ["identf"], ["ident"])
        memset("pool", UT, 1.0, ["UT"])
        S.op("pool", lambda e: e.affine_select(out=UT, in_=UT, pattern=[[1, 128]], compare_op=ALU.is_ge,
                                               fill=0.0, base=0, channel_multiplier=-1), reads=["UT"], writes=["UT"])
        memset("dve", onesf, 1.0, ["onesf"])
        memset("dve", epsb, EPS, ["epsb"])
        memset("dve", accF, 0.0, ["accF"])
        mtmp = AR.f32(512)
        for o in range(4):
            memset("pool", mtmp, 1.0, ["mtmp"])
            S.op("pool", lambda e, o=o: e.affine_select(out=mtmp, in_=mtmp, pattern=[[1, 512]], compare_op=ALU.is_ge,
                                                        fill=0.0, base=-128 * o, channel_multiplier=-1),
                 reads=["mtmp"], writes=["mtmp"])
            cp("pool", masks[:, o, :], mtmp, ["mtmp"], ["masks"])
        dma("sp", flag_t, flagc[:, :], w=["flag"])
        dma("sp", bf_t, b_f[0:1, :].partition_broadcast(128).rearrange("p a n -> p (a n)"), w=["bf_t"])
        gq64 = AR.f32(64)
        gk64 = AR.f32(64)
        dma("sp", gq64, g_q[0:1, :].partition_broadcast(128).rearrange("p a n -> p (a n)"), w=["gq64"])
        dma("sp", gk64, g_k[0:1, :].partition_broadcast(128).rearrange("p a n -> p (a n)"), w=["gk64"])
        cp("dve", gq_t.rearrange("p (h d) -> p h d", h=NH), gq64.unsqueeze(1).to_broadcast([128, NH, DH]), ["gq64"], ["gq_t"])
        cp("dve", gk_t.rearrange("p (h d) -> p h d", h=NH), gk64.unsqueeze(1).to_broadcast([128, NH, DH]), ["gk64"], ["gk_t"])
        gcolqk = AR.f32(2)
        AR.set_mark()
        wup_pre = AR.bf16(8 * 2 * DFF).rearrange("p (k n) -> p k n", k=8)
        if "A" in phases:
            wup_v0 = w_up[0].rearrange("(k p) n -> p k n", p=128)
            for gi, (f0, f1) in enumerate(WUP_GROUPS):
                for half in range(2):
                    c0, c1 = half * DFF + f0 * 128, half * DFF + f1 * 128
                    dma("pool", wup_pre[:, :, c0:c1], wup_v0[:, :, c0:c1], w=[("wup", gi)])
        grow2 = AR.f32(128)
        for j_ in range(2):
            dma("sp", grow2[0:1, 64 * j_:64 * j_ + 64], g_q[0:1, :], w=["grow2q"])
            dma("sp", grow2[32:33, 64 * j_:64 * j_ + 64], g_k[0:1, :], w=["grow2k"])
        pgq, pkgq = psb()
        mm(pgq[:, 0:1], grow2[0:1, :], identf[0:1, 0:1], True, True, ["grow2q", "identf"], [pkgq])
        mm(pgq[:, 1:2], grow2[32:33, :], onesf[32:33, 0:1], True, True, ["grow2k", "onesf"], [pkgq])
        cp("dve", gcolqk, pgq[:, 0:2], [pkgq], ["gcolqk"])
        mq = AR.f32(2)
        S.op("dve", lambda e: e.tensor_reduce(out=mq[:, 0:1], in_=gq64, axis=AX.X, op=ALU.max, apply_absolute_value=True),
             reads=["gq64"], writes=["mq"])
        S.op("dve", lambda e: e.tensor_reduce(out=mq[:, 1:2], in_=gk64, axis=AX.X, op=ALU.max, apply_absolute_value=True),
             reads=["gk64"], writes=["mq"])
        tt("dve", cshift, mq[:, 0:1], mq[:, 1:2], ALU.mult, ["mq"], ["cshift"])
        ts("dve", cshift, cshift, 8.0, None, ALU.mult, None, ["cshift"], ["cshift"])

        crow = AR.f32(D)
        cT = AR.bf16(8 * 3).rearrange("p (k s) -> p k s", s=3)
        mrow = AR.f32(9 * D)
        grow = AR.f32(D)
        c3row = AR.f32(512)
        cwrow = AR.f32(512)
        dma("sp", crow[0:3, :], cvec[:, :], w=["crow"])
        dma("sp", grow[0:3, :], g3[:, :], w=["grow"])
        dma("sp", c3row[0:3, :], cvec3[:, :], w=["c3row"])
        dma("sp", cwrow[0:CW, :], conv_w[:, :], w=["cwrow"])
        pst, pk = psb()
        for kc in range(8):
            mm(pst[:, kc * 3:kc * 3 + 3], crow[0:3, kc * 128:(kc + 1) * 128], identf[0:3, 0:3], True, True, ["crow", "identf"], [pk])
        act(cT, pst[:, 0:24].rearrange("p (k s) -> p k s", s=3), AF.Silu, [pk], ["cT"])
        pst, pk = psb()
        for kc in range(8):
            mm(pst[:, kc * 3:kc * 3 + 3], grow[0:3, kc * 128:(kc + 1) * 128], identf[0:3, 0:3], True, True, ["grow", "identf"], [pk])
        cp("dve", gcol, pst[:, 0:24].rearrange("p (k s) -> p k s", s=3), [pk], ["gcol"])
        pst, pk = psb()
        for kc in range(4):
            mm(pst[:, kc * 3:kc * 3 + 3], c3row[0:3, kc * 128:(kc + 1) * 128], identf[0:3, 0:3], True, True, ["c3row", "identf"], [pk])
        cp("dve", ccol, pst[:, 0:12].rearrange("p (k s) -> p k s", s=3), [pk], ["ccol"])
        pst, pk = psb()
        for kc in range(4):
            mm(pst[:, kc * CW:(kc + 1) * CW], cwrow[0:CW, kc * 128:(kc + 1) * 128], identf[0:CW, 0:CW], True, True, ["cwrow", "identf"], [pk])
        cp("dve", wcol, pst[:, 0:4 * CW].rearrange("p (k j) -> p k j", k=4), [pk], ["wcol"])

        wa_ring = [AR.bf16(8 * 512).rearrange("p (k n) -> p k n", k=8) for _ in range(3)]
        bch = [AR.f32(512) for _ in range(2)]
        w_ada_v = w_ada.rearrange("(k p) n -> p k n", p=128)
        for n in range(18):
            wr = wa_ring[n % 3]
            dma("pool", wr, w_ada_v[:, :, n * 512:(n + 1) * 512], w=[("wa", n % 3)])
            bb = bch[n % 2]
            dma("sp", bb[0:3, :], b_ada[0:1, n * 512:(n + 1) * 512].partition_broadcast(3).rearrange("p a n -> p (a n)"), w=[("bch", n % 2)])
            pst, pk = psb()
            for kc in range(8):
                mm(pst[0:3, :], cT[:, kc, :], wr[:, kc, :], kc == 0, kc == 7, ["cT", ("wa", n % 3)], [pk])
            tt("dve", mrow[0:3, n * 512:(n + 1) * 512], pst[0:3, :], bb[0:3, :], ALU.add, [pk, ("bch", n % 2)], ["mrow"])
        dma("sp", modrows[:, :], mrow[0:3, :], r=["mrow"], w=["modrows"])
        for jj in range(0, 72, 24):
            pst, pk = psb()
            for j in range(jj, jj + 24):
                mm(pst[:, (j - jj) * 3:(j - jj) * 3 + 3], mrow[0:3, j * 128:(j + 1) * 128], identf[0:3, 0:3], True, True, ["mrow", "identf"], [pk])
            cp("dve", modcol[:, jj:jj + 24, :], pst[:, 0:72].rearrange("p (j s) -> p j s", s=3), [pk], ["modcol"])
        for n in range(3):
            ts("dve", gsc[n], modcol[:, (3 * n + 1) * 8:(3 * n + 2) * 8, :], 1.0, None, ALU.add, None, ["modcol"], [("gsc", n)])
            tt("dve", gsc[n], gsc[n], gcol[:, :, n:n + 1].to_broadcast([128, 8, 3]), ALU.mult, [("gsc", n), "gcol"], [("gsc", n)])
        S.barrier()
        AR.reset()

        def shcol(n, kc, s):
            return modcol[:, 3 * n * 8 + kc, s:s + 1]

        def load_gate(gt, n, sample):
            c0 = (3 * n + 2) * D
            if not sample:
                dma("sp", gt, modrows[0:1, c0:c0 + D].partition_broadcast(128).rearrange("p a n -> p (a n)"), r=["modrows"], w=["gate"])
            else:
                for j in range(2):
                    dma("sp", gt[16 * j:16 * j + 16, :], modrows[1 + j:2 + j, c0:c0 + D].partition_broadcast(16).rearrange("p a n -> p (a n)"),
                        r=["modrows"], w=["gate"])

        def normA(xt, xk, nb, np_, scr):
            junk, ss, xn = scr
            for b in range(nb):
                act(junk[0:np_, :], xt[0:np_, b, :], AF.Square, [xk], ["junk", "ss"], accum_out=ss[0:np_, b:b + 1])
            rsqrt_small(ss[0:np_, 0:nb], ss[0:np_, 0:nb], 1.0 / D, ["ss"], ["ss"])
            for b in range(nb):
                act(xn[0:np_, b, :], xt[0:np_, b, :], AF.Identity, [xk, "ss"], [("xn", b)], scale=ss[0:np_, b:b + 1])

        def transp(nb, np_, n, sample, hT, scr, hkey="hT"):
            junk, ss, xn = scr
            ntok = (nb - 1) * 128 + np_
            segs = [(0, ntok, 0)] if not sample else [(0, 16, 1), (16, 32, 2)]
            for kc in range(8):
                pst, pk = psb()
                for b in range(nb):
                    mm(pst[:, b * 128:b * 128 + np_], xn[0:np_, b, kc * 128:(kc + 1) * 128], ident[0:np_, 0:np_], True, True,
                       [("xn", b), "ident"], [pk])
                for (c0, c1, s) in segs:
                    if kc % 2 == 0:
                        act(hT[:, kc, c0:c1], pst[:, c0:c1], AF.Identity, [pk], [hkey], bias=shcol(n, kc, s), scale=gsc[n][:, kc, s:s + 1])
                    else:
                        ts("dve", hT[:, kc, c0:c1], pst[:, c0:c1], gsc[n][:, kc, s:s + 1], shcol(n, kc, s), ALU.mult, ALU.add, [pk], [hkey])
            return ntok

        def ffn_phase(which, tiles, final_norm):
            n = 0 if which == 0 else 2
            AR.reset()
            wup = AR.bf16(8 * 2 * DFF).rearrange("p (k n) -> p k n", k=8)
            XB = [AR.f32(4 * D).rearrange("p (b d) -> p b d", b=4) for _ in range(2)]
            hT = AR.bf16(8 * T).rearrange("p (k t) -> p k t", k=8)
            actT = AR.bf16(NF * T).rearrange("p (f t) -> p f t", f=NF)
            gate = AR.f32(D)
            wdr = [AR.bf16(D) for _ in range(4)]
            junk = AR.bf16(D)
            ss = AR.f32(4)
            xn = AR.bf16(4 * D).rearrange("p (b d) -> p b d", b=4)
            sg = [AR.f32(T) for _ in range(2)]
            tmp = [AR.f32(512) for _ in range(2)]
            gfin = AR.f32(D) if final_norm else None
            wup_v = w_up[which].rearrange("(k p) n -> p k n", p=128)
            wdv = w_dn[which]
            if which != 0:
                for gi, (f0, f1) in enumerate(WUP_GROUPS):
                    for half in range(2):
                        c0, c1 = half * DFF + f0 * 128, half * DFF + f1 * 128
                        dma("pool", wup[:, :, c0:c1], wup_v[:, :, c0:c1], w=[("wup", gi)])
            if final_norm:
                dma("sp", gfin, g_final[0:1, :].partition_broadcast(128).rearrange("p a n -> p (a n)"), w=["gfin"])

            def load_x(i):
                src, _, nb, np_, sample = tiles[i]
                xb = XB[i % 2]
                if sample:
                    dma("sp", xb[0:np_, 0, :], src, w=[("xb", i % 2)])
                else:
                    dma("sp", xb[:, 0:nb, :], src.rearrange("(b p) d -> p b d", p=128), w=[("xb", i % 2)])

            ss2 = AR.f32(4)
            scr = (junk, ss, xn)
            load_x(0)
            normA(XB[0], ("xb", 0), tiles[0][2], tiles[0][3], scr)
            transp(tiles[0][2], tiles[0][3], n, tiles[0][4], hT, scr)
            cur_gate = None
            for i, (src, dst, nb, np_, sample) in enumerate(tiles):
                xb = XB[i % 2]
                xk = ("xb", i % 2)
                ntok = (nb - 1) * 128 + np_
                if i + 1 < len(tiles):
                    load_x(i + 1)
                if cur_gate != sample:
                    load_gate(gate, n, sample)
                    ts("dve", gate, gate, 0.5, None, ALU.mult, None, ["gate"], ["gate"])
                    cur_gate = sample
                for f in range(4):
                    dma("pool", wdr[f], wdv[f * 128:(f + 1) * 128, :], w=[("wd", f)])
                for f in range(NF):
                    pa, pka = psb()
                    pb, pkb = psb()
                    wk = ("wup", [gi for gi, (f0, f1) in enumerate(WUP_GROUPS) if f0 <= f < f1][0])
                    for kc in range(8):
                        mm(pa[:, 0:ntok], wup[:, kc, f * 128:(f + 1) * 128], hT[:, kc, 0:ntok], kc == 0, kc == 7, ["hT", wk], [pka])
                    for kc in range(8):
                        mm(pb[:, 0:ntok], wup[:, kc, DFF + f * 128:DFF + (f + 1) * 128], hT[:, kc, 0:ntok], kc == 0, kc == 7, ["hT", wk], [pkb])
                    sgb = sg[f % 2]
                    act(sgb[:, 0:ntok], pa[:, 0:ntok], AF.Silu, [pka], [("sg", f % 2)])
                    tt("dve", actT[:, f, 0:ntok], sgb[:, 0:ntok], pb[:, 0:ntok], ALU.mult, [("sg", f % 2), pkb], [("actT", f)])
                    if f == 10 and i + 1 < len(tiles):
                        normA(XB[(i + 1) % 2], ("xb", (i + 1) % 2), tiles[i + 1][2], tiles[i + 1][3], scr)
                if i + 1 < len(tiles):
                    transp(tiles[i + 1][2], tiles[i + 1][3], n, tiles[i + 1][4], hT, scr)
                accs = {}
                for b in range(nb):
                    for h in range(2):
                        accs[(b, h)] = psb()
                for f in range(NF):
                    for b in range(nb):
                        for h in range(2):
                            pacc, pkk = accs[(b, h)]
                            mm(pacc[0:np_, :], actT[:, f, b * 128:b * 128 + np_], wdr[f % 4][:, h * 512:(h + 1) * 512], f == 0, f == NF - 1,
                               [("actT", f), ("wd", f % 4)], [pkk])
                    if f + 4 < NF:
                        dma("pool", wdr[f % 4], wdv[(f + 4) * 128:(f + 5) * 128, :], w=[("wd", f % 4)])
                k = 0
                for b in range(nb):
                    for h in range(2):
                        pacc, pkk = accs[(b, h)]
                        tb = tmp[k % 2]
                        tt("dve", tb[0:np_, :], pacc[0:np_, :], gate[0:np_, h * 512:(h + 1) * 512], ALU.mult, [pkk, "gate"], [("tmp", k % 2)])
                        tt("pool", xb[0:np_, b, h * 512:(h + 1) * 512], tb[0:np_, :], xb[0:np_, b, h * 512:(h + 1) * 512], ALU.add,
                           [("tmp", k % 2), xk], [xk])
                        k += 1
                if final_norm:
                    for b in range(nb):
                        act(junk[0:np_, :], xb[0:np_, b, :], AF.Square, [xk], ["junk", "ss2"], accum_out=ss2[0:np_, b:b + 1])
                    rsqrt_small(ss2[0:np_, 0:nb], ss2[0:np_, 0:nb], 1.0 / D, ["ss2"], ["ss2"])
                    for b in range(nb):
                        S.op("dve", lambda e, b=b, xb=xb, np_=np_: e.scalar_tensor_tensor(
                            out=xb[0:np_, b, :], in0=xb[0:np_, b, :], scalar=ss2[0:np_, b:b + 1], in1=gfin[0:np_, :],
                            op0=ALU.mult, op1=ALU.mult), reads=[xk, "ss2", "gfin"], writes=[xk])
                if sample:
                    dma("sp", dst, xb[0:np_, 0, :], r=[xk], w=[("dst", i)], is_output=final_norm)
                else:
                    dma("sp", dst.rearrange("(b p) d -> p b d", p=128), xb[:, 0:nb, :], r=[xk], w=[("dst", i)], is_output=final_norm)
            S.barrier()

        if "A" in phases:
            tilesA = [(xp[t * T:(t + 1) * T, :], x1d[t * T:(t + 1) * T, :], 4, 128, False) for t in range(NTA)]
            tilesA.append((xs[:, :], x1d[TP:TP + 32, :], 1, 32, True))
            ffn_phase(0, tilesA, False)

        def qknorm(ps_t, pk, np_, gtile, sq, sqi, ssh, kn, kni, kb, idx, want_f32):
            act(sq[0:np_, :], ps_t[0:np_, :], AF.Square, [pk], [("sq", sqi)])
            S.op("dve", lambda e: e.tensor_reduce(out=ssh[0:np_, :], in_=sq[0:np_, :].rearrange("p (h d) -> p h d", h=NH),
                                                  axis=AX.X, op=ALU.add), reads=[("sq", sqi)], writes=[("ssh", idx)])
            rsqrt_small(ssh[0:np_, :], ssh[0:np_, :], 1.0 / DH, [("ssh", idx)], [("ssh", idx)])
            tt("dve", kb[0:np_, :].rearrange("p (h d) -> p h d", h=NH), ps_t[0:np_, :].rearrange("p (h d) -> p h d", h=NH),
               ssh[0:np_, :].unsqueeze(2).to_broadcast([np_, NH, DH]), ALU.mult, [pk, ("ssh", idx)], [("kb", idx)])
            if want_f32:
                tt("dve", kn[0:np_, :].rearrange("p (h d) -> p h d", h=NH), ps_t[0:np_, :].rearrange("p (h d) -> p h d", h=NH),
                   ssh[0:np_, :].unsqueeze(2).to_broadcast([np_, NH, DH]), ALU.mult, [pk, ("ssh", idx)], [("kn", kni)])
                tt("pool", kn[0:np_, :], kn[0:np_, :], gtile[0:np_, :], ALU.mult, [("kn", kni), "gq_t", "gk_t"], [("kn", kni)])

        lfsd = dscr("lfsd", [32, NH])

        def mix_setup(emit=True):
            wout = AR.bf16(8 * D).rearrange("p (k n) -> p k n", k=8)
            diag = AR.bf16(4 * CW * 128).rearrange("p (c j m) -> p c j m", c=4, j=CW)
            if emit:
                mix_emit_wout(wout)
                for c in range(4):
                    mix_emit_diag(diag, c)
            return wout, diag

        def mix_emit_wout(wout):
            dma("pool", wout, w_out.rearrange("(k p) n -> p k n", p=128), w=["wout"])

        def mix_emit_diag(diag, c):
            for j in range(CW):
                if j % 2 == 0:
                    ts("dve", diag[:, c, j, :], identf, wcol[:, c, j:j + 1], None, ALU.mult, None, ["identf", "wcol"], [("diag", c)])
                else:
                    act(diag[:, c, j, :], identf, AF.Identity, ["identf", "wcol"], [("diag", c)], scale=wcol[:, c, j:j + 1])

        def proj_phase():
            AR.reset()
            stage_c = "C" in phases
            if stage_c:
                wout_c, diag_c = mix_setup(emit=False)
            win = AR.bf16(8 * DIN).rearrange("p (k n) -> p k n", k=8)
            XB = [AR.f32(4 * D).rearrange("p (b d) -> p b d", b=4) for _ in range(2)]
            hTs = [AR.bf16(8 * T).rearrange("p (k t) -> p k t", k=8) for _ in range(2)]
            junk = AR.bf16(D)
            ss = AR.f32(4)
            xn = AR.bf16(4 * D).rearrange("p (b d) -> p b d", b=4)
            sq = [AR.f32(512) for _ in range(2)]
            ssh = [AR.f32(NH) for _ in range(4)]
            kn = [AR.f32(512) for _ in range(2)]
            kb = [AR.bf16(512) for _ in range(4)]
            vf = [AR.f32(512) for _ in range(2)]
            vb = [AR.bf16(512) for _ in range(2)]
            kT_sb = AR.bf16(4 * T).rearrange("p (c t) -> p c t", c=4)
            qT_sb = AR.bf16(4 * T).rearrange("p (c t) -> p c t", c=4)
            uT_sb = AR.bf16(4 * T).rearrange("p (c t) -> p c t", c=4)
            sgm = [AR.f32(T) for _ in range(2)]
            lx = AR.f32(4 * NH)
            la = AR.f32(4 * NH)
            lm = AR.f32(4 * NH)
            lf = AR.f32(4 * NH).rearrange("p (b h) -> p b h", b=4)
            cumF_sb = AR.f32(4 * NH)
            utk = AR.f32(512)
            win_v = w_in.rearrange("(k p) n -> p k n", p=128)
            for kc in range(8):
                dma("pool", win[:, kc, :], win_v[:, kc, :], w=[("win", kc)])
            wink = [("win", kc) for kc in range(8)]
            kTd_v = kTd.rearrange("(c p) t -> p c t", p=128)
            qTd_v = qTd.rearrange("(c p) t -> p c t", p=128)
            uTd_v = uTd.rearrange("(c p) t -> p c t", p=128)
            PSF = PS[7]
            pkf = ("ps", 7)
            reserved.add(7)
            psb7 = psb

            tiles = [(t * T, 4, 128, False, (t - NTO) if t >= NTO else None) for t in range(NTA)]
            tiles.append((TP, 1, 32, True, NTO))

            def load_x(i):
                row0, nb, np_, sample, _ = tiles[i]
                xb = XB[i % 2]
                if sample:
                    dma("sp", xb[0:np_, 0, :], x1d[row0:row0 + np_, :], w=[("xb", i % 2)])
                else:
                    dma("sp", xb[:, 0:nb, :], x1d[row0:row0 + nb * 128, :].rearrange("(b p) d -> p b d", p=128), w=[("xb", i % 2)])

            scr = (junk, ss, xn)
            load_x(0)
            normA(XB[0], ("xb", 0), tiles[0][1], tiles[0][2], scr)
            transp(tiles[0][1], tiles[0][2], 1, tiles[0][3], hTs[0], scr, ("hT", 0))
            cnt = [0]
            pending_cs = []
            for i, (row0, nb, np_, sample, own) in enumerate(tiles):
                xb = XB[i % 2]
                xk = ("xb", i % 2)
                hT = hTs[i % 2]
                hk = ("hT", i % 2)
                ntok = (nb - 1) * 128 + np_
                if i + 1 < len(tiles):
                    load_x(i + 1)
                fo = 128 * (i % 2)
                pkf = ("psf", i % 2)
                for b in range(nb):
                    for kc in range(8):
                        mm(PSF[0:np_, fo + b * NH:fo + (b + 1) * NH], hT[:, kc, b * 128:b * 128 + np_], win[:, kc, 1536:1536 + NH], kc == 0, kc == 7,
                           [hk, ("win", kc)], [pkf])
                kout, vout = (ks, vs) if sample else (kp, vp)
                pending = []

                def flush():
                    for (part, b_, ix_) in pending:
                        ptt, pkt = psb7()
                        for c in range(4):
                            mm(ptt[:, c * 128:c * 128 + np_], kb[ix_][0:np_, c * 128:(c + 1) * 128], ident[0:np_, 0:np_], True, True,
                               [("kb", ix_), "ident"], [pkt])
                        dst_sb = kT_sb if part == "k" else qT_sb
                        act(dst_sb[:, :, b_ * 128:b_ * 128 + np_], ptt[:, :].rearrange("p (c t) -> p c t", c=4)[:, :, 0:np_], AF.Identity,
                            [pkt, "gcolqk"], [("T" + part,)], scale=gcolqk[:, (0 if part == "q" else 1):(1 if part == "q" else 2)])
                    del pending[:]

                for b in range(nb):
                    r0 = 0 if sample else row0 + b * 128
                    parts = ["k", "v"] + (["q"] if own is not None else [])
                    newp = []
                    for part in parts:
                        c0 = {"q": 0, "k": 512, "v": 1024}[part]
                        pt_, pk_ = psb7()
                        for kc in range(8):
                            mm(pt_[0:np_, :], hT[:, kc, b * 128:b * 128 + np_], win[:, kc, c0:c0 + 512], kc == 0, kc == 7,
                               [hk, ("win", kc)], [pk_])
                        if part == "v":
                            ix = cnt[0] % 2
                            cnt[0] += 1
                            cp("act", vf[ix][0:np_, :], pt_[0:np_, :], [pk_], [("vf", ix)])
                            dma("sp", vout[r0:r0 + np_, :], vf[ix][0:np_, :], r=[("vf", ix)], w=[("vout", i, b)], is_output=True)
                            cp("pool", vb[ix][0:np_, :], vf[ix][0:np_, :], [("vf", ix)], [("vb", ix)])
                            dma("sp", vSd[row0 + b * 128:row0 + b * 128 + np_, :], vb[ix][0:np_, :], r=[("vb", ix)], w=[("vSd", i, b)])
                            continue
                        ix = (2 * b + (1 if part == "q" else 0)) % 4
                        qknorm(pt_, pk_, np_, gq_t if part == "q" else gk_t, sq[ix % 2], ix % 2, ssh[ix], kn[b % 2], b % 2, kb[ix], ix, part == "k")
                        if part == "k":
                            dma("sp", kout[r0:r0 + np_, :], kn[b % 2][0:np_, :], r=[("kn", b % 2)], w=[("kout", i, b)], is_output=True)
                        newp.append((part, b, ix))
                    flush()
                    pending.extend(newp)
                    if b == 0:
                        while pending_cs:
                            pending_cs.pop(0)()
                    if b == 0 and i + 1 < len(tiles):
                        normA(XB[(i + 1) % 2], ("xb", (i + 1) % 2), tiles[i + 1][1], tiles[i + 1][2], scr)
                uproj_pending = True
                u0 = 0 if (own is not None) else ntok - 128
                for c in range(4):
                    pa, pka = psb7()
                    pg, pkg = psb7()
                    for kc in range(8):
                        mm(pa[:, u0:ntok], win[:, kc, 1544 + c * 128:1544 + (c + 1) * 128], hT[:, kc, u0:ntok], kc == 0, kc == 7,
                           [hk, ("win", kc)], [pka])
                    for kc in range(8):
                        mm(pg[:, u0:ntok], win[:, kc, 2056 + c * 128:2056 + (c + 1) * 128], hT[:, kc, u0:ntok], kc == 0, kc == 7,
                           [hk, ("win", kc)], [pkg])
                    act(sgm[c % 2][:, u0:ntok], pg[:, u0:ntok], AF.Sigmoid, [pkg], [("sgm", c % 2)])
                    tt("dve", uT_sb[:, c, u0:ntok], pa[:, u0:ntok], sgm[c % 2][:, u0:ntok], ALU.mult, [pka, ("sgm", c % 2)], ["uT_sb"])
                flush()
                dma("sp", kTd_v[:, :, row0:row0 + ntok], kT_sb[:, :, 0:ntok], r=[("Tk",)], w=[("kTd", i)])
                if own is not None:
                    dma("sp", qTd_v[:, :, own * T:own * T + ntok], qT_sb[:, :, 0:ntok], r=[("Tq",)], w=[("qTd", i)])
                dma("sp", uTd_v[:, :, row0 + u0:row0 + ntok], uT_sb[:, :, u0:ntok], r=["uT_sb"], w=[("uTd", i)])
                if sample or i == NTA - 1:
                    bl = nb - 1
                    pa, pka = psb7()
                    pg, pkg = psb7()
                    for kc in range(8):
                        mm(pa[0:np_, :], hT[:, kc, bl * 128:bl * 128 + np_], win[:, kc, 1544:2056], kc == 0, kc == 7, [hk, ("win", kc)], [pka])
                    for kc in range(8):
                        mm(pg[0:np_, :], hT[:, kc, bl * 128:bl * 128 + np_], win[:, kc, 2056:2568], kc == 0, kc == 7, [hk, ("win", kc)], [pkg])
                    act(sgm[0][0:np_, :], pg[0:np_, :], AF.Sigmoid, [pkg], [("sgm", 0)])
                    tt("dve", utk[0:np_, :], pa[0:np_, :], sgm[0][0:np_, :], ALU.mult, [pka, ("sgm", 0)], ["utk"])
                    if sample:
                        for j in range(2):
                            dma("sp", cvs[j, 14:30, :], utk[16 * j:16 * j + 16, :], r=["utk"], w=[("cvs", j)], is_output=True)
                            dma("sp", cvs[j, 0:14, :], sconv[j, 16:30, :], w=[("cvs0", j)], is_output=True)
                    else:
                        dma("sp", cvp[:, :], utk[98:128, :], r=["utk"], w=["cvp"], is_output=True)
                nl = nb * NH
                tt("dve", lx[0:np_, 0:nl].rearrange("p (b h) -> p b h", h=NH), PSF[0:np_, fo:fo + nl].rearrange("p (b h) -> p b h", h=NH),
                   bf_t[0:np_, :].unsqueeze(1).to_broadcast([np_, nb, NH]), ALU.add, [pkf, "bf_t"], ["lx"])
                ts("dve", lm[0:np_, 0:nl], lx[0:np_, 0:nl], -1.0, None, ALU.mult, None, ["lx"], ["lm"])
                tt("dve", la[0:np_, 0:nl], lx[0:np_, 0:nl], lm[0:np_, 0:nl], ALU.max, ["lx", "lm"], ["la"])
                act(la[0:np_, 0:nl], la[0:np_, 0:nl], AF.Exp, ["la"], ["la"], scale=-1.0)
                act(la[0:np_, 0:nl], la[0:np_, 0:nl], AF.Ln, ["la"], ["la"], bias=onesf[0:np_, 0:1])
                ts("dve", lm[0:np_, 0:nl], lx[0:np_, 0:nl], 0.0, None, ALU.min, None, ["lx"], ["lm"])
                lf2 = lf[0:np_, 0:nb, :]
                tt("dve", lf2, lm[0:np_, 0:nl].rearrange("p (b h) -> p b h", h=NH), la[0:np_, 0:nl].rearrange("p (b h) -> p b h", h=NH),
                   ALU.subtract, ["lm", "la"], ["lf"])
                if sample:
                    dma("sp", lfs[:, :], lf[0:np_, 0, :], r=["lf"], w=["lfs"], is_output=True)
                    dma("sp", lfsd[:, :], lf[0:np_, 0, :], r=["lf"], w=["lfsd"])
                else:
                    dma("sp", lfp[row0:row0 + T, :].rearrange("(b p) h -> p b h", p=128), lf[:, 0:nb, :], r=["lf"], w=[("lfp", i)], is_output=True)
                    def cumsum_job(i=i, row0=row0, fo=fo, pkf=pkf, nb=nb):
                        memset("dve", accF, 0.0, ["accF"])
                        for b in range(nb):
                            mm(PSF[:, fo + 64 + b * NH:fo + 64 + (b + 1) * NH], UT, lf[:, b, :], True, False, ["UT", "lf"], [pkf])
                            mm(PSF[:, fo + 64 + b * NH:fo + 64 + (b + 1) * NH], onesf, accF, False, True, ["onesf", "accF"], [pkf])
                            tt("dve", accF, accF, lf[:, b, :], ALU.add, ["accF", "lf"], ["accF"])
                        cp("dve", cumF_sb, PSF[:, fo + 64:fo + 64 + 4 * NH], [pkf], ["cumF_sb"])
                        dma("sp", cumFd[row0:row0 + T, :].rearrange("(b p) h -> p b h", p=128), cumF_sb.rearrange("p (b h) -> p b h", h=NH),
                            r=["cumF_sb"], w=[("cumFd", i)])
                    pending_cs.append(cumsum_job)
                if i + 1 < len(tiles):
                    transp(tiles[i + 1][1], tiles[i + 1][2], 1, tiles[i + 1][3], hTs[(i + 1) % 2], scr, ("hT", (i + 1) % 2))
                if stage_c and len(tiles) >= 6:
                    if 1 <= i <= 4:
                        mix_emit_diag(diag_c, i - 1)
                    elif i == 5:
                        mix_emit_wout(wout_c)
            while pending_cs:
                pending_cs.pop(0)()
            if stage_c and len(tiles) < 6:
                mix_emit_wout(wout_c)
                for c in range(4):
                    mix_emit_diag(diag_c, c)
            reserved.clear()
            S.barrier()

        if "B" in phases:
            proj_phase()

        def conv_a(rhs_of, ntok, segs, diag, bufs, ukey):
            yf, y2, mean, ex2, rstd, t1 = bufs
            for c in range(4):
                pc, pkc = psb()
                for si, (c0, c1) in enumerate(segs):
                    for tap in range(CW):
                        mm(pc[:, c0:c1], diag[:, c, tap, :], rhs_of(c, si, tap), tap == 0, tap == CW - 1, [("diag", c), ukey], [pkc])
                act(yf[:, c, 0:ntok], pc[:, 0:ntok], AF.Identity, [pkc], [("yf", c)], bias=ccol[:, c, 0:1])
                act(y2[:, c, 0:ntok], yf[:, c, 0:ntok], AF.Square, [("yf", c)], [("y2", c)])

        def conv_b(ntok, catT, bufs):
            yf, y2, mean, ex2, rstd, t1 = bufs
            p1, pk1 = psb()
            p2, pk2 = psb()
            for c in range(4):
                mm(p1[:, 0:ntok], onesf, yf[:, c, 0:ntok], c == 0, c == 3, ["onesf", ("yf", c)], [pk1])
            for c in range(4):
                mm(p2[:, 0:ntok], onesf, y2[:, c, 0:ntok], c == 0, c == 3, ["onesf", ("y2", c)], [pk2])
            ts("dve", mean[:, 0:ntok], p1[:, 0:ntok], 1.0 / 512, None, ALU.mult, None, [pk1], ["mean"])
            ts("dve", ex2[:, 0:ntok], p2[:, 0:ntok], 1.0 / 512, None, ALU.mult, None, [pk2], ["ex2"])
            tt("pool", rstd[:, 0:ntok], mean[:, 0:ntok], mean[:, 0:ntok], ALU.mult, ["mean"], ["rstd"])
            tt("dve", rstd[:, 0:ntok], ex2[:, 0:ntok], rstd[:, 0:ntok], ALU.subtract, ["ex2", "rstd"], ["rstd"])
            act(rstd[:, 0:ntok], rstd[:, 0:ntok], AF.Ln, ["rstd"], ["rstd"], bias=epsb[:, 0:1], scale=1.0)
            act(rstd[:, 0:ntok], rstd[:, 0:ntok], AF.Exp, ["rstd"], ["rstd"], scale=-0.5)
            for c in range(4):
                tb = t1[c % 2]
                tt("dve", tb[:, 0:ntok], yf[:, c, 0:ntok], mean[:, 0:ntok], ALU.subtract, [("yf", c), "mean"], [("t1", c % 2)])
                tt("pool", tb[:, 0:ntok], tb[:, 0:ntok], rstd[:, 0:ntok], ALU.mult, [("t1", c % 2), "rstd"], [("t1", c % 2)])
                act(catT[:, 4 + c, 0:ntok], tb[:, 0:ntok], AF.Silu, [("t1", c % 2)], [("catT", 4 + c)], bias=ccol[:, c, 2:3], scale=ccol[:, c, 1:2])

        def attn_finish_group(items, catT, ncol, col0, recs, rshs):
            geo = []
            for i, (acc, pka, hl, cc) in enumerate(items):
                o0, d0 = (0, 64) if hl % 2 == 0 else (64, 0)
                geo.append((o0, d0))
                recip(recs[i][d0:d0 + 1, 0:ncol], acc[d0:d0 + 1, 0:ncol], [pka], [("rec", i)])
            for i, (acc, pka, hl, cc) in enumerate(items):
                o0, d0 = geo[i]
                pb_, pkb_ = psb()
                mm(pb_[:, 0:ncol], onesf[d0:d0 + 1, 0:128], recs[i][d0:d0 + 1, 0:ncol], True, True, ["onesf", ("rec", i)], [pkb_])
                cp("act", rshs[i][o0:o0 + 64, 0:ncol], pb_[o0:o0 + 64, 0:ncol], [pkb_], [("rsh", i)])
            for i, (acc, pka, hl, cc) in enumerate(items):
                o0, d0 = geo[i]
                tt("dve", catT[o0:o0 + 64, cc, col0:col0 + ncol], acc[o0:o0 + 64, 0:ncol], rshs[i][o0:o0 + 64, 0:ncol], ALU.mult,
                   [pka, ("rsh", i)], [("catT", cc)])

        def attn_finish_start(items, catT, accsb):
            jobs = []
            geo = []
            for i, (acc, pka, hl, cc) in enumerate(items):
                o0, d0 = (0, 64) if hl % 2 == 0 else (64, 0)
                geo.append((o0, d0))
                cp("act" if i % 2 == 0 else "dve", accsb[i], acc[:, 0:T], [pka], [("accsb", i)])
            for i, (acc, pka, hl, cc) in enumerate(items):
                o0, d0 = geo[i]
                recip(accsb[i][d0:d0 + 1, :], accsb[i][d0:d0 + 1, :], [("accsb", i)], [("accsb", i)])

                def job(i=i, o0=o0, d0=d0, cc=cc):
                    pb_, pkb_ = psb()
                    mm(pb_[:, 0:T], onesf[d0:d0 + 1, 0:128], accsb[i][d0:d0 + 1, :], True, True, ["onesf", ("accsb", i)], [pkb_])
                    tt("dve", catT[o0:o0 + 64, cc, 0:T], accsb[i][o0:o0 + 64, :], pb_[o0:o0 + 64, 0:T], ALU.mult,
                       [("accsb", i), pkb_], [("catT", cc)])
                jobs.append(job)
            return jobs

        def out_proj(catT, wout, xb, xk, nb, np_, gate, tmp):
            k = 0
            for b in range(nb):
                for h in range(2):
                    po, pko = psb()
                    for c in range(8):
                        mm(po[0:np_, :], catT[:, c, b * 128:b * 128 + np_], wout[:, c, h * 512:(h + 1) * 512], c == 0, c == 7,
                           [("catT", c), "wout"], [pko])
                    tb = tmp[k % 2]
                    tt("dve", tb[0:np_, :], po[0:np_, :], gate[0:np_, h * 512:(h + 1) * 512], ALU.mult, [pko, "gate"], [("tmp", k % 2)])
                    tt("pool", xb[0:np_, b, h * 512:(h + 1) * 512], tb[0:np_, :], xb[0:np_, b, h * 512:(h + 1) * 512], ALU.add,
                       [("tmp", k % 2), xk], [xk])
                    k += 1

        def mix_prompt():
            AR.reset()
            wout, diag = mix_setup(emit=("B" not in phases))
            NBK = NTA * 4
            Fg = AR.f32(NBK * NH).rearrange("p (j h) -> p j h", h=NH)
            Fst = AR.f32(NTO * NH).rearrange("p (m h) -> p m h", h=NH)
            Fen = AR.f32(NTO * NH).rearrange("p (m h) -> p m h", h=NH)
            Rc = AR.f32(NTO * NH).rearrange("p (m h) -> p m h", h=NH)
            gate = AR.f32(D)
            XB = [AR.f32(4 * D).rearrange("p (b d) -> p b d", b=4) for _ in range(2)]
            qT_sb = [AR.bf16(NH * T).rearrange("p (h t) -> p h t", h=NH) for _ in range(2)]
            UW = T + 32
            uh = [AR.bf16(4 * UW).rearrange("p (c t) -> p c t", c=4) for _ in range(2)]
            catT = AR.bf16(8 * T).rearrange("p (c t) -> p c t", c=8)
            halo = [AR.bf16(4 * 2 * 32).rearrange("p (c w t) -> p c w t", c=4, w=2)[:, :, :, 0:30] for _ in range(2)]
            kTs = [AR.bf16(2 * T).rearrange("p (c t) -> p c t", c=2) for _ in range(2)]
            Vst = [AR.bf16(4 * 256).rearrange("p (b f) -> p b f", b=4) for _ in range(2)]
            Vp = [AR.bf16(4 * 4 * 128).rearrange("p (b h m) -> p b h m", b=4, h=4) for _ in range(2)]
            pT = [AR.bf16(T) for _ in range(4)]
            bias = [AR.f32(4 * NH).rearrange("p (b h) -> p b h", h=NH) for _ in range(2)]
            accsb = [AR.f32(T) for _ in range(4)]
            fin_jobs = []
            yf = AR.f32(4 * T).rearrange("p (c t) -> p c t", c=4)
            y2 = AR.f32(4 * T).rearrange("p (c t) -> p c t", c=4)
            mean = AR.f32(T); ex2 = AR.f32(T); rstd = AR.f32(T)
            t1 = [AR.f32(T) for _ in range(2)]
            tmp = [AR.f32(512) for _ in range(2)]

            dma("sp", Fg, cumFd.rearrange("(j p) h -> p j h", p=128), w=["Fg"])
            own_v = cumFd[NTO * T:NTA * T, :].rearrange("(m t) h -> m t h", t=T)
            dma("sp", Fst, own_v[:, 0, :].partition_broadcast(128), w=["Fst"])
            dma("sp", Fen, own_v[:, T - 1, :].partition_broadcast(128), w=["Fen"])
            tot = AR.f32(NH)
            Lsb = AR.f32(NTA)
            rhs3 = AR.f32(NTA * NH)
            off = AR.f32(NTA * NH).rearrange("p (j h) -> p j h", h=NH)
            dma("sp", tot[0:NTA, :], cumFd.rearrange("(j t) h -> j t h", t=T)[:, T - 1, :], w=["tot"])
            dma("sp", Lsb[0:NTA, :], lmat[:, :], w=["Lsb"])
            tt("dve", rhs3[0:NTA, :].rearrange("p (j h) -> p j h", h=NH), Lsb[0:NTA, :].unsqueeze(2).to_broadcast([NTA, NTA, NH]),
               tot[0:NTA, :].unsqueeze(1).to_broadcast([NTA, NTA, NH]), ALU.mult, ["Lsb", "tot"], ["rhs3"])
            pof, pko = psb()
            mm(pof[:, 0:NTA * NH], onesf[0:NTA, :], rhs3[0:NTA, :], True, True, ["onesf", "rhs3"], [pko])
            cp("dve", off, pof[:, 0:NTA * NH].rearrange("p (j h) -> p j h", h=NH), [pko], ["off"])
            tt("dve", Fg.rearrange("p (j b) h -> p j b h", b=4), Fg.rearrange("p (j b) h -> p j b h", b=4),
               off.unsqueeze(2).to_broadcast([128, NTA, 4, NH]), ALU.add, ["Fg", "off"], ["Fg"])
            tt("dve", Fst, Fst, off[:, NTO:NTA, :], ALU.add, ["Fst", "off"], ["Fst"])
            tt("dve", Fen, Fen, off[:, NTO:NTA, :], ALU.add, ["Fen", "off"], ["Fen"])
            tt("dve", Rc, Fst, Fen, ALU.add, ["Fst", "Fen"], ["Rc"])
            ts("dve", Rc, Rc, 0.5, cshift[:, 0:1], ALU.mult, ALU.subtract, ["Rc", "cshift"], ["Rc"])
            load_gate(gate, 1, False)
            for vb_ in Vp:
                memset("pool", vb_, 1.0, ["Vp0", "Vp1"])
            for qi, qb_ in enumerate(qT_sb):
                memset("dve", qb_, 0.0, [("qT", qi)])
            qTd_h = qTd.rearrange("(c e d) t -> e d c t", e=2, d=64)
            kTd_v = kTd.rearrange("(c p) t -> p c t", p=128)
            qTd_v = qTd.rearrange("(c p) t -> p c t", p=128)
            uTd_v = uTd.rearrange("(c p) t -> p c t", p=128)
            for i in (0, 1, 2, 3):
                reserved.add(i)

            def load_x1(m):
                g = NTO + m
                dma("sp", XB[m % 2][:, :, :], x1d[g * T:(g + 1) * T, :].rearrange("(b p) d -> p b d", p=128), w=[("xb", m % 2)])

            def load_tile(m):
                g = NTO + m
                q4 = qT_sb[m % 2].rearrange("p (c e) t -> p c e t", e=2)
                for e_ in range(2):
                    dma("sp", q4[64 * e_:64 * e_ + 64, :, e_, :], qTd_h[e_, :, :, m * T:(m + 1) * T], w=[("qT", m % 2)])
                dma("sp", uh[m % 2][:, :, 32:UW], uTd_v[:, :, g * T:(g + 1) * T], w=[("uh", m % 2)])
                po_ = (NTO + m - 1) if m >= 1 else 0
                dma("sp", halo[m % 2][:, :, 0, :], uTd_v[:, :, po_ * T + T - 30:(po_ + 1) * T], w=[("halo", m % 2)])
                dma("sp", halo[m % 2][:, :, 1, :], uTd_v[:, :, m * T + T - 30:(m + 1) * T], w=[("halo", m % 2)])

            load_tile(0)
            kvn = [0]
            ptn = [0]
            kvmap = {}
            cbufs = (yf, y2, mean, ex2, rstd, t1)

            def conv_front(m):
                uhh = uh[m % 2]; uk = ("uh", m % 2)
                hl_ = halo[m % 2]
                ts("pool", hl_[:, :, 0, :], hl_[:, :, 0, :], flag_t[:, NTO + m:NTO + m + 1], None, ALU.mult, None, [("halo", m % 2), "flag"], [("halo", m % 2)])
                S.op("dve", lambda e: e.scalar_tensor_tensor(
                    out=uhh[:, :, 2:32], in0=hl_[:, :, 1, :], scalar=flag_t[:, 2 * NTO + m:2 * NTO + m + 1], in1=hl_[:, :, 0, :],
                    op0=ALU.mult, op1=ALU.add), reads=[("halo", m % 2), "flag"], writes=[uk])

                def rhs_of(c, si, tap):
                    return uhh[:, c, 2 + tap:2 + tap + T]
                conv_a(rhs_of, T, [(0, T)], diag, cbufs, uk)

            conv_front(0)
            for m in range(NTO):
                g = NTO + m
                xb = XB[m % 2]; xk = ("xb", m % 2)
                qt = qT_sb[m % 2]; qk = ("qT", m % 2)
                uhh = uh[m % 2]; uk = ("uh", m % 2)
                if m + 1 < NTO:
                    load_tile(m + 1)
                for hg in range(2):
                    accs = [(PS[hl], ("ps", hl)) for hl in range(4)]
                    def prep(p, hg=hg, m=m):
                        if (m, hg, p) in kvmap:
                            return
                        bi = kvn[0] % 2
                        kvn[0] += 1
                        kvmap[(m, hg, p)] = bi
                        kb_, vs_, vp_, bs_ = kTs[bi], Vst[bi], Vp[bi], bias[bi]
                        dma("sp", kb_, kTd_v[:, 2 * hg:2 * hg + 2, p * T:(p + 1) * T], w=[("kTs", bi)])
                        dma("sp", vs_, vSd[p * T:(p + 1) * T, hg * 256:(hg + 1) * 256].rearrange("(b s) f -> s b f", s=128), w=[("Vst", bi)])
                        vs5 = vs_.rearrange("p b (c e d) -> p b c e d", c=2, e=2)
                        vp5 = vp_.rearrange("p b (c e) m -> p b c e m", c=2)
                        cp("pool", vp5[:, :, :, 0, 0:64], vs5[:, :, :, 0, :], [("Vst", bi)], ["Vp%d" % bi])
                        cp("dve", vp5[:, :, :, 1, 64:128], vs5[:, :, :, 1, :], [("Vst", bi)], ["Vp%d" % bi])
                        tt("dve", bs_, Rc[:, m:m + 1, :].to_broadcast([128, 4, NH]), Fg[:, p * 4:(p + 1) * 4, :], ALU.subtract, ["Rc", "Fg"], [("bias", bi)])
                        if p == m:
                            ts("dve", bs_, bs_, flag_t[:, m:m + 1], None, ALU.add, None, [("bias", bi), "flag"], [("bias", bi)])

                    plist = list(range(m + 1)) + list(range(NTO, g + 1))
                    units = [(p, b, hl) for p in plist for b in range(4) for hl in range(4)]
                    ufirst = units[0][0]
                    stb = {}

                    def qk(u, hg=hg, qt=qt, qk_=qk, m_=m):
                        p, b, hl = u
                        prep(p)
                        bi = kvmap[(m_, hg, p)]
                        cc = hl // 2
                        r0 = (hl % 2) * 64
                        st, pks = psb()
                        stb[u] = (st, pks)
                        mm(st[:, 0:T], kTs[bi][:, cc, b * 128:(b + 1) * 128], qt[:, hg * 4 + hl, :], True, True,
                           [("kTs", bi), qk_], [pks])

                    def rest(u, hg=hg, g=g, ufirst=ufirst, m_=m):
                        p, b, hl = u
                        bi = kvmap[(m_, hg, p)]
                        h = hg * 4 + hl
                        st, pks = stb.pop(u)
                        pi = ptn[0] % 4
                        ptn[0] += 1
                        act(pT[pi], st[:, 0:T], AF.Exp, [pks, ("bias", bi)], [("pT", pi)], bias=bias[bi][:, b, h:h + 1], scale=0.125)
                        if p == g:
                            tt("dve", pT[pi], pT[pi], masks[:, b, :], ALU.mult, [("pT", pi), "masks"], [("pT", pi)])
                        acc, pka = accs[hl]
                        mm(acc[:, 0:T], Vp[bi][:, b, hl, :], pT[pi], (p == ufirst and b == 0), (p == g and b == 3),
                           ["Vp%d" % bi, ("pT", pi)], [pka])

                    LOOK = 3
                    for idx in range(min(LOOK, len(units))):
                        qk(units[idx])
                    for idx, u in enumerate(units):
                        rest(u)
                        if fin_jobs and idx in (6, 12, 18, 24):
                            fin_jobs.pop(0)()
                        if idx + LOOK < len(units):
                            qk(units[idx + LOOK])
                    while fin_jobs:
                        fin_jobs.pop(0)()
                    if hg == 0:
                        load_x1(m)
                        prep(0, hg=1, m=m)
                    elif m + 1 < NTO:
                        prep(0, hg=0, m=m + 1)
                    fin_jobs.extend(attn_finish_start([(accs[hl][0], accs[hl][1], hl, hg * 2 + hl // 2) for hl in range(4)], catT, accsb))
                conv_b(T, catT, cbufs)
                if m + 1 < NTO:
                    conv_front(m + 1)
                while fin_jobs:
                    fin_jobs.pop(0)()
                out_proj(catT, wout, xb, xk, 4, 128, gate, tmp)
                dma("sp", x2d[m * T:(m + 1) * T, :].rearrange("(b p) d -> p b d", p=128), xb[:, :, :], r=[xk], w=[("x2d", m)])
            reserved.clear()
            S.barrier()

        if "C" in phases:
            mix_prompt()

        def mix_sample():
            AR.reset()
            wout, diag = mix_setup(emit=("C" not in phases))
            gate = AR.f32(D)
            xb = AR.f32(D).rearrange("p (b d) -> p b d", b=1)
            qs = AR.bf16(4 * 32).rearrange("p (c t) -> p c t", c=4)
            kn_ = AR.bf16(4 * 32).rearrange("p (c t) -> p c t", c=4)
            us = AR.bf16(4 * 32).rearrange("p (c t) -> p c t", c=4)
            vnew = AR.bf16(512)
            VpN = AR.bf16(NH * 128).rearrange("p (h m) -> p h m", h=NH)
            uhs = AR.bf16(4 * 2 * 48).rearrange("p (c j t) -> p c j t", c=4, j=2)
            scf = AR.f32(512)
            scb = AR.bf16(512)
            catT = AR.bf16(8 * 32).rearrange("p (c t) -> p c t", c=8)
            ckbs = [AR.bf16(NPB * 512).rearrange("p (b f) -> p b f", b=NPB) for _ in range(2)]
            cvbs = [AR.bf16(NPB * 512).rearrange("p (b f) -> p b f", b=NPB) for _ in range(2)]
            lfcs = [AR.f32(NPB * NH).rearrange("p (b h) -> p b h", h=NH) for _ in range(2)]
            for j in range(2):
                dma("pool", ckbs[j], ck[j, :, :].rearrange("(b p) f -> p b f", p=128), w=[("ckb", j)])
                dma("pool", cvbs[j], cv[j, :, :].rearrange("(b p) f -> p b f", p=128), w=[("cvb", j)])
                dma("sp", lfcs[j], clf[j, :, :].rearrange("(b p) h -> p b h", p=128), w=[("lfc", j)])
            kTc = AR.bf16(4 * PAST).rearrange("p (c t) -> p c t", c=4)
            VpC = AR.bf16(NPB * NH * 128).rearrange("p (b h m) -> p b h m", b=NPB, h=NH)
            Fc = AR.f32(NPB * NH).rearrange("p (b h) -> p b h", h=NH)
            accS = AR.f32(NH)
            Rcs = AR.f32(NH)
            lfn = AR.f32(NH)
            Fn = AR.f32(NH)
            biasC = AR.f32(NPB * NH).rearrange("p (b h) -> p b h", h=NH)
            biasN = AR.f32(NH)
            pT = [AR.bf16(16) for _ in range(4)]
            rec = [AR.f32(16) for _ in range(2)]
            rsh = [AR.f32(16) for _ in range(2)]
            yf = AR.f32(4 * 32).rearrange("p (c t) -> p c t", c=4)
            y2 = AR.f32(4 * 32).rearrange("p (c t) -> p c t", c=4)
            mean = AR.f32(32); ex2 = AR.f32(32); rstd = AR.f32(32)
            t1 = [AR.f32(32) for _ in range(2)]
            tmp = [AR.f32(512) for _ in range(2)]
            kTd_v = kTd.rearrange("(c p) t -> p c t", p=128)
            qTd_v = qTd.rearrange("(c p) t -> p c t", p=128)
            uTd_v = uTd.rearrange("(c p) t -> p c t", p=128)

            load_gate(gate, 1, True)
            dma("sp", xb[0:32, 0, :], x1d[TP:TP + 32, :], w=["xb"])
            dma("sp", qs, qTd_v[:, :, TO:TO + 32], w=["qs"])
            dma("sp", kn_, kTd_v[:, :, TP:TP + 32], w=["kn_"])
            dma("sp", us, uTd_v[:, :, TP:TP + 32], w=["us"])
            memset("pool", VpC, 1.0, ["VpC"])
            for j in range(2):
                dma("sp", scf[0:30, :], sconv[j, :, :], w=["scf"])
                cp("dve", scb[0:30, :], scf[0:30, :], ["scf"], ["scb"])
                pst, pk = psb()
                for c in range(4):
                    mm(pst[:, c * 32:c * 32 + 30], scb[0:30, c * 128:(c + 1) * 128], ident[0:30, 0:30], True, True, ["scb", "ident"], [pk])
                cp("act", uhs[:, :, j, 0:30], pst[:, 0:128].rearrange("p (c t) -> p c t", c=4)[:, :, 0:30], [pk], ["uhs"])
                cp("dve", uhs[:, :, j, 30:46], us[:, :, 16 * j:16 * j + 16], ["us"], ["uhs"])

            def rhs_of(c, si, tap):
                return uhs[:, c, si, tap:tap + 16]
            conv_a(rhs_of, 32, [(0, 16), (16, 32)], diag, (yf, y2, mean, ex2, rstd, t1), "uhs")
            conv_b(32, catT, (yf, y2, mean, ex2, rstd, t1))
            for i in (0, 1):
                reserved.add(i)
            for j in range(2):
                ckb, cvb, lfc = ckbs[j], cvbs[j], lfcs[j]
                dma("sp", lfn[0:16, :], lfsd[16 * j:16 * j + 16, :], r=["lfsd"], w=["lfn"])
                dma("sp", vnew[0:16, :], vSd[TP + 16 * j:TP + 16 * j + 16, :], w=["vnew"])
                for blk in range(NPB):
                    pst, pk = psb()
                    for c in range(4):
                        mm(pst[:, c * 128:(c + 1) * 128], ckb[:, blk, c * 128:(c + 1) * 128], ident, True, True, [("ckb", j), "ident"], [pk])
                    cp("act" if blk % 2 == 0 else "dve", kTc[:, :, blk * 128:(blk + 1) * 128], pst[:, :].rearrange("p (c t) -> p c t", c=4), [pk], ["kTc"])
                    cv4 = cvb[:, blk, :].rearrange("p (c e d) -> p c e d", e=2, d=64)
                    vp4 = VpC[:, blk, :, :].rearrange("p (c e) m -> p c e m", e=2)
                    cp("pool", vp4[:, :, 0, 0:64], cv4[:, :, 0, :], [("cvb", j), "VpC"], ["VpC"])
                    cp("dve", vp4[:, :, 1, 64:128], cv4[:, :, 1, :], [("cvb", j), "VpC"], ["VpC"])
                memset("pool", VpN, 1.0, ["VpN"])
                for h in range(NH):
                    e0 = (h % 2) * 64
                    cp("pool", VpN[0:16, h, e0:e0 + 64], vnew[0:16, h * 64:(h + 1) * 64], ["vnew", "VpN"], ["VpN"])
                memset("dve", accS, 0.0, ["accS"])
                pcs, pkc = psb()
                for blk in range(NPB):
                    mm(pcs[:, blk * NH:(blk + 1) * NH], UT, lfc[:, blk, :], True, False, ["UT", ("lfc", j)], [pkc])
                    mm(pcs[:, blk * NH:(blk + 1) * NH], onesf, accS, False, True, ["onesf", "accS"], [pkc])
                    tt("dve", accS, accS, lfc[:, blk, :], ALU.add, ["accS", ("lfc", j)], ["accS"])
                cp("dve", Fc, pcs[:, 0:NPB * NH].rearrange("p (b h) -> p b h", h=NH), [pkc], ["Fc"])
                pr, pkr = psb()
                mm(pr[:, 0:NH], onesf, accS, True, True, ["onesf", "accS"], [pkr])
                mm(pr[0:16, NH:2 * NH], UT[0:16, 0:16], lfn[0:16, :], True, False, ["UT", "lfn"], [pkr])
                mm(pr[0:16, NH:2 * NH], onesf[:, 0:16], accS, False, True, ["onesf", "accS"], [pkr])
                ts("dve", Rcs, pr[:, 0:NH], cshift[:, 0:1], None, ALU.subtract, None, [pkr, "cshift"], ["Rcs"])
                cp("dve", Fn[0:16, :], pr[0:16, NH:2 * NH], [pkr], ["Fn"])
                tt("dve", biasC, Rcs.unsqueeze(1).to_broadcast([128, NPB, NH]), Fc, ALU.subtract, ["Rcs", "Fc"], ["biasC"])
                tt("dve", biasN[0:16, :], Rcs[0:16, :], Fn[0:16, :], ALU.subtract, ["Rcs", "Fn"], ["biasN"])
                pn = [0]
                units = [(h, blk) for h in range(NH) for blk in range(NPB + 1)]
                stb = {}

                def qk(u, j=j):
                    h, blk = u
                    cc = h // 2
                    r0 = (h % 2) * 64
                    qcol = qs[r0:r0 + 64, cc, 16 * j:16 * j + 16]
                    st, pks = psb()
                    stb[u] = (st, pks)
                    if blk < NPB:
                        mm(st[:, 0:16], kTc[r0:r0 + 64, cc, blk * 128:(blk + 1) * 128], qcol, True, True, ["kTc", "qs"], [pks])
                    else:
                        mm(st[0:16, 0:16], kn_[r0:r0 + 64, cc, 16 * j:16 * j + 16], qcol, True, True, ["kn_", "qs"], [pks])

                def rest(u, j=j):
                    h, blk = u
                    cc = h // 2
                    acc, pka = PS[h % 2], ("ps", h % 2)
                    st, pks = stb.pop(u)
                    pi = pn[0] % 4
                    pn[0] += 1
                    if blk < NPB:
                        act(pT[pi], st[:, 0:16], AF.Exp, [pks, "biasC"], [("pT", pi)], bias=biasC[:, blk, h:h + 1], scale=0.125)
                        mm(acc[:, 0:16], VpC[:, blk, h, :], pT[pi], blk == 0, False, ["VpC", ("pT", pi)], [pka])
                    else:
                        act(pT[pi][0:16, :], st[0:16, 0:16], AF.Exp, [pks, "biasN"], [("pT", pi)], bias=biasN[0:16, h:h + 1], scale=0.125)
                        tt("pool", pT[pi][0:16, :], pT[pi][0:16, :], masks[0:16, 0, 0:16], ALU.mult, [("pT", pi), "masks"], [("pT", pi)])
                        mm(acc[:, 0:16], VpN[0:16, h, :], pT[pi][0:16, :], False, True, ["VpN", ("pT", pi)], [pka])
                        attn_finish_group([(acc, pka, h, cc)], catT, 16, 16 * j, [rec[h % 2]], [rsh[h % 2]])

                LOOK = 3
                for idx in range(LOOK):
                    qk(units[idx])
                for idx, u in enumerate(units):
                    rest(u)
                    if idx + LOOK < len(units):
                        qk(units[idx + LOOK])
            reserved.clear()
            out_proj(catT, wout, xb, "xb", 1, 32, gate, tmp)
            dma("sp", x2d[TO:TO + 32, :], xb[0:32, 0, :], r=["xb"], w=["x2ds"])
            S.barrier()

        if "C" in phases:
            mix_sample()

        if "D" in phases:
            tilesD = [(x2d[m * T:(m + 1) * T, :], yp[m * T:(m + 1) * T, :], 4, 128, False) for m in range(NTO)]
            tilesD.append((x2d[TO:TO + 32, :], ys[:, :], 1, 32, True))
            ffn_phase(1, tilesD, True)

        S.emit()
    return nc


_CACHE = {}


def kernel(x_prompt, x_sample, c_prompt, c_sample, cache_k, cache_v, cache_logf, state_conv,
           w_ada, b_ada, g_ffn1, w_up1, w_down1, g_mix, w_in, b_f, g_q, g_k,
           conv_w, conv_b, conv_ln_g, conv_ln_b, w_out, g_ffn2, w_up2, w_down2, g_final, _phases="0ABCD"):
    f = lambda a: np.ascontiguousarray(np.asarray(a, dtype=np.float32))
    x_prompt = f(x_prompt); x_sample = f(x_sample)
    B, SEQ, _ = x_prompt.shape
    SB, SS, _ = x_sample.shape
    PAST = cache_k.shape[2]
    assert B * 2 == 8 and SB == 16 and SS == 16 and SEQ % (2 * T) == 0
    HALF = SEQ // 2
    NTO = HALF // T
    key = (NTO, PAST, _phases)
    if key not in _CACHE:
        _CACHE[key] = build(NTO, PAST, _phases)
    nc = _CACHE[key]
    shared = {
        "w_ada": f(w_ada)[0], "b_ada": f(b_ada), "g3": np.stack([f(g_ffn1)[0], f(g_mix)[0], f(g_ffn2)[0]]),
        "w_up1": f(w_up1)[0], "w_up2": f(w_up2)[0], "w_down1": f(w_down1)[0], "w_down2": f(w_down2)[0],
        "w_in": f(w_in)[0], "b_f": f(b_f), "g_q": f(g_q), "g_k": f(g_k), "conv_w": f(conv_w)[0],
        "cvec3": np.stack([f(conv_b)[0], f(conv_ln_g)[0], f(conv_ln_b)[0]]), "w_out": f(w_out)[0], "g_final": f(g_final),
    }
    cs_ = f(c_sample); cp_ = f(c_prompt)
    ck_ = f(cache_k)[0].reshape(SB, PAST, 512); cv_ = f(cache_v)[0].reshape(SB, PAST, 512)
    clf_ = f(cache_logf)[0]; sc_ = f(state_conv)[0]
    in_maps = []
    NTA = 2 * NTO
    own_t = {0: [t for t in range(NTA) if t % 4 in (0, 3)], 1: [t for t in range(NTA) if t % 4 in (1, 2)]}
    storage = {r: own_t[1 - r] + own_t[r] for r in (0, 1)}
    for core in range(8):
        b, r = core // 2, core % 2
        xt_ = x_prompt[b].reshape(NTA, T, D)
        m = dict(shared)
        m["xp"] = np.ascontiguousarray(xt_[storage[r]].reshape(NTA * T, D))
        glob = storage[r]
        lm = np.zeros((NTA, NTA), np.float32)
        for a_ in range(NTA):
            for c_ in range(NTA):
                lm[a_, c_] = 1.0 if glob[a_] < glob[c_] else 0.0
        m["lmat"] = lm
        m["xs"] = np.ascontiguousarray(x_sample[2 * core:2 * core + 2].reshape(32, D))
        m["cvec"] = np.ascontiguousarray(np.stack([cp_[b], cs_[2 * core], cs_[2 * core + 1]]))
        m["ck"] = np.ascontiguousarray(ck_[2 * core:2 * core + 2]); m["cv"] = np.ascontiguousarray(cv_[2 * core:2 * core + 2])
        m["clf"] = np.ascontiguousarray(clf_[2 * core:2 * core + 2]); m["sconv"] = np.ascontiguousarray(sc_[2 * core:2 * core + 2])
        fl = np.zeros((128, 3 * NTO), np.float32)
        for mm_ in range(NTO):
            o_, x_ = own_t[r][mm_], own_t[1 - r][mm_]
            fl[:, mm_] = NEGBIG if x_ > o_ else 0.0
            pred = o_ - 1
            if pred >= 0:
                if mm_ >= 1 and own_t[r][mm_ - 1] == pred:
                    fl[:, NTO + mm_] = 1.0
                else:
                    assert x_ == pred, (r, mm_, pred)
                    fl[:, 2 * NTO + mm_] = 1.0
        m["flagc"] = fl
        in_maps.append(m)
    res = run_bass_kernel_spmd(nc, in_maps, core_ids=list(range(8)))
    R = res.results
    y_prompt = np.zeros((B, SEQ, D), np.float32)
    k_prompt = np.zeros((1, B, SEQ, NH, DH), np.float32); v_prompt = np.zeros_like(k_prompt)
    logf_prompt = np.zeros((1, B, SEQ, NH), np.float32); conv_prompt = np.zeros((1, B, 30, 512), np.float32)
    y_sample = np.zeros((SB, SS, D), np.float32)
    k_sample = np.zeros((1, SB, SS, NH, DH), np.float32); v_sample = np.zeros_like(k_sample)
    logf_sample = np.zeros((1, SB, SS, NH), np.float32); conv_sample = np.zeros((1, SB, 30, 512), np.float32)
    for core in range(8):
        b, r = core // 2, core % 2
        o = R[core]
        yv = y_prompt[b].reshape(NTA, T, D)
        yv[own_t[r]] = o["yp"].reshape(NTO, T, D)
        if r == 1:
            inv = storage[r]
            k_prompt[0, b].reshape(NTA, T, NH, DH)[inv] = o["kp"].reshape(NTA, T, NH, DH)
            v_prompt[0, b].reshape(NTA, T, NH, DH)[inv] = o["vp"].reshape(NTA, T, NH, DH)
            logf_prompt[0, b].reshape(NTA, T, NH)[inv] = o["lfp"].reshape(NTA, T, NH)
        if own_t[r][-1] == NTA - 1:
            conv_prompt[0, b] = o["cvp"]
        sl = slice(2 * core, 2 * core + 2)
        y_sample[sl] = o["ys"].reshape(2, SS, D)
        k_sample[0, sl] = o["ks"].reshape(2, SS, NH, DH); v_sample[0, sl] = o["vs"].reshape(2, SS, NH, DH)
        logf_sample[0, sl] = o["lfs"].reshape(2, SS, NH); conv_sample[0, sl] = o["cvs"]
    return (y_prompt, y_sample, k_prompt, v_prompt, logf_prompt, conv_prompt, k_sample, v_sample, logf_sample, conv_sample)
```

```python
import contextlib
import numpy as np
import ml_dtypes
import concourse.bass as bass
import concourse.mybir as mybir
from concourse.bass_utils import run_bass_kernel_spmd

F32 = mybir.dt.float32
BF16 = mybir.dt.bfloat16
AF = mybir.ActivationFunctionType
ALU = mybir.AluOpType
AX = mybir.AxisListType

COMPUTE = ("pe", "act", "dve", "pool")
NDMA_SEMS = 32
NSW_SEMS = 8

D = 1024
DFF = 2816
NF = DFF // 128
NH = 8
DH = 64
DIN = 2568
T = 512
CW = 31
EPS = 1e-6
NEGBIG = -30000.0
WUP_GROUPS = ((0, 4), (4, 10), (10, 16), (16, 22))


class Sched:
    def __init__(self, nc, same_engine_sync=True):
        self.nc = nc
        self.same_engine_sync = same_engine_sync
        self.queues = {e: [] for e in ("pe", "act", "dve", "pool", "sp")}
        self.count = {e: 0 for e in COMPUTE}
        self.dma_n = 0
        self.dma_sw = 0
        self.dma_slot_last = [0] * NDMA_SEMS
        self.last_write = {}
        self.reads = {}
        self.waited = {e: {} for e in self.queues}
        self.out_dma = []
        self.needed = {e: set() for e in COMPUTE}
        self.pending = {e: {} for e in self.queues}

    def _deps(self, eng, reads, writes):
        need = dict(self.pending[eng])
        self.pending[eng] = {}

        def add(tok):
            if tok is None:
                return
            k, v = tok
            if need.get(k, 0) < v:
                need[k] = v

        for r in reads:
            add(self.last_write.get(r))
        for w in writes:
            add(self.last_write.get(w))
            for t in self.reads.get(w, ()):
                add(t)
        out = {}
        for k, v in need.items():
            if k == eng and (eng == "pe" or not self.same_engine_sync):
                continue
            if self.waited[eng].get(k, 0) >= v:
                continue
            self.waited[eng][k] = v
            out[k] = v
            if not isinstance(k, tuple):
                self.needed[k].add(v)
        return out

    def _commit(self, tok, reads, writes):
        for r in reads:
            self.reads.setdefault(r, []).append(tok)
        for w in writes:
            self.last_write[w] = tok
            self.reads[w] = []

    def op(self, eng, fn, reads=(), writes=()):
        waits = self._deps(eng, reads, writes)
        self.count[eng] += 1
        tok = (eng, self.count[eng])
        self.queues[eng].append((waits, fn, tok))
        self._commit(tok, reads, writes)
        return tok

    def dma(self, q, fn, reads=(), writes=(), is_output=False):
        if q == "pool":
            slot = NDMA_SEMS - NSW_SEMS + self.dma_sw % NSW_SEMS
            self.dma_sw += 1
        else:
            slot = self.dma_n % (NDMA_SEMS - NSW_SEMS)
            self.dma_n += 1
        key = ("dma", slot)
        waits = self._deps(q, reads, writes)
        prev = self.dma_slot_last[slot]
        if prev and self.waited[q].get(key, 0) < prev:
            waits[key] = max(waits.get(key, 0), prev)
            self.waited[q][key] = prev
        val = prev + 16
        self.dma_slot_last[slot] = val
        tok = (key, val)
        self.queues[q].append((waits, fn, tok))
        self._commit(tok, reads, writes)
        if is_output:
            self.out_dma.append(tok)
        return tok

    def barrier(self):
        allw = {}
        for e in COMPUTE:
            if self.count[e]:
                allw[e] = self.count[e]
        for i in range(NDMA_SEMS):
            if self.dma_slot_last[i]:
                allw[("dma", i)] = self.dma_slot_last[i]
        for e in self.queues:
            p = self.pending[e]
            for k, v in allw.items():
                if p.get(k, 0) < v:
                    p[k] = v
        self.last_write = {}
        self.reads = {}

    def emit(self):
        nc = self.nc
        with contextlib.ExitStack() as es:
            sems = {}
            for e in COMPUTE:
                sems[e] = es.enter_context(nc.semaphore("s_" + e))
            for i in range(NDMA_SEMS):
                sems[("dma", i)] = es.enter_context(nc.semaphore("s_dma%d" % i))
            final_waits = {}
            for i in range(NDMA_SEMS):
                if self.dma_slot_last[i]:
                    final_waits[("dma", i)] = self.dma_slot_last[i]
            for e in COMPUTE:
                if self.count[e]:
                    self.needed[e].add(self.count[e])
                    final_waits[e] = self.count[e]
            rank = {}
            for e in COMPUTE:
                for i, v in enumerate(sorted(self.needed[e])):
                    rank[(e, v)] = i + 1
            block = es.enter_context(nc.Block())
            queues = self.queues

            def semval(k, v):
                return v if isinstance(k, tuple) else rank[(k, v)]

            def run(engobj, q, extra_final=False):
                for waits, fn, tok in queues[q]:
                    for k, v in waits.items():
                        engobj.wait_ge(sems[k], semval(k, v))
                    ins = fn(engobj)
                    k, v = tok
                    if isinstance(k, tuple):
                        ins.then_inc(sems[k], 16)
                    elif (k, v) in rank:
                        ins.then_inc(sems[k], 1)
                if extra_final:
                    for k, v in final_waits.items():
                        engobj.wait_ge(sems[k], semval(k, v))

            @block.sync
            def _(e):
                run(e, "sp", extra_final=True)

            @block.tensor
            def _(e):
                run(e, "pe")

            @block.scalar
            def _(e):
                run(e, "act")

            @block.vector
            def _(e):
                run(e, "dve")

            @block.gpsimd
            def _(e):
                run(e, "pool")


class Arena:
    def __init__(self, ap, nwords):
        self.ap = ap
        self.n = nwords
        self.off = 0
        self.mark = 0

    def set_mark(self):
        self.mark = self.off

    def reset(self):
        self.off = self.mark

    def f32(self, n):
        assert self.off + n <= self.n, ("arena overflow", self.off, n, self.n)
        a = self.ap[:, self.off:self.off + n]
        self.off += n
        return a

    def bf16(self, n):
        w = (n + 1) // 2
        assert self.off + w <= self.n, ("arena overflow", self.off, w, self.n)
        a = self.ap[:, self.off:self.off + w].bitcast(BF16)
        self.off += w
        return a[:, 0:n]


def build(NTO, PAST, phases="0ABCD"):
    NTA = 2 * NTO
    TP = NTA * T
    TO = NTO * T
    NPB = PAST // 128
    nc = bass.Bass("TRN2", target_bir_lowering=False)

    def din(name, shape, dt=F32):
        return nc.dram_tensor(name, list(shape), dt, kind="ExternalInput").ap()

    def dout(name, shape, dt=F32):
        return nc.dram_tensor(name, list(shape), dt, kind="ExternalOutput").ap()

    def dscr(name, shape, dt=F32):
        return nc.dram_tensor(name, list(shape), dt).ap()

    xp = din("xp", [TP, D]); xs = din("xs", [32, D]); cvec = din("cvec", [3, D])
    ck = din("ck", [2, PAST, 512]); cv = din("cv", [2, PAST, 512]); clf = din("clf", [2, PAST, NH])
    sconv = din("sconv", [2, 30, 512]); flagc = din("flagc", [128, 3 * NTO]); lmat = din("lmat", [NTA, NTA])
    w_ada = din("w_ada", [D, 9 * D]); b_ada = din("b_ada", [1, 9 * D])
    g3 = din("g3", [3, D])
    w_up = [din("w_up1", [D, 2 * DFF]), din("w_up2", [D, 2 * DFF])]
    w_dn = [din("w_down1", [DFF, D]), din("w_down2", [DFF, D])]
    w_in = din("w_in", [D, DIN]); b_f = din("b_f", [1, NH]); g_q = din("g_q", [1, DH]); g_k = din("g_k", [1, DH])
    conv_w = din("conv_w", [CW, 512]); cvec3 = din("cvec3", [3, 512])
    w_out = din("w_out", [D, D]); g_final = din("g_final", [1, D])

    yp = dout("yp", [TO, D]); ys = dout("ys", [32, D])
    kp = dout("kp", [TP, 512]); vp = dout("vp", [TP, 512]); lfp = dout("lfp", [TP, NH]); cvp = dout("cvp", [30, 512])
    ks = dout("ks", [32, 512]); vs = dout("vs", [32, 512]); lfs = dout("lfs", [32, NH]); cvs = dout("cvs", [2, 30, 512])

    TPX = TP + 32
    x1d = dscr("x1d", [TPX, D]); x2d = dscr("x2d", [TO + 32, D])
    kTd = dscr("kTd", [512, TPX], BF16); uTd = dscr("uTd", [512, TPX], BF16); qTd = dscr("qTd", [512, TO + 32], BF16)
    vSd = dscr("vSd", [TPX, 512], BF16); cumFd = dscr("cumFd", [TP, NH]); modrows = dscr("modrows", [3, 9 * D])

    S = Sched(nc)
    es = contextlib.ExitStack()
    with es:
        es.enter_context(nc.allow_low_precision("bf16 matmul operands, fp32 accumulate"))
        es.enter_context(nc.allow_non_contiguous_dma("small strided loads"))
        NW = 51200
        arena_t = es.enter_context(nc.sbuf_tensor("arena", [128, NW], F32))
        AR = Arena(arena_t, NW)
        PS = [es.enter_context(nc.psum_tensor("ps%d" % i, [128, 512], F32)) for i in range(8)]
        ps_ctr = [0]

        reserved = set()

        def psb():
            while True:
                i = ps_ctr[0] % 8
                ps_ctr[0] += 1
                if i not in reserved:
                    return PS[i], ("ps", i)

        def dma(q, out, in_, r=(), w=(), is_output=False):
            S.dma(q, lambda e: e.dma_start(out=out, in_=in_), reads=r, writes=w, is_output=is_output)

        def mm(out, lhsT, rhs, start, stop, r, w):
            S.op("pe", lambda e: e.matmul(out, lhsT=lhsT, rhs=rhs, start=start, stop=stop), reads=r, writes=w)

        def act(out, in_, func, r, w, bias=None, scale=None, accum_out=None):
            kw = {}
            if bias is not None:
                kw["bias"] = bias
            if scale is not None:
                kw["scale"] = scale
            if accum_out is not None:
                kw["accum_out"] = accum_out
            S.op("act", lambda e: e.activation(out=out, in_=in_, func=func, **kw), reads=r, writes=w)

        def tt(eng, out, in0, in1, op, r, w):
            S.op(eng, lambda e: e.tensor_tensor(out=out, in0=in0, in1=in1, op=op), reads=r, writes=w)

        def ts(eng, out, in0, s1, s2, op0, op1, r, w):
            if op1 is None:
                S.op(eng, lambda e: e.tensor_scalar(out=out, in0=in0, scalar1=s1, scalar2=s2, op0=op0), reads=r, writes=w)
            else:
                S.op(eng, lambda e: e.tensor_scalar(out=out, in0=in0, scalar1=s1, scalar2=s2, op0=op0, op1=op1), reads=r, writes=w)

        def cp(eng, out, in_, r, w):
            if eng == "act":
                S.op("act", lambda e: e.copy(out=out, in_=in_), reads=r, writes=w)
            else:
                S.op(eng, lambda e: e.tensor_copy(out=out, in_=in_), reads=r, writes=w)

        def memset(eng, ap, val, w):
            S.op(eng, lambda e: e.memset(ap, val), writes=w)

        def recip(out, in_, r, w):
            S.op("dve", lambda e: e.reciprocal(out=out, in_=in_), reads=r, writes=w)

        def rsqrt_small(out, in_, scale, r, w):
            act(out, in_, AF.Sqrt, r, w, bias=epsb[0:out.shape[0], 0:1], scale=scale)
            recip(out, out, w, w)

        ident = AR.bf16(128)
        identf = AR.f32(128)
        UT = AR.f32(128)
        onesf = AR.f32(128)
        masks = AR.bf16(4 * 512).rearrange("p (o t) -> p o t", o=4)
        modcol = AR.f32(72 * 3).rearrange("p (j s) -> p j s", s=3)
        gcol = AR.f32(8 * 3).rearrange("p (k s) -> p k s", s=3)
        gsc = [AR.f32(8 * 3).rearrange("p (k s) -> p k s", s=3) for _ in range(3)]
        ccol = AR.f32(4 * 3).rearrange("p (k s) -> p k s", s=3)
        wcol = AR.f32(4 * CW).rearrange("p (k j) -> p k j", k=4)
        flag_t = AR.f32(3 * NTO)
        bf_t = AR.f32(NH)
        gq_t = AR.f32(512)
        gk_t = AR.f32(512)
        cshift = AR.f32(1)
        epsb = AR.f32(1)
        accF = AR.f32(NH)
        AR.set_mark()

        memset("pool", identf, 0.0, ["identf"])
        S.op("pool", lambda e: e.affine_select(out=identf, in_=identf, pattern=[[-1, 128]], compare_op=ALU.not_equal,
                                               fill=1.0, base=0, channel_multiplier=1), reads=["identf"], writes=["identf"])
        cp("dve", ident, identf, ["identf"], ["ident"])
        memset("pool", UT, 1.0, ["UT"])
        S.op("pool", lambda e: e.affine_select(out=UT, in_=UT, pattern=[[1, 128]], compare_op=ALU.is_ge,
                                               fill=0.0, base=0, channel_multiplier=-1), reads=["UT"], writes=["UT"])
        memset("dve", onesf, 1.0, ["onesf"])
        memset("dve", epsb, EPS, ["epsb"])
        memset("dve", accF, 0.0, ["accF"])
        mtmp = AR.f32(512)
        for o in range(4):
            memset("pool", mtmp, 1.0, ["mtmp"])
            S.op("pool", lambda e, o=o: e.affine_select(out=mtmp, in_=mtmp, pattern=[[1, 512]], compare_op=ALU.is_ge,
                                                        fill=0.0, base=-128 * o, channel_multiplier=-1),
                 reads=["mtmp"], writes=["mtmp"])
            cp("pool", masks[:, o, :], mtmp, ["mtmp"], ["masks"])
        dma("sp", flag_t, flagc[:, :], w=["flag"])
        dma("sp", bf_t, b_f[0:1, :].partition_broadcast(128).rearrange("p a n -> p (a n)"), w=["bf_t"])
        gq64 = AR.f32(64)
        gk64 = AR.f32(64)
        dma("sp", gq64, g_q[0:1, :].partition_broadcast(128).rearrange("p a n -> p (a n)"), w=["gq64"])
        dma("sp", gk64, g_k[0:1, :].partition_broadcast(128).rearrange("p a n -> p (a n)"), w=["gk64"])
        cp("dve", gq_t.rearrange("p (h d) -> p h d", h=NH), gq64.unsqueeze(1).to_broadcast([128, NH, DH]), ["gq64"], ["gq_t"])
        cp("dve", gk_t.rearrange("p (h d) -> p h d", h=NH), gk64.unsqueeze(1).to_broadcast([128, NH, DH]), ["gk64"], ["gk_t"])
        gcolqk = AR.f32(2)
        AR.set_mark()
        wup_pre = AR.bf16(8 * 2 * DFF).rearrange("p (k n) -> p k n", k=8)
        if "A" in phases:
            wup_v0 = w_up[0].rearrange("(k p) n -> p k n", p=128)
            for gi, (f0, f1) in enumerate(WUP_GROUPS):
                for half in range(2):
                    c0, c1 = half * DFF + f0 * 128, half * DFF + f1 * 128
                    dma("pool", wup_pre[:, :, c0:c1], wup_v0[:, :, c0:c1], w=[("wup", gi)])
        grow2 = AR.f32(128)
        for j_ in range(2):
            dma("sp", grow2[0:1, 64 * j_:64 * j_ + 64], g_q[0:1, :], w=["grow2q"])
            dma("sp", grow2[32:33, 64 * j_:64 * j_ + 64], g_k[0:1, :], w=["grow2k"])
        pgq, pkgq = psb()
        mm(pgq[:, 0:1], grow2[0:1, :], identf[0:1, 0:1], True, True, ["grow2q", "identf"], [pkgq])
        mm(pgq[:, 1:2], grow2[32:33, :], onesf[32:33, 0:1], True, True, ["grow2k", "onesf"], [pkgq])
        cp("dve", gcolqk, pgq[:, 0:2], [pkgq], ["gcolqk"])
        mq = AR.f32(2)
        S.op("dve", lambda e: e.tensor_reduce(out=mq[:, 0:1], in_=gq64, axis=AX.X, op=ALU.max, apply_absolute_value=True),
             reads=["gq64"], writes=["mq"])
        S.op("dve", lambda e: e.tensor_reduce(out=mq[:, 1:2], in_=gk64, axis=AX.X, op=ALU.max, apply_absolute_value=True),
             reads=["gk64"], writes=["mq"])
        tt("dve", cshift, mq[:, 0:1], mq[:, 1:2], ALU.mult, ["mq"], ["cshift"])
        ts("dve", cshift, cshift, 8.0, None, ALU.mult, None, ["cshift"], ["cshift"])

        crow = AR.f32(D)
        cT = AR.bf16(8 * 3).rearrange("p (k s) -> p k s", s=3)
        mrow = AR.f32(9 * D)
        grow = AR.f32(D)
        c3row = AR.f32(512)
        cwrow = AR.f32(512)
        dma("sp", crow[0:3, :], cvec[:, :], w=["crow"])
        dma("sp", grow[0:3, :], g3[:, :], w=["grow"])
        dma("sp", c3row[0:3, :], cvec3[:, :], w=["c3row"])
        dma("sp", cwrow[0:CW, :], conv_w[:, :], w=["cwrow"])
        pst, pk = psb()
        for kc in range(8):
            mm(pst[:, kc * 3:kc * 3 + 3], crow[0:3, kc * 128:(kc + 1) * 128], identf[0:3, 0:3], True, True, ["crow", "identf"], [pk])
        act(cT, pst[:, 0:24].rearrange("p (k s) -> p k s", s=3), AF.Silu, [pk], ["cT"])
        pst, pk = psb()
        for kc in range(8):
            mm(pst[:, kc * 3:kc * 3 + 3], grow[0:3, kc * 128:(kc + 1) * 128], identf[0:3, 0:3], True, True, ["grow", "identf"], [pk])
        cp("dve", gcol, pst[:, 0:24].rearrange("p (k s) -> p k s", s=3), [pk], ["gcol"])
        pst, pk = psb()
        for kc in range(4):
            mm(pst[:, kc * 3:kc * 3 + 3], c3row[0:3, kc * 128:(kc + 1) * 128], identf[0:3, 0:3], True, True, ["c3row", "identf"], [pk])
        cp("dve", ccol, pst[:, 0:12].rearrange("p (k s) -> p k s", s=3), [pk], ["ccol"])
        pst, pk = psb()
        for kc in range(4):
            mm(pst[:, kc * CW:(kc + 1) * CW], cwrow[0:CW, kc * 128:(kc + 1) * 128], identf[0:CW, 0:CW], True, True, ["cwrow", "identf"], [pk])
        cp("dve", wcol, pst[:, 0:4 * CW].rearrange("p (k j) -> p k j", k=4), [pk], ["wcol"])

        wa_ring = [AR.bf16(8 * 512).rearrange("p (k n) -> p k n", k=8) for _ in range(3)]
        bch = [AR.f32(512) for _ in range(2)]
        w_ada_v = w_ada.rearrange("(k p) n -> p k n", p=128)
        for n in range(18):
            wr = wa_ring[n % 3]
            dma("pool", wr, w_ada_v[:, :, n * 512:(n + 1) * 512], w=[("wa", n % 3)])
            bb = bch[n % 2]
            dma("sp", bb[0:3, :], b_ada[0:1, n * 512:(n + 1) * 512].partition_broadcast(3).rearrange("p a n -> p (a n)"), w=[("bch", n % 2)])
            pst, pk = psb()
            for kc in range(8):
                mm(pst[0:3, :], cT[:, kc, :], wr[:, kc, :], kc == 0, kc == 7, ["cT", ("wa", n % 3)], [pk])
            tt("dve", mrow[0:3, n * 512:(n + 1) * 512], pst[0:3, :], bb[0:3, :], ALU.add, [pk, ("bch", n % 2)], ["mrow"])
        dma("sp", modrows[:, :], mrow[0:3, :], r=["mrow"], w=["modrows"])
        for jj in range(0, 72, 24):
            pst, pk = psb()
            for j in range(jj, jj + 24):
                mm(pst[:, (j - jj) * 3:(j - jj) * 3 + 3], mrow[0:3, j * 128:(j + 1) * 128], identf[0:3, 0:3], True, True, ["mrow", "identf"], [pk])
            cp("dve", modcol[:, jj:jj + 24, :], pst[:, 0:72].rearrange("p (j s) -> p j s", s=3), [pk], ["modcol"])
        for n in range(3):
            ts("dve", gsc[n], modcol[:, (3 * n + 1) * 8:(3 * n + 2) * 8, :], 1.0, None, ALU.add, None, ["modcol"], [("gsc", n)])
            tt("dve", gsc[n], gsc[n], gcol[:, :, n:n + 1].to_broadcast([128, 8, 3]), ALU.mult, [("gsc", n), "gcol"], [("gsc", n)])
        S.barrier()
        AR.reset()

        def shcol(n, kc, s):
            return modcol[:, 3 * n * 8 + kc, s:s + 1]

        def load_gate(gt, n, sample):
            c0 = (3 * n + 2) * D
            if not sample:
                dma("sp", gt, modrows[0:1, c0:c0 + D].partition_broadcast(128).rearrange("p a n -> p (a n)"), r=["modrows"], w=["gate"])
            else:
                for j in range(2):
                    dma("sp", gt[16 * j:16 * j + 16, :], modrows[1 + j:2 + j, c0:c0 + D].partition_broadcast(16).rearrange("p a n -> p (a n)"),
                        r=["modrows"], w=["gate"])

        def normA(xt, xk, nb, np_, scr):
            junk, ss, xn = scr
            for b in range(nb):
                act(junk[0:np_, :], xt[0:np_, b, :], AF.Square, [xk], ["junk", "ss"], accum_out=ss[0:np_, b:b + 1])
            rsqrt_small(ss[0:np_, 0:nb], ss[0:np_, 0:nb], 1.0 / D, ["ss"], ["ss"])
            for b in range(nb):
                act(xn[0:np_, b, :], xt[0:np_, b, :], AF.Identity, [xk, "ss"], [("xn", b)], scale=ss[0:np_, b:b + 1])

        def transp(nb, np_, n, sample, hT, scr, hkey="hT"):
            junk, ss, xn = scr
            ntok = (nb - 1) * 128 + np_
            segs = [(0, ntok, 0)] if not sample else [(0, 16, 1), (16, 32, 2)]
            for kc in range(8):
                pst, pk = psb()
                for b in range(nb):
                    mm(pst[:, b * 128:b * 128 + np_], xn[0:np_, b, kc * 128:(kc + 1) * 128], ident[0:np_, 0:np_], True, True,
                       [("xn", b), "ident"], [pk])
                for (c0, c1, s) in segs:
                    if kc % 2 == 0:
                        act(hT[:, kc, c0:c1], pst[:, c0:c1], AF.Identity, [pk], [hkey], bias=shcol(n, kc, s), scale=gsc[n][:, kc, s:s + 1])
                    else:
                        ts("dve", hT[:, kc, c0:c1], pst[:, c0:c1], gsc[n][:, kc, s:s + 1], shcol(n, kc, s), ALU.mult, ALU.add, [pk], [hkey])
            return ntok

        def ffn_phase(which, tiles, final_norm):
            n = 0 if which == 0 else 2
            AR.reset()
            wup = AR.bf16(8 * 2 * DFF).rearrange("p (k n) -> p k n", k=8)
            XB = [AR.f32(4 * D).rearrange("p (b d) -> p b d", b=4) for _ in range(2)]
            hT = AR.bf16(8 * T).rearrange("p (k t) -> p k t", k=8)
            actT = AR.bf16(NF * T).rearrange("p (f t) -> p f t", f=NF)
            gate = AR.f32(D)
            wdr = [AR.bf16(D) for _ in range(4)]
            junk = AR.bf16(D)
            ss = AR.f32(4)
            xn = AR.bf16(4 * D).rearrange("p (b d) -> p b d", b=4)
            sg = [AR.f32(T) for _ in range(2)]
            tmp = [AR.f32(512) for _ in range(2)]
            gfin = AR.f32(D) if final_norm else None
            wup_v = w_up[which].rearrange("(k p) n -> p k n", p=128)
            wdv = w_dn[which]
            if which != 0:
                for gi, (f0, f1) in enumerate(WUP_GROUPS):
                    for half in range(2):
                        c0, c1 = half * DFF + f0 * 128, half * DFF + f1 * 128
                        dma("pool", wup[:, :, c0:c1], wup_v[:, :, c0:c1], w=[("wup", gi)])
            if final_norm:
                dma("sp", gfin, g_final[0:1, :].partition_broadcast(128).rearrange("p a n -> p (a n)"), w=["gfin"])

            def load_x(i):
                src, _, nb, np_, sample = tiles[i]
                xb = XB[i % 2]
                if sample:
                    dma("sp", xb[0:np_, 0, :], src, w=[("xb", i % 2)])
                else:
                    dma("sp", xb[:, 0:nb, :], src.rearrange("(b p) d -> p b d", p=128), w=[("xb", i % 2)])

            ss2 = AR.f32(4)
            scr = (junk, ss, xn)
            load_x(0)
            normA(XB[0], ("xb", 0), tiles[0][2], tiles[0][3], scr)
            transp(tiles[0][2], tiles[0][3], n, tiles[0][4], hT, scr)
            cur_gate = None
            for i, (src, dst, nb, np_, sample) in enumerate(tiles):
                xb = XB[i % 2]
                xk = ("xb", i % 2)
                ntok = (nb - 1) * 128 + np_
                if i + 1 < len(tiles):
                    load_x(i + 1)
                if cur_gate != sample:
                    load_gate(gate, n, sample)
                    ts("dve", gate, gate, 0.5, None, ALU.mult, None, ["gate"], ["gate"])
                    cur_gate = sample
                for f in range(4):
                    dma("pool", wdr[f], wdv[f * 128:(f + 1) * 128, :], w=[("wd", f)])
                for f in range(NF):
                    pa, pka = psb()
                    pb, pkb = psb()
                    wk = ("wup", [gi for gi, (f0, f1) in enumerate(WUP_GROUPS) if f0 <= f < f1][0])
                    for kc in range(8):
                        mm(pa[:, 0:ntok], wup[:, kc, f * 128:(f + 1) * 128], hT[:, kc, 0:ntok], kc == 0, kc == 7, ["hT", wk], [pka])
                    for kc in range(8):
                        mm(pb[:, 0:ntok], wup[:, kc, DFF + f * 128:DFF + (f + 1) * 128], hT[:, kc, 0:ntok], kc == 0, kc == 7, ["hT", wk], [pkb])
                    sgb = sg[f % 2]
                    act(sgb[:, 0:ntok], pa[:, 0:ntok], AF.Silu, [pka], [("sg", f % 2)])
                    tt("dve", actT[:, f, 0:ntok], sgb[:, 0:ntok], pb[:, 0:ntok], ALU.mult, [("sg", f % 2), pkb], [("actT", f)])
                    if f == 10 and i + 1 < len(tiles):
                        normA(XB[(i + 1) % 2], ("xb", (i + 1) % 2), tiles[i + 1][2], tiles[i + 1][3], scr)
                if i + 1 < len(tiles):
                    transp(tiles[i + 1][2], tiles[i + 1][3], n, tiles[i + 1][4], hT, scr)
                accs = {}
                for b in range(nb):
                    for h in range(2):
                        accs[(b, h)] = psb()
                for f in range(NF):
                    for b in range(nb):
                        for h in range(2):
                            pacc, pkk = accs[(b, h)]
                            mm(pacc[0:np_, :], actT[:, f, b * 128:b * 128 + np_], wdr[f % 4][:, h * 512:(h + 1) * 512], f == 0, f == NF - 1,
                               [("actT", f), ("wd", f % 4)], [pkk])
                    if f + 4 < NF:
                        dma("pool", wdr[f % 4], wdv[(f + 4) * 128:(f + 5) * 128, :], w=[("wd", f % 4)])
                k = 0
                for b in range(nb):
                    for h in range(2):
                        pacc, pkk = accs[(b, h)]
                        tb = tmp[k % 2]
                        tt("dve", tb[0:np_, :], pacc[0:np_, :], gate[0:np_, h * 512:(h + 1) * 512], ALU.mult, [pkk, "gate"], [("tmp", k % 2)])
                        tt("pool", xb[0:np_, b, h * 512:(h + 1) * 512], tb[0:np_, :], xb[0:np_, b, h * 512:(h + 1) * 512], ALU.add,
                           [("tmp", k % 2), xk], [xk])
                        k += 1
                if final_norm:
                    for b in range(nb):
                        act(junk[0:np_, :], xb[0:np_, b, :], AF.Square, [xk], ["junk", "ss2"], accum_out=ss2[0:np_, b:b + 1])
                    rsqrt_small(ss2[0:np_, 0:nb], ss2[0:np_, 0:nb], 1.0 / D, ["ss2"], ["ss2"])
                    for b in range(nb):
                        S.op("dve", lambda e, b=b, xb=xb, np_=np_: e.scalar_tensor_tensor(
                            out=xb[0:np_, b, :], in0=xb[0:np_, b, :], scalar=ss2[0:np_, b:b + 1], in1=gfin[0:np_, :],
                            op0=ALU.mult, op1=ALU.mult), reads=[xk, "ss2", "gfin"], writes=[xk])
                if sample:
                    dma("sp", dst, xb[0:np_, 0, :], r=[xk], w=[("dst", i)], is_output=final_norm)
                else:
                    dma("sp", dst.rearrange("(b p) d -> p b d", p=128), xb[:, 0:nb, :], r=[xk], w=[("dst", i)], is_output=final_norm)
            S.barrier()

        if "A" in phases:
            tilesA = [(xp[t * T:(t + 1) * T, :], x1d[t * T:(t + 1) * T, :], 4, 128, False) for t in range(NTA)]
            tilesA.append((xs[:, :], x1d[TP:TP + 32, :], 1, 32, True))
            ffn_phase(0, tilesA, False)

        def qknorm(ps_t, pk, np_, gtile, sq, sqi, ssh, kn, kni, kb, idx, want_f32):
            act(sq[0:np_, :], ps_t[0:np_, :], AF.Square, [pk], [("sq", sqi)])
            S.op("dve", lambda e: e.tensor_reduce(out=ssh[0:np_, :], in_=sq[0:np_, :].rearrange("p (h d) -> p h d", h=NH),
                                                  axis=AX.X, op=ALU.add), reads=[("sq", sqi)], writes=[("ssh", idx)])
            rsqrt_small(ssh[0:np_, :], ssh[0:np_, :], 1.0 / DH, [("ssh", idx)], [("ssh", idx)])
            tt("dve", kb[0:np_, :].rearrange("p (h d) -> p h d", h=NH), ps_t[0:np_, :].rearrange("p (h d) -> p h d", h=NH),
               ssh[0:np_, :].unsqueeze(2).to_broadcast([np_, NH, DH]), ALU.mult, [pk, ("ssh", idx)], [("kb", idx)])
            if want_f32:
                tt("dve", kn[0:np_, :].rearrange("p (h d) -> p h d", h=NH), ps_t[0:np_, :].rearrange("p (h d) -> p h d", h=NH),
                   ssh[0:np_, :].unsqueeze(2).to_broadcast([np_, NH, DH]), ALU.mult, [pk, ("ssh", idx)], [("kn", kni)])
                tt("pool", kn[0:np_, :], kn[0:np_, :], gtile[0:np_, :], ALU.mult, [("kn", kni), "gq_t", "gk_t"], [("kn", kni)])

        lfsd = dscr("lfsd", [32, NH])

        def mix_setup(emit=True):
            wout = AR.bf16(8 * D).rearrange("p (k n) -> p k n", k=8)
            diag = AR.bf16(4 * CW * 128).rearrange("p (c j m) -> p c j m", c=4, j=CW)
            if emit:
                mix_emit_wout(wout)
                for c in range(4):
                    mix_emit_diag(diag, c)
            return wout, diag

        def mix_emit_wout(wout):
            dma("pool", wout, w_out.rearrange("(k p) n -> p k n", p=128), w=["wout"])

        def mix_emit_diag(diag, c):
            for j in range(CW):
                if j % 2 == 0:
                    ts("dve", diag[:, c, j, :], identf, wcol[:, c, j:j + 1], None, ALU.mult, None, ["identf", "wcol"], [("diag", c)])
                else:
                    act(diag[:, c, j, :], identf, AF.Identity, ["identf", "wcol"], [("diag", c)], scale=wcol[:, c, j:j + 1])

        def proj_phase():
            AR.reset()
            stage_c = "C" in phases
            if stage_c:
                wout_c, diag_c = mix_setup(emit=False)
            win = AR.bf16(8 * DIN).rearrange("p (k n) -> p k n", k=8)
            XB = [AR.f32(4 * D).rearrange("p (b d) -> p b d", b=4) for _ in range(2)]
            hTs = [AR.bf16(8 * T).rearrange("p (k t) -> p k t", k=8) for _ in range(2)]
            junk = AR.bf16(D)
            ss = AR.f32(4)
            xn = AR.bf16(4 * D).rearrange("p (b d) -> p b d", b=4)
            sq = [AR.f32(512) for _ in range(2)]
            ssh = [AR.f32(NH) for _ in range(4)]
            kn = [AR.f32(512) for _ in range(2)]
            kb = [AR.bf16(512) for _ in range(4)]
            vf = [AR.f32(512) for _ in range(2)]
            vb = [AR.bf16(512) for _ in range(2)]
            kT_sb = AR.bf16(4 * T).rearrange("p (c t) -> p c t", c=4)
            qT_sb = AR.bf16(4 * T).rearrange("p (c t) -> p c t", c=4)
            uT_sb = AR.bf16(4 * T).rearrange("p (c t) -> p c t", c=4)
            sgm = [AR.f32(T) for _ in range(2)]
            lx = AR.f32(4 * NH)
            la = AR.f32(4 * NH)
            lm = AR.f32(4 * NH)
            lf = AR.f32(4 * NH).rearrange("p (b h) -> p b h", b=4)
            cumF_sb = AR.f32(4 * NH)
            utk = AR.f32(512)
            win_v = w_in.rearrange("(k p) n -> p k n", p=128)
            for kc in range(8):
                dma("pool", win[:, kc, :], win_v[:, kc, :], w=[("win", kc)])
            wink = [("win", kc) for kc in range(8)]
            kTd_v = kTd.rearrange("(c p) t -> p c t", p=128)
            qTd_v = qTd.rearrange("(c p) t -> p c t", p=128)
            uTd_v = uTd.rearrange("(c p) t -> p c t", p=128)
            PSF = PS[7]
            pkf = ("ps", 7)
            reserved.add(7)
            psb7 = psb

            tiles = [(t * T, 4, 128, False, (t - NTO) if t >= NTO else None) for t in range(NTA)]
            tiles.append((TP, 1, 32, True, NTO))

            def load_x(i):
                row0, nb, np_, sample, _ = tiles[i]
                xb = XB[i % 2]
                if sample:
                    dma("sp", xb[0:np_, 0, :], x1d[row0:row0 + np_, :], w=[("xb", i % 2)])
                else:
                    dma("sp", xb[:, 0:nb, :], x1d[row0:row0 + nb * 128, :].rearrange("(b p) d -> p b d", p=128), w=[("xb", i % 2)])

            scr = (junk, ss, xn)
            load_x(0)
            normA(XB[0], ("xb", 0), tiles[0][1], tiles[0][2], scr)
            transp(tiles[0][1], tiles[0][2], 1, tiles[0][3], hTs[0], scr, ("hT", 0))
            cnt = [0]
            pending_cs = []
            for i, (row0, nb, np_, sample, own) in enumerate(tiles):
                xb = XB[i % 2]
                xk = ("xb", i % 2)
                hT = hTs[i % 2]
                hk = ("hT", i % 2)
                ntok = (nb - 1) * 128 + np_
                if i + 1 < len(tiles):
                    load_x(i + 1)
                fo = 128 * (i % 2)
                pkf = ("psf", i % 2)
                for b in range(nb):
                    for kc in range(8):
                        mm(PSF[0:np_, fo + b * NH:fo + (b + 1) * NH], hT[:, kc, b * 128:b * 128 + np_], win[:, kc, 1536:1536 + NH], kc == 0, kc == 7,
                           [hk, ("win", kc)], [pkf])
                kout, vout = (ks, vs) if sample else (kp, vp)
                pending = []

                def flush():
                    for (part, b_, ix_) in pending:
                        ptt, pkt = psb7()
                        for c in range(4):
                            mm(ptt[:, c * 128:c * 128 + np_], kb[ix_][0:np_, c * 128:(c + 1) * 128], ident[0:np_, 0:np_], True, True,
                               [("kb", ix_), "ident"], [pkt])
                        dst_sb = kT_sb if part == "k" else qT_sb
                        act(dst_sb[:, :, b_ * 128:b_ * 128 + np_], ptt[:, :].rearrange("p (c t) -> p c t", c=4)[:, :, 0:np_], AF.Identity,
                            [pkt, "gcolqk"], [("T" + part,)], scale=gcolqk[:, (0 if part == "q" else 1):(1 if part == "q" else 2)])
                    del pending[:]

                for b in range(nb):
                    r0 = 0 if sample else row0 + b * 128
                    parts = ["k", "v"] + (["q"] if own is not None else [])
                    newp = []
                    for part in parts:
                        c0 = {"q": 0, "k": 512, "v": 1024}[part]
                        pt_, pk_ = psb7()
                        for kc in range(8):
                            mm(pt_[0:np_, :], hT[:, kc, b * 128:b * 128 + np_], win[:, kc, c0:c0 + 512], kc == 0, kc == 7,
                               [hk, ("win", kc)], [pk_])
                        if part == "v":
                            ix = cnt[0] % 2
                            cnt[0] += 1
                            cp("act", vf[ix][0:np_, :], pt_[0:np_, :], [pk_], [("vf", ix)])
                            dma("sp", vout[r0:r0 + np_, :], vf[ix][0:np_, :], r=[("vf", ix)], w=[("vout", i, b)], is_output=True)
                            cp("pool", vb[ix][0:np_, :], vf[ix][0:np_, :], [("vf", ix)], [("vb", ix)])
                            dma("sp", vSd[row0 + b * 128:row0 + b * 128 + np_, :], vb[ix][0:np_, :], r=[("vb", ix)], w=[("vSd", i, b)])
                            continue
                        ix = (2 * b + (1 if part == "q" else 0)) % 4
                        qknorm(pt_, pk_, np_, gq_t if part == "q" else gk_t, sq[ix % 2], ix % 2, ssh[ix], kn[b % 2], b % 2, kb[ix], ix, part == "k")
                        if part == "k":
                            dma("sp", kout[r0:r0 + np_, :], kn[b % 2][0:np_, :], r=[("kn", b % 2)], w=[("kout", i, b)], is_output=True)
                        newp.append((part, b, ix))
                    flush()
                    pending.extend(newp)
                    if b == 0:
                        while pending_cs:
                            pending_cs.pop(0)()
                    if b == 0 and i + 1 < len(tiles):
                        normA(XB[(i + 1) % 2], ("xb", (i + 1) % 2), tiles[i + 1][1], tiles[i + 1][2], scr)
                uproj_pending = True
                u0 = 0 if (own is not None) else ntok - 128
                for c in range(4):
                    pa, pka = psb7()
                    pg, pkg = psb7()
                    for kc in range(8):
                        mm(pa[:, u0:ntok], win[:, kc, 1544 + c * 128:1544 + (c + 1) * 128], hT[:, kc, u0:ntok], kc == 0, kc == 7,
                           [hk, ("win", kc)], [pka])
                    for kc in range(8):
                        mm(pg[:, u0:ntok], win[:, kc, 2056 + c * 128:2056 + (c + 1) * 128], hT[:, kc, u0:ntok], kc == 0, kc == 7,
                           [hk, ("win", kc)], [pkg])
                    act(sgm[c % 2][:, u0:ntok], pg[:, u0:ntok], AF.Sigmoid, [pkg], [("sgm", c % 2)])
                    tt("dve", uT_sb[:, c, u0:ntok], pa[:, u0:ntok], sgm[c % 2][:, u0:ntok], ALU.mult, [pka, ("sgm", c % 2)], ["uT_sb"])
                flush()
                dma("sp", kTd_v[:, :, row0:row0 + ntok], kT_sb[:, :, 0:ntok], r=[("Tk",)], w=[("kTd", i)])
                if own is not None:
                    dma("sp", qTd_v[:, :, own * T:own * T + ntok], qT_sb[:, :, 0:ntok], r=[("Tq",)], w=[("qTd", i)])
                dma("sp", uTd_v[:, :, row0 + u0:row0 + ntok], uT_sb[:, :, u0:ntok], r=["uT_sb"], w=[("uTd", i)])
                if sample or i == NTA - 1:
                    bl = nb - 1
                    pa, pka = psb7()
                    pg, pkg = psb7()
                    for kc in range(8):
                        mm(pa[0:np_, :], hT[:, kc, bl * 128:bl * 128 + np_], win[:, kc, 1544:2056], kc == 0, kc == 7, [hk, ("win", kc)], [pka])
                    for kc in range(8):
                        mm(pg[0:np_, :], hT[:, kc, bl * 128:bl * 128 + np_], win[:, kc, 2056:2568], kc == 0, kc == 7, [hk, ("win", kc)], [pkg])
                    act(sgm[0][0:np_, :], pg[0:np_, :], AF.Sigmoid, [pkg], [("sgm", 0)])
                    tt("dve", utk[0:np_, :], pa[0:np_, :], sgm[0][0:np_, :], ALU.mult, [pka, ("sgm", 0)], ["utk"])
                    if sample:
                        for j in range(2):
                            dma("sp", cvs[j, 14:30, :], utk[16 * j:16 * j + 16, :], r=["utk"], w=[("cvs", j)], is_output=True)
                            dma("sp", cvs[j, 0:14, :], sconv[j, 16:30, :], w=[("cvs0", j)], is_output=True)
                    else:
                        dma("sp", cvp[:, :], utk[98:128, :], r=["utk"], w=["cvp"], is_output=True)
                nl = nb * NH
                tt("dve", lx[0:np_, 0:nl].rearrange("p (b h) -> p b h", h=NH), PSF[0:np_, fo:fo + nl].rearrange("p (b h) -> p b h", h=NH),
                   bf_t[0:np_, :].unsqueeze(1).to_broadcast([np_, nb, NH]), ALU.add, [pkf, "bf_t"], ["lx"])
                ts("dve", lm[0:np_, 0:nl], lx[0:np_, 0:nl], -1.0, None, ALU.mult, None, ["lx"], ["lm"])
                tt("dve", la[0:np_, 0:nl], lx[0:np_, 0:nl], lm[0:np_, 0:nl], ALU.max, ["lx", "lm"], ["la"])
                act(la[0:np_, 0:nl], la[0:np_, 0:nl], AF.Exp, ["la"], ["la"], scale=-1.0)
                act(la[0:np_, 0:nl], la[0:np_, 0:nl], AF.Ln, ["la"], ["la"], bias=onesf[0:np_, 0:1])
                ts("dve", lm[0:np_, 0:nl], lx[0:np_, 0:nl], 0.0, None, ALU.min, None, ["lx"], ["lm"])
                lf2 = lf[0:np_, 0:nb, :]
                tt("dve", lf2, lm[0:np_, 0:nl].rearrange("p (b h) -> p b h", h=NH), la[0:np_, 0:nl].rearrange("p (b h) -> p b h", h=NH),
                   ALU.subtract, ["lm", "la"], ["lf"])
                if sample:
                    dma("sp", lfs[:, :], lf[0:np_, 0, :], r=["lf"], w=["lfs"], is_output=True)
                    dma("sp", lfsd[:, :], lf[0:np_, 0, :], r=["lf"], w=["lfsd"])
                else:
                    dma("sp", lfp[row0:row0 + T, :].rearrange("(b p) h -> p b h", p=128), lf[:, 0:nb, :], r=["lf"], w=[("lfp", i)], is_output=True)
                    def cumsum_job(i=i, row0=row0, fo=fo, pkf=pkf, nb=nb):
                        memset("dve", accF, 0.0, ["accF"])
                        for b in range(nb):
                            mm(PSF[:, fo + 64 + b * NH:fo + 64 + (b + 1) * NH], UT, lf[:, b, :], True, False, ["UT", "lf"], [pkf])
                            mm(PSF[:, fo + 64 + b * NH:fo + 64 + (b + 1) * NH], onesf, accF, False, True, ["onesf", "accF"], [pkf])
                            tt("dve", accF, accF, lf[:, b, :], ALU.add, ["accF", "lf"], ["accF"])
                        cp("dve", cumF_sb, PSF[:, fo + 64:fo + 64 + 4 * NH], [pkf], ["cumF_sb"])
                        dma("sp", cumFd[row0:row0 + T, :].rearrange("(b p) h -> p b h", p=128), cumF_sb.rearrange("p (b h) -> p b h", h=NH),
                            r=["cumF_sb"], w=[("cumFd", i)])
                    pending_cs.append(cumsum_job)
                if i + 1 < len(tiles):
                    transp(tiles[i + 1][1], tiles[i + 1][2], 1, tiles[i + 1][3], hTs[(i + 1) % 2], scr, ("hT", (i + 1) % 2))
                if stage_c and len(tiles) >= 6:
                    if 1 <= i <= 4:
                        mix_emit_diag(diag_c, i - 1)
                    elif i == 5:
                        mix_emit_wout(wout_c)
            while pending_cs:
                pending_cs.pop(0)()
            if stage_c and len(tiles) < 6:
                mix_emit_wout(wout_c)
                for c in range(4):
                    mix_emit_diag(diag_c, c)
            reserved.clear()
            S.barrier()

        if "B" in phases:
            proj_phase()

        def conv_a(rhs_of, ntok, segs, diag, bufs, ukey):
            yf, y2, mean, ex2, rstd, t1 = bufs
            for c in range(4):
                pc, pkc = psb()
                for si, (c0, c1) in enumerate(segs):
                    for tap in range(CW):
                        mm(pc[:, c0:c1], diag[:, c, tap, :], rhs_of(c, si, tap), tap == 0, tap == CW - 1, [("diag", c), ukey], [pkc])
                act(yf[:, c, 0:ntok], pc[:, 0:ntok], AF.Identity, [pkc], [("yf", c)], bias=ccol[:, c, 0:1])
                act(y2[:, c, 0:ntok], yf[:, c, 0:ntok], AF.Square, [("yf", c)], [("y2", c)])

        def conv_b(ntok, catT, bufs):
            yf, y2, mean, ex2, rstd, t1 = bufs
            p1, pk1 = psb()
            p2, pk2 = psb()
            for c in range(4):
                mm(p1[:, 0:ntok], onesf, yf[:, c, 0:ntok], c == 0, c == 3, ["onesf", ("yf", c)], [pk1])
            for c in range(4):
                mm(p2[:, 0:ntok], onesf, y2[:, c, 0:ntok], c == 0, c == 3, ["onesf", ("y2", c)], [pk2])
            ts("dve", mean[:, 0:ntok], p1[:, 0:ntok], 1.0 / 512, None, ALU.mult, None, [pk1], ["mean"])
            ts("dve", ex2[:, 0:ntok], p2[:, 0:ntok], 1.0 / 512, None, ALU.mult, None, [pk2], ["ex2"])
            tt("pool", rstd[:, 0:ntok], mean[:, 0:ntok], mean[:, 0:ntok], ALU.mult, ["mean"], ["rstd"])
            tt("dve", rstd[:, 0:ntok], ex2[:, 0:ntok], rstd[:, 0:ntok], ALU.subtract, ["ex2", "rstd"], ["rstd"])
            act(rstd[:, 0:ntok], rstd[:, 0:ntok], AF.Ln, ["rstd"], ["rstd"], bias=epsb[:, 0:1], scale=1.0)
            act(rstd[:, 0:ntok], rstd[:, 0:ntok], AF.Exp, ["rstd"], ["rstd"], scale=-0.5)
            for c in range(4):
                tb = t1[c % 2]
                tt("dve", tb[:, 0:ntok], yf[:, c, 0:ntok], mean[:, 0:ntok], ALU.subtract, [("yf", c), "mean"], [("t1", c % 2)])
                tt("pool", tb[:, 0:ntok], tb[:, 0:ntok], rstd[:, 0:ntok], ALU.mult, [("t1", c % 2), "rstd"], [("t1", c % 2)])
                act(catT[:, 4 + c, 0:ntok], tb[:, 0:ntok], AF.Silu, [("t1", c % 2)], [("catT", 4 + c)], bias=ccol[:, c, 2:3], scale=ccol[:, c, 1:2])

        def attn_finish_group(items, catT, ncol, col0, recs, rshs):
            geo = []
            for i, (acc, pka, hl, cc) in enumerate(items):
                o0, d0 = (0, 64) if hl % 2 == 0 else (64, 0)
                geo.append((o0, d0))
                recip(recs[i][d0:d0 + 1, 0:ncol], acc[d0:d0 + 1, 0:ncol], [pka], [("rec", i)])
            for i, (acc, pka, hl, cc) in enumerate(items):
                o0, d0 = geo[i]
                pb_, pkb_ = psb()
                mm(pb_[:, 0:ncol], onesf[d0:d0 + 1, 0:128], recs[i][d0:d0 + 1, 0:ncol], True, True, ["onesf", ("rec", i)], [pkb_])
                cp("act", rshs[i][o0:o0 + 64, 0:ncol], pb_[o0:o0 + 64, 0:ncol], [pkb_], [("rsh", i)])
            for i, (acc, pka, hl, cc) in enumerate(items):
                o0, d0 = geo[i]
                tt("dve", catT[o0:o0 + 64, cc, col0:col0 + ncol], acc[o0:o0 + 64, 0:ncol], rshs[i][o0:o0 + 64, 0:ncol], ALU.mult,
                   [pka, ("rsh", i)], [("catT", cc)])

        def attn_finish_start(items, catT, accsb):
            jobs = []
            geo = []
            for i, (acc, pka, hl, cc) in enumerate(items):
                o0, d0 = (0, 64) if hl % 2 == 0 else (64, 0)
                geo.append((o0, d0))
                cp("act" if i % 2 == 0 else "dve", accsb[i], acc[:, 0:T], [pka], [("accsb", i)])
            for i, (acc, pka, hl, cc) in enumerate(items):
                o0, d0 = geo[i]
                recip(accsb[i][d0:d0 + 1, :], accsb[i][d0:d0 + 1, :], [("accsb", i)], [("accsb", i)])

                def job(i=i, o0=o0, d0=d0, cc=cc):
                    pb_, pkb_ = psb()
                    mm(pb_[:, 0:T], onesf[d0:d0 + 1, 0:128], accsb[i][d0:d0 + 1, :], True, True, ["onesf", ("accsb", i)], [pkb_])
                    tt("dve", catT[o0:o0 + 64, cc, 0:T], accsb[i][o0:o0 + 64, :], pb_[o0:o0 + 64, 0:T], ALU.mult,
                       [("accsb", i), pkb_], [("catT", cc)])
                jobs.append(job)
            return jobs

        def out_proj(catT, wout, xb, xk, nb, np_, gate, tmp):
            k = 0
            for b in range(nb):
                for h in range(2):
                    po, pko = psb()
                    for c in range(8):
                        mm(po[0:np_, :], catT[:, c, b * 128:b * 128 + np_], wout[:, c, h * 512:(h + 1) * 512], c == 0, c == 7,
                           [("catT", c), "wout"], [pko])
                    tb = tmp[k % 2]
                    tt("dve", tb[0:np_, :], po[0:np_, :], gate[0:np_, h * 512:(h + 1) * 512], ALU.mult, [pko, "gate"], [("tmp", k % 2)])
                    tt("pool", xb[0:np_, b, h * 512:(h + 1) * 512], tb[0:np_, :], xb[0:np_, b, h * 512:(h + 1) * 512], ALU.add,
                       [("tmp", k % 2), xk], [xk])
                    k += 1

        def mix_prompt():
            AR.reset()
            wout, diag = mix_setup(emit=("B" not in phases))
            NBK = NTA * 4
            Fg = AR.f32(NBK * NH).rearrange("p (j h) -> p j h", h=NH)
            Fst = AR.f32(NTO * NH).rearrange("p (m h) -> p m h", h=NH)
            Fen = AR.f32(NTO * NH).rearrange("p (m h) -> p m h", h=NH)
            Rc = AR.f32(NTO * NH).rearrange("p (m h) -> p m h", h=NH)
            gate = AR.f32(D)
            XB = [AR.f32(4 * D).rearrange("p (b d) -> p b d", b=4) for _ in range(2)]
            qT_sb = [AR.bf16(NH * T).rearrange("p (h t) -> p h t", h=NH) for _ in range(2)]
            UW = T + 32
            uh = [AR.bf16(4 * UW).rearrange("p (c t) -> p c t", c=4) for _ in range(2)]
            catT = AR.bf16(8 * T).rearrange("p (c t) -> p c t", c=8)
            halo = [AR.bf16(4 * 2 * 32).rearrange("p (c w t) -> p c w t", c=4, w=2)[:, :, :, 0:30] for _ in range(2)]
            kTs = [AR.bf16(2 * T).rearrange("p (c t) -> p c t", c=2) for _ in range(2)]
            Vst = [AR.bf16(4 * 256).rearrange("p (b f) -> p b f", b=4) for _ in range(2)]
            Vp = [AR.bf16(4 * 4 * 128).rearrange("p (b h m) -> p b h m", b=4, h=4) for _ in range(2)]
            pT = [AR.bf16(T) for _ in range(4)]
            bias = [AR.f32(4 * NH).rearrange("p (b h) -> p b h", h=NH) for _ in range(2)]
            accsb = [AR.f32(T) for _ in range(4)]
            fin_jobs = []
            yf = AR.f32(4 * T).rearrange("p (c t) -> p c t", c=4)
            y2 = AR.f32(4 * T).rearrange("p (c t) -> p c t", c=4)
            mean = AR.f32(T); ex2 = AR.f32(T); rstd = AR.f32(T)
            t1 = [AR.f32(T) for _ in range(2)]
            tmp = [AR.f32(512) for _ in range(2)]

            dma("sp", Fg, cumFd.rearrange("(j p) h -> p j h", p=128), w=["Fg"])
            own_v = cumFd[NTO * T:NTA * T, :].rearrange("(m t) h -> m t h", t=T)
            dma("sp", Fst, own_v[:, 0, :].partition_broadcast(128), w=["Fst"])
            dma("sp", Fen, own_v[:, T - 1, :].partition_broadcast(128), w=["Fen"])
            tot = AR.f32(NH)
            Lsb = AR.f32(NTA)
            rhs3 = AR.f32(NTA * NH)
            off = AR.f32(NTA * NH).rearrange("p (j h) -> p j h", h=NH)
            dma("sp", tot[0:NTA, :], cumFd.rearrange("(j t) h -> j t h", t=T)[:, T - 1, :], w=["tot"])
            dma("sp", Lsb[0:NTA, :], lmat[:, :], w=["Lsb"])
            tt("dve", rhs3[0:NTA, :].rearrange("p (j h) -> p j h", h=NH), Lsb[0:NTA, :].unsqueeze(2).to_broadcast([NTA, NTA, NH]),
               tot[0:NTA, :].unsqueeze(1).to_broadcast([NTA, NTA, NH]), ALU.mult, ["Lsb", "tot"], ["rhs3"])
            pof, pko = psb()
            mm(pof[:, 0:NTA * NH], onesf[0:NTA, :], rhs3[0:NTA, :], True, True, ["onesf", "rhs3"], [pko])
            cp("dve", off, pof[:, 0:NTA * NH].rearrange("p (j h) -> p j h", h=NH), [pko], ["off"])
            tt("dve", Fg.rearrange("p (j b) h -> p j b h", b=4), Fg.rearrange("p (j b) h -> p j b h", b=4),
               off.unsqueeze(2).to_broadcast([128, NTA, 4, NH]), ALU.add, ["Fg", "off"], ["Fg"])
            tt("dve", Fst, Fst, off[:, NTO:NTA, :], ALU.add, ["Fst", "off"], ["Fst"])
            tt("dve", Fen, Fen, off[:, NTO:NTA, :], ALU.add, ["Fen", "off"], ["Fen"])
            tt("dve", Rc, Fst, Fen, ALU.add, ["Fst", "Fen"], ["Rc"])
            ts("dve", Rc, Rc, 0.5, cshift[:, 0:1], ALU.mult, ALU.subtract, ["Rc", "cshift"], ["Rc"])
            load_gate(gate, 1, False)
            for vb_ in Vp:
                memset("pool", vb_, 1.0, ["Vp0", "Vp1"])
            for qi, qb_ in enumerate(qT_sb):
                memset("dve", qb_, 0.0, [("qT", qi)])
            qTd_h = qTd.rearrange("(c e d) t -> e d c t", e=2, d=64)
            kTd_v = kTd.rearrange("(c p) t -> p c t", p=128)
            qTd_v = qTd.rearrange("(c p) t -> p c t", p=128)
            uTd_v = uTd.rearrange("(c p) t -> p c t", p=128)
            for i in (0, 1, 2, 3):
                reserved.add(i)

            def load_x1(m):
                g = NTO + m
                dma("sp", XB[m % 2][:, :, :], x1d[g * T:(g + 1) * T, :].rearrange("(b p) d -> p b d", p=128), w=[("xb", m % 2)])

            def load_tile(m):
                g = NTO + m
                q4 = qT_sb[m % 2].rearrange("p (c e) t -> p c e t", e=2)
                for e_ in range(2):
                    dma("sp", q4[64 * e_:64 * e_ + 64, :, e_, :], qTd_h[e_, :, :, m * T:(m + 1) * T], w=[("qT", m % 2)])
                dma("sp", uh[m % 2][:, :, 32:UW], uTd_v[:, :, g * T:(g + 1) * T], w=[("uh", m % 2)])
                po_ = (NTO + m - 1) if m >= 1 else 0
                dma("sp", halo[m % 2][:, :, 0, :], uTd_v[:, :, po_ * T + T - 30:(po_ + 1) * T], w=[("halo", m % 2)])
                dma("sp", halo[m % 2][:, :, 1, :], uTd_v[:, :, m * T + T - 30:(m + 1) * T], w=[("halo", m % 2)])

            load_tile(0)
            kvn = [0]
            ptn = [0]
            kvmap = {}
            cbufs = (yf, y2, mean, ex2, rstd, t1)

            def conv_front(m):
                uhh = uh[m % 2]; uk = ("uh", m % 2)
                hl_ = halo[m % 2]
                ts("pool", hl_[:, :, 0, :], hl_[:, :, 0, :], flag_t[:, NTO + m:NTO + m + 1], None, ALU.mult, None, [("halo", m % 2), "flag"], [("halo", m % 2)])
                S.op("dve", lambda e: e.scalar_tensor_tensor(
                    out=uhh[:, :, 2:32], in0=hl_[:, :, 1, :], scalar=flag_t[:, 2 * NTO + m:2 * NTO + m + 1], in1=hl_[:, :, 0, :],
                    op0=ALU.mult, op1=ALU.add), reads=[("halo", m % 2), "flag"], writes=[uk])

                def rhs_of(c, si, tap):
                    return uhh[:, c, 2 + tap:2 + tap + T]
                conv_a(rhs_of, T, [(0, T)], diag, cbufs, uk)

            conv_front(0)
            for m in range(NTO):
                g = NTO + m
                xb = XB[m % 2]; xk = ("xb", m % 2)
                qt = qT_sb[m % 2]; qk = ("qT", m % 2)
                uhh = uh[m % 2]; uk = ("uh", m % 2)
                if m + 1 < NTO:
                    load_tile(m + 1)
                for hg in range(2):
                    accs = [(PS[hl], ("ps", hl)) for hl in range(4)]
                    def prep(p, hg=hg, m=m):
                        if (m, hg, p) in kvmap:
                            return
                        bi = kvn[0] % 2
                        kvn[0] += 1
                        kvmap[(m, hg, p)] = bi
                        kb_, vs_, vp_, bs_ = kTs[bi], Vst[bi], Vp[bi], bias[bi]
                        dma("sp", kb_, kTd_v[:, 2 * hg:2 * hg + 2, p * T:(p + 1) * T], w=[("kTs", bi)])
                        dma("sp", vs_, vSd[p * T:(p + 1) * T, hg * 256:(hg + 1) * 256].rearrange("(b s) f -> s b f", s=128), w=[("Vst", bi)])
                        vs5 = vs_.rearrange("p b (c e d) -> p b c e d", c=2, e=2)
                        vp5 = vp_.rearrange("p b (c e) m -> p b c e m", c=2)
                        cp("pool", vp5[:, :, :, 0, 0:64], vs5[:, :, :, 0, :], [("Vst", bi)], ["Vp%d" % bi])
                        cp("dve", vp5[:, :, :, 1, 64:128], vs5[:, :, :, 1, :], [("Vst", bi)], ["Vp%d" % bi])
                        tt("dve", bs_, Rc[:, m:m + 1, :].to_broadcast([128, 4, NH]), Fg[:, p * 4:(p + 1) * 4, :], ALU.subtract, ["Rc", "Fg"], [("bias", bi)])
                        if p == m:
                            ts("dve", bs_, bs_, flag_t[:, m:m + 1], None, ALU.add, None, [("bias", bi), "flag"], [("bias", bi)])

                    plist = list(range(m + 1)) + list(range(NTO, g + 1))
                    units = [(p, b, hl) for p in plist for b in range(4) for hl in range(4)]
                    ufirst = units[0][0]
                    stb = {}

                    def qk(u, hg=hg, qt=qt, qk_=qk, m_=m):
                        p, b, hl = u
                        prep(p)
                        bi = kvmap[(m_, hg, p)]
                        cc = hl // 2
                        r0 = (hl % 2) * 64
                        st, pks = psb()
                        stb[u] = (st, pks)
                        mm(st[:, 0:T], kTs[bi][:, cc, b * 128:(b + 1) * 128], qt[:, hg * 4 + hl, :], True, True,
                           [("kTs", bi), qk_], [pks])

                    def rest(u, hg=hg, g=g, ufirst=ufirst, m_=m):
                        p, b, hl = u
                        bi = kvmap[(m_, hg, p)]
                        h = hg * 4 + hl
                        st, pks = stb.pop(u)
                        pi = ptn[0] % 4
                        ptn[0] += 1
                        act(pT[pi], st[:, 0:T], AF.Exp, [pks, ("bias", bi)], [("pT", pi)], bias=bias[bi][:, b, h:h + 1], scale=0.125)
                        if p == g:
                            tt("dve", pT[pi], pT[pi], masks[:, b, :], ALU.mult, [("pT", pi), "masks"], [("pT", pi)])
                        acc, pka = accs[hl]
                        mm(acc[:, 0:T], Vp[bi][:, b, hl, :], pT[pi], (p == ufirst and b == 0), (p == g and b == 3),
                           ["Vp%d" % bi, ("pT", pi)], [pka])

                    LOOK = 3
                    for idx in range(min(LOOK, len(units))):
                        qk(units[idx])
                    for idx, u in enumerate(units):
                        rest(u)
                        if fin_jobs and idx in (6, 12, 18, 24):
                            fin_jobs.pop(0)()
                        if idx + LOOK < len(units):
                            qk(units[idx + LOOK])
                    while fin_jobs:
                        fin_jobs.pop(0)()
                    if hg == 0:
                        prep(0, hg=1, m=m)
                        load_x1(m)
                    elif m + 1 < NTO:
                        prep(0, hg=0, m=m + 1)
                    fin_jobs.extend(attn_finish_start([(accs[hl][0], accs[hl][1], hl, hg * 2 + hl // 2) for hl in range(4)], catT, accsb))
                conv_b(T, catT, cbufs)
                if m + 1 < NTO:
                    conv_front(m + 1)
                while fin_jobs:
                    fin_jobs.pop(0)()
                out_proj(catT, wout, xb, xk, 4, 128, gate, tmp)
                dma("sp", x2d[m * T:(m + 1) * T, :].rearrange("(b p) d -> p b d", p=128), xb[:, :, :], r=[xk], w=[("x2d", m)])
            reserved.clear()
            S.barrier()

        if "C" in phases:
            mix_prompt()

        def mix_sample():
            AR.reset()
            wout, diag = mix_setup(emit=("C" not in phases))
            gate = AR.f32(D)
            xb = AR.f32(D).rearrange("p (b d) -> p b d", b=1)
            qs = AR.bf16(4 * 32).rearrange("p (c t) -> p c t", c=4)
            kn_ = AR.bf16(4 * 32).rearrange("p (c t) -> p c t", c=4)
            us = AR.bf16(4 * 32).rearrange("p (c t) -> p c t", c=4)
            vnew = AR.bf16(512)
            VpN = AR.bf16(NH * 128).rearrange("p (h m) -> p h m", h=NH)
            uhs = AR.bf16(4 * 2 * 48).rearrange("p (c j t) -> p c j t", c=4, j=2)
            scf = AR.f32(512)
            scb = AR.bf16(512)
            catT = AR.bf16(8 * 32).rearrange("p (c t) -> p c t", c=8)
            ckbs = [AR.bf16(NPB * 512).rearrange("p (b f) -> p b f", b=NPB) for _ in range(2)]
            cvbs = [AR.bf16(NPB * 512).rearrange("p (b f) -> p b f", b=NPB) for _ in range(2)]
            lfcs = [AR.f32(NPB * NH).rearrange("p (b h) -> p b h", h=NH) for _ in range(2)]
            for j in range(2):
                dma("pool", ckbs[j], ck[j, :, :].rearrange("(b p) f -> p b f", p=128), w=[("ckb", j)])
                dma("pool", cvbs[j], cv[j, :, :].rearrange("(b p) f -> p b f", p=128), w=[("cvb", j)])
                dma("sp", lfcs[j], clf[j, :, :].rearrange("(b p) h -> p b h", p=128), w=[("lfc", j)])
            kTc = AR.bf16(4 * PAST).rearrange("p (c t) -> p c t", c=4)
            VpC = AR.bf16(NPB * NH * 128).rearrange("p (b h m) -> p b h m", b=NPB, h=NH)
            Fc = AR.f32(NPB * NH).rearrange("p (b h) -> p b h", h=NH)
            accS = AR.f32(NH)
            Rcs = AR.f32(NH)
            lfn = AR.f32(NH)
            Fn = AR.f32(NH)
            biasC = AR.f32(NPB * NH).rearrange("p (b h) -> p b h", h=NH)
            biasN = AR.f32(NH)
            pT = [AR.bf16(16) for _ in range(4)]
            rec = [AR.f32(16) for _ in range(2)]
            rsh = [AR.f32(16) for _ in range(2)]
            yf = AR.f32(4 * 32).rearrange("p (c t) -> p c t", c=4)
            y2 = AR.f32(4 * 32).rearrange("p (c t) -> p c t", c=4)
            mean = AR.f32(32); ex2 = AR.f32(32); rstd = AR.f32(32)
            t1 = [AR.f32(32) for _ in range(2)]
            tmp = [AR.f32(512) for _ in range(2)]
            kTd_v = kTd.rearrange("(c p) t -> p c t", p=128)
            qTd_v = qTd.rearrange("(c p) t -> p c t", p=128)
            uTd_v = uTd.rearrange("(c p) t -> p c t", p=128)

            load_gate(gate, 1, True)
            dma("sp", xb[0:32, 0, :], x1d[TP:TP + 32, :], w=["xb"])
            dma("sp", qs, qTd_v[:, :, TO:TO + 32], w=["qs"])
            dma("sp", kn_, kTd_v[:, :, TP:TP + 32], w=["kn_"])
            dma("sp", us, uTd_v[:, :, TP:TP + 32], w=["us"])
            memset("pool", VpC, 1.0, ["VpC"])
            for j in range(2):
                dma("sp", scf[0:30, :], sconv[j, :, :], w=["scf"])
                cp("dve", scb[0:30, :], scf[0:30, :], ["scf"], ["scb"])
                pst, pk = psb()
                for c in range(4):
                    mm(pst[:, c * 32:c * 32 + 30], scb[0:30, c * 128:(c + 1) * 128], ident[0:30, 0:30], True, True, ["scb", "ident"], [pk])
                cp("act", uhs[:, :, j, 0:30], pst[:, 0:128].rearrange("p (c t) -> p c t", c=4)[:, :, 0:30], [pk], ["uhs"])
                cp("dve", uhs[:, :, j, 30:46], us[:, :, 16 * j:16 * j + 16], ["us"], ["uhs"])

            def rhs_of(c, si, tap):
                return uhs[:, c, si, tap:tap + 16]
            conv_a(rhs_of, 32, [(0, 16), (16, 32)], diag, (yf, y2, mean, ex2, rstd, t1), "uhs")
            conv_b(32, catT, (yf, y2, mean, ex2, rstd, t1))
            for i in (0, 1):
                reserved.add(i)
            for j in range(2):
                ckb, cvb, lfc = ckbs[j], cvbs[j], lfcs[j]
                dma("sp", lfn[0:16, :], lfsd[16 * j:16 * j + 16, :], r=["lfsd"], w=["lfn"])
                dma("sp", vnew[0:16, :], vSd[TP + 16 * j:TP + 16 * j + 16, :], w=["vnew"])
                for blk in range(NPB):
                    pst, pk = psb()
                    for c in range(4):
                        mm(pst[:, c * 128:(c + 1) * 128], ckb[:, blk, c * 128:(c + 1) * 128], ident, True, True, [("ckb", j), "ident"], [pk])
                    cp("act" if blk % 2 == 0 else "dve", kTc[:, :, blk * 128:(blk + 1) * 128], pst[:, :].rearrange("p (c t) -> p c t", c=4), [pk], ["kTc"])
                    cv4 = cvb[:, blk, :].rearrange("p (c e d) -> p c e d", e=2, d=64)
                    vp4 = VpC[:, blk, :, :].rearrange("p (c e) m -> p c e m", e=2)
                    cp("pool", vp4[:, :, 0, 0:64], cv4[:, :, 0, :], [("cvb", j), "VpC"], ["VpC"])
                    cp("dve", vp4[:, :, 1, 64:128], cv4[:, :, 1, :], [("cvb", j), "VpC"], ["VpC"])
                memset("pool", VpN, 1.0, ["VpN"])
                for h in range(NH):
                    e0 = (h % 2) * 64
                    cp("pool", VpN[0:16, h, e0:e0 + 64], vnew[0:16, h * 64:(h + 1) * 64], ["vnew", "VpN"], ["VpN"])
                memset("dve", accS, 0.0, ["accS"])
                pcs, pkc = psb()
                for blk in range(NPB):
                    mm(pcs[:, blk * NH:(blk + 1) * NH], UT, lfc[:, blk, :], True, False, ["UT", ("lfc", j)], [pkc])
                    mm(pcs[:, blk * NH:(blk + 1) * NH], onesf, accS, False, True, ["onesf", "accS"], [pkc])
                    tt("dve", accS, accS, lfc[:, blk, :], ALU.add, ["accS", ("lfc", j)], ["accS"])
                cp("dve", Fc, pcs[:, 0:NPB * NH].rearrange("p (b h) -> p b h", h=NH), [pkc], ["Fc"])
                pr, pkr = psb()
                mm(pr[:, 0:NH], onesf, accS, True, True, ["onesf", "accS"], [pkr])
                mm(pr[0:16, NH:2 * NH], UT[0:16, 0:16], lfn[0:16, :], True, False, ["UT", "lfn"], [pkr])
                mm(pr[0:16, NH:2 * NH], onesf[:, 0:16], accS, False, True, ["onesf", "accS"], [pkr])
                ts("dve", Rcs, pr[:, 0:NH], cshift[:, 0:1], None, ALU.subtract, None, [pkr, "cshift"], ["Rcs"])
                cp("dve", Fn[0:16, :], pr[0:16, NH:2 * NH], [pkr], ["Fn"])
                tt("dve", biasC, Rcs.unsqueeze(1).to_broadcast([128, NPB, NH]), Fc, ALU.subtract, ["Rcs", "Fc"], ["biasC"])
                tt("dve", biasN[0:16, :], Rcs[0:16, :], Fn[0:16, :], ALU.subtract, ["Rcs", "Fn"], ["biasN"])
                pn = [0]
                units = [(h, blk) for h in range(NH) for blk in range(NPB + 1)]
                stb = {}

                def qk(u, j=j):
                    h, blk = u
                    cc = h // 2
                    r0 = (h % 2) * 64
                    qcol = qs[r0:r0 + 64, cc, 16 * j:16 * j + 16]
                    st, pks = psb()
                    stb[u] = (st, pks)
                    if blk < NPB:
                        mm(st[:, 0:16], kTc[r0:r0 + 64, cc, blk * 128:(blk + 1) * 128], qcol, True, True, ["kTc", "qs"], [pks])
                    else:
                        mm(st[0:16, 0:16], kn_[r0:r0 + 64, cc, 16 * j:16 * j + 16], qcol, True, True, ["kn_", "qs"], [pks])

                def rest(u, j=j):
                    h, blk = u
                    cc = h // 2
                    acc, pka = PS[h % 2], ("ps", h % 2)
                    st, pks = stb.pop(u)
                    pi = pn[0] % 4
                    pn[0] += 1
                    if blk < NPB:
                        act(pT[pi], st[:, 0:16], AF.Exp, [pks, "biasC"], [("pT", pi)], bias=biasC[:, blk, h:h + 1], scale=0.125)
                        mm(acc[:, 0:16], VpC[:, blk, h, :], pT[pi], blk == 0, False, ["VpC", ("pT", pi)], [pka])
                    else:
                        act(pT[pi][0:16, :], st[0:16, 0:16], AF.Exp, [pks, "biasN"], [("pT", pi)], bias=biasN[0:16, h:h + 1], scale=0.125)
                        tt("pool", pT[pi][0:16, :], pT[pi][0:16, :], masks[0:16, 0, 0:16], ALU.mult, [("pT", pi), "masks"], [("pT", pi)])
                        mm(acc[:, 0:16], VpN[0:16, h, :], pT[pi][0:16, :], False, True, ["VpN", ("pT", pi)], [pka])
                        attn_finish_group([(acc, pka, h, cc)], catT, 16, 16 * j, [rec[h % 2]], [rsh[h % 2]])

                LOOK = 3
                for idx in range(LOOK):
                    qk(units[idx])
                for idx, u in enumerate(units):
                    rest(u)
                    if idx + LOOK < len(units):
                        qk(units[idx + LOOK])
            reserved.clear()
            out_proj(catT, wout, xb, "xb", 1, 32, gate, tmp)
            dma("sp", x2d[TO:TO + 32, :], xb[0:32, 0, :], r=["xb"], w=["x2ds"])
            S.barrier()

        if "C" in phases:
            mix_sample()

        if "D" in phases:
            tilesD = [(x2d[m * T:(m + 1) * T, :], yp[m * T:(m + 1) * T, :], 4, 128, False) for m in range(NTO)]
            tilesD.append((x2d[TO:TO + 32, :], ys[:, :], 1, 32, True))
            ffn_phase(1, tilesD, True)

        S.emit()
    return nc


_CACHE = {}


def kernel(x_prompt, x_sample, c_prompt, c_sample, cache_k, cache_v, cache_logf, state_conv,
           w_ada, b_ada, g_ffn1, w_up1, w_down1, g_mix, w_in, b_f, g_q, g_k,
           conv_w, conv_b, conv_ln_g, conv_ln_b, w_out, g_ffn2, w_up2, w_down2, g_final, _phases="0ABCD"):
    f = lambda a: np.ascontiguousarray(np.asarray(a, dtype=np.float32))
    x_prompt = f(x_prompt); x_sample = f(x_sample)
    B, SEQ, _ = x_prompt.shape
    SB, SS, _ = x_sample.shape
    PAST = cache_k.shape[2]
    assert B * 2 == 8 and SB == 16 and SS == 16 and SEQ % (2 * T) == 0
    HALF = SEQ // 2
    NTO = HALF // T
    key = (NTO, PAST, _phases)
    if key not in _CACHE:
        _CACHE[key] = build(NTO, PAST, _phases)
    nc = _CACHE[key]
    shared = {
        "w_ada": f(w_ada)[0], "b_ada": f(b_ada), "g3": np.stack([f(g_ffn1)[0], f(g_mix)[0], f(g_ffn2)[0]]),
        "w_up1": f(w_up1)[0], "w_up2": f(w_up2)[0], "w_down1": f(w_down1)[0], "w_down2": f(w_down2)[0],
        "w_in": f(w_in)[0], "b_f": f(b_f), "g_q": f(g_q), "g_k": f(g_k), "conv_w": f(conv_w)[0],
        "cvec3": np.stack([f(conv_b)[0], f(conv_ln_g)[0], f(conv_ln_b)[0]]), "w_out": f(w_out)[0], "g_final": f(g_final),
    }
    cs_ = f(c_sample); cp_ = f(c_prompt)
    ck_ = f(cache_k)[0].reshape(SB, PAST, 512); cv_ = f(cache_v)[0].reshape(SB, PAST, 512)
    clf_ = f(cache_logf)[0]; sc_ = f(state_conv)[0]
    in_maps = []
    NTA = 2 * NTO
    own_t = {0: [t for t in range(NTA) if t % 4 in (0, 3)], 1: [t for t in range(NTA) if t % 4 in (1, 2)]}
    storage = {r: own_t[1 - r] + own_t[r] for r in (0, 1)}
    for core in range(8):
        b, r = core // 2, core % 2
        xt_ = x_prompt[b].reshape(NTA, T, D)
        m = dict(shared)
        m["xp"] = np.ascontiguousarray(xt_[storage[r]].reshape(NTA * T, D))
        glob = storage[r]
        lm = np.zeros((NTA, NTA), np.float32)
        for a_ in range(NTA):
            for c_ in range(NTA):
                lm[a_, c_] = 1.0 if glob[a_] < glob[c_] else 0.0
        m["lmat"] = lm
        m["xs"] = np.ascontiguousarray(x_sample[2 * core:2 * core + 2].reshape(32, D))
        m["cvec"] = np.ascontiguousarray(np.stack([cp_[b], cs_[2 * core], cs_[2 * core + 1]]))
        m["ck"] = np.ascontiguousarray(ck_[2 * core:2 * core + 2]); m["cv"] = np.ascontiguousarray(cv_[2 * core:2 * core + 2])
        m["clf"] = np.ascontiguousarray(clf_[2 * core:2 * core + 2]); m["sconv"] = np.ascontiguousarray(sc_[2 * core:2 * core + 2])
        fl = np.zeros((128, 3 * NTO), np.float32)
        for mm_ in range(NTO):
            o_, x_ = own_t[r][mm_], own_t[1 - r][mm_]
            fl[:, mm_] = NEGBIG if x_ > o_ else 0.0
            pred = o_ - 1
            if pred >= 0:
                if mm_ >= 1 and own_t[r][mm_ - 1] == pred:
                    fl[:, NTO + mm_] = 1.0
                else:
                    assert x_ == pred, (r, mm_, pred)
                    fl[:, 2 * NTO + mm_] = 1.0
        m["flagc"] = fl
        in_maps.append(m)
    res = run_bass_kernel_spmd(nc, in_maps, core_ids=list(range(8)))
    R = res.results
    y_prompt = np.zeros((B, SEQ, D), np.float32)
    k_prompt = np.zeros((1, B, SEQ, NH, DH), np.float32); v_prompt = np.zeros_like(k_prompt)
    logf_prompt = np.zeros((1, B, SEQ, NH), np.float32); conv_prompt = np.zeros((1, B, 30, 512), np.float32)
    y_sample = np.zeros((SB, SS, D), np.float32)
    k_sample = np.zeros((1, SB, SS, NH, DH), np.float32); v_sample = np.zeros_like(k_sample)
    logf_sample = np.zeros((1, SB, SS, NH), np.float32); conv_sample = np.zeros((1, SB, 30, 512), np.float32)
    for core in range(8):
        b, r = core // 2, core % 2
        o = R[core]
        yv = y_prompt[b].reshape(NTA, T, D)
        yv[own_t[r]] = o["yp"].reshape(NTO, T, D)
        if r == 1:
            inv = storage[r]
            k_prompt[0, b].reshape(NTA, T, NH, DH)[inv] = o["kp"].reshape(NTA, T, NH, DH)
            v_prompt[0, b].reshape(NTA, T, NH, DH)[inv] = o["vp"].reshape(NTA, T, NH, DH)
            logf_prompt[0, b].reshape(NTA, T, NH)[inv] = o["lfp"].reshape(NTA, T, NH)
        if own_t[r][-1] == NTA - 1:
            conv_prompt[0, b] = o["cvp"]
        sl = slice(2 * core, 2 * core + 2)
        y_sample[sl] = o["ys"].reshape(2, SS, D)
        k_sample[0, sl] = o["ks"].reshape(2, SS, NH, DH); v_sample[0, sl] = o["vs"].reshape(2, SS, NH, DH)
        logf_sample[0, sl] = o["lfs"].reshape(2, SS, NH); conv_sample[0, sl] = o["cvs"]
    return (y_prompt, y_sample, k_prompt, v_prompt, logf_prompt, conv_prompt, k_sample, v_sample, logf_sample, conv_sample)
```

```python
import contextlib
import numpy as np
import ml_dtypes
import concourse.bass as bass
import concourse.mybir as mybir
from concourse.bass_utils import run_bass_kernel_spmd

F32 = mybir.dt.float32
BF16 = mybir.dt.bfloat16
AF = mybir.ActivationFunctionType
ALU = mybir.AluOpType
AX = mybir.AxisListType

COMPUTE = ("pe", "act", "dve", "pool")
NDMA_SEMS = 32
NSW_SEMS = 8

D = 1024
DFF = 2816
NF = DFF // 128
NH = 8
DH = 64
DIN = 2568
T = 512
CW = 31
EPS = 1e-6
NEGBIG = -30000.0
WUP_GROUPS = ((0, 4), (4, 10), (10, 16), (16, 22))


class Sched:
    def __init__(self, nc, same_engine_sync=True):
        self.nc = nc
        self.same_engine_sync = same_engine_sync
        self.queues = {e: [] for e in ("pe", "act", "dve", "pool", "sp")}
        self.count = {e: 0 for e in COMPUTE}
        self.dma_n = 0
        self.dma_sw = 0
        self.dma_slot_last = [0] * NDMA_SEMS
        self.last_write = {}
        self.reads = {}
        self.waited = {e: {} for e in self.queues}
        self.out_dma = []
        self.needed = {e: set() for e in COMPUTE}
        self.pending = {e: {} for e in self.queues}

    def _deps(self, eng, reads, writes):
        need = dict(self.pending[eng])
        self.pending[eng] = {}

        def add(tok):
            if tok is None:
                return
            k, v = tok
            if need.get(k, 0) < v:
                need[k] = v

        for r in reads:
            add(self.last_write.get(r))
        for w in writes:
            add(self.last_write.get(w))
            for t in self.reads.get(w, ()):
                add(t)
        out = {}
        for k, v in need.items():
            if k == eng and (eng == "pe" or not self.same_engine_sync):
                continue
            if self.waited[eng].get(k, 0) >= v:
                continue
            self.waited[eng][k] = v
            out[k] = v
            if not isinstance(k, tuple):
                self.needed[k].add(v)
        return out

    def _commit(self, tok, reads, writes):
        for r in reads:
            self.reads.setdefault(r, []).append(tok)
        for w in writes:
            self.last_write[w] = tok
            self.reads[w] = []

    def op(self, eng, fn, reads=(), writes=()):
        waits = self._deps(eng, reads, writes)
        self.count[eng] += 1
        tok = (eng, self.count[eng])
        self.queues[eng].append((waits, fn, tok))
        self._commit(tok, reads, writes)
        return tok

    def dma(self, q, fn, reads=(), writes=(), is_output=False):
        if q == "pool":
            slot = NDMA_SEMS - NSW_SEMS + self.dma_sw % NSW_SEMS
            self.dma_sw += 1
        else:
            slot = self.dma_n % (NDMA_SEMS - NSW_SEMS)
            self.dma_n += 1
        key = ("dma", slot)
        waits = self._deps(q, reads, writes)
        prev = self.dma_slot_last[slot]
        if prev and self.waited[q].get(key, 0) < prev:
            waits[key] = max(waits.get(key, 0), prev)
            self.waited[q][key] = prev
        val = prev + 16
        self.dma_slot_last[slot] = val
        tok = (key, val)
        self.queues[q].append((waits, fn, tok))
        self._commit(tok, reads, writes)
        if is_output:
            self.out_dma.append(tok)
        return tok

    def barrier(self):
        allw = {}
        for e in COMPUTE:
            if self.count[e]:
                allw[e] = self.count[e]
        for i in range(NDMA_SEMS):
            if self.dma_slot_last[i]:
                allw[("dma", i)] = self.dma_slot_last[i]
        for e in self.queues:
            p = self.pending[e]
            for k, v in allw.items():
                if p.get(k, 0) < v:
                    p[k] = v
        self.last_write = {}
        self.reads = {}

    def emit(self):
        nc = self.nc
        with contextlib.ExitStack() as es:
            sems = {}
            for e in COMPUTE:
                sems[e] = es.enter_context(nc.semaphore("s_" + e))
            for i in range(NDMA_SEMS):
                sems[("dma", i)] = es.enter_context(nc.semaphore("s_dma%d" % i))
            final_waits = {}
            for i in range(NDMA_SEMS):
                if self.dma_slot_last[i]:
                    final_waits[("dma", i)] = self.dma_slot_last[i]
            for e in COMPUTE:
                if self.count[e]:
                    self.needed[e].add(self.count[e])
                    final_waits[e] = self.count[e]
            rank = {}
            for e in COMPUTE:
                for i, v in enumerate(sorted(self.needed[e])):
                    rank[(e, v)] = i + 1
            block = es.enter_context(nc.Block())
            queues = self.queues

            def semval(k, v):
                return v if isinstance(k, tuple) else rank[(k, v)]

            def run(engobj, q, extra_final=False):
                for waits, fn, tok in queues[q]:
                    for k, v in waits.items():
                        engobj.wait_ge(sems[k], semval(k, v))
                    ins = fn(engobj)
                    k, v = tok
                    if isinstance(k, tuple):
                        ins.then_inc(sems[k], 16)
                    elif (k, v) in rank:
                        ins.then_inc(sems[k], 1)
                if extra_final:
                    for k, v in final_waits.items():
                        engobj.wait_ge(sems[k], semval(k, v))

            @block.sync
            def _(e):
                run(e, "sp", extra_final=True)

            @block.tensor
            def _(e):
                run(e, "pe")

            @block.scalar
            def _(e):
                run(e, "act")

            @block.vector
            def _(e):
                run(e, "dve")

            @block.gpsimd
            def _(e):
                run(e, "pool")


class Arena:
    def __init__(self, ap, nwords):
        self.ap = ap
        self.n = nwords
        self.off = 0
        self.mark = 0

    def set_mark(self):
        self.mark = self.off

    def reset(self):
        self.off = self.mark

    def f32(self, n):
        assert self.off + n <= self.n, ("arena overflow", self.off, n, self.n)
        a = self.ap[:, self.off:self.off + n]
        self.off += n
        return a

    def bf16(self, n):
        w = (n + 1) // 2
        assert self.off + w <= self.n, ("arena overflow", self.off, w, self.n)
        a = self.ap[:, self.off:self.off + w].bitcast(BF16)
        self.off += w
        return a[:, 0:n]


def build(NTO, PAST, phases="0ABCD"):
    NTA = 2 * NTO
    TP = NTA * T
    TO = NTO * T
    NPB = PAST // 128
    nc = bass.Bass("TRN2", target_bir_lowering=False)

    def din(name, shape, dt=F32):
        return nc.dram_tensor(name, list(shape), dt, kind="ExternalInput").ap()

    def dout(name, shape, dt=F32):
        return nc.dram_tensor(name, list(shape), dt, kind="ExternalOutput").ap()

    def dscr(name, shape, dt=F32):
        return nc.dram_tensor(name, list(shape), dt).ap()

    xp = din("xp", [TP, D]); xs = din("xs", [32, D]); cvec = din("cvec", [3, D])
    ck = din("ck", [2, PAST, 512]); cv = din("cv", [2, PAST, 512]); clf = din("clf", [2, PAST, NH])
    sconv = din("sconv", [2, 30, 512]); flagc = din("flagc", [128, 3 * NTO]); lmat = din("lmat", [NTA, NTA])
    w_ada = din("w_ada", [D, 9 * D]); b_ada = din("b_ada", [1, 9 * D])
    g3 = din("g3", [3, D])
    w_up = [din("w_up1", [D, 2 * DFF]), din("w_up2", [D, 2 * DFF])]
    w_dn = [din("w_down1", [DFF, D]), din("w_down2", [DFF, D])]
    w_in = din("w_in", [D, DIN]); b_f = din("b_f", [1, NH]); g_q = din("g_q", [1, DH]); g_k = din("g_k", [1, DH])
    conv_w = din("conv_w", [CW, 512]); cvec3 = din("cvec3", [3, 512])
    w_out = din("w_out", [D, D]); g_final = din("g_final", [1, D])

    yp = dout("yp", [TO, D]); ys = dout("ys", [32, D])
    kp = dout("kp", [TP, 512]); vp = dout("vp", [TP, 512]); lfp = dout("lfp", [TP, NH]); cvp = dout("cvp", [30, 512])
    ks = dout("ks", [32, 512]); vs = dout("vs", [32, 512]); lfs = dout("lfs", [32, NH]); cvs = dout("cvs", [2, 30, 512])

    TPX = TP + 32
    x1d = dscr("x1d", [TPX, D]); x2d = dscr("x2d", [TO + 32, D])
    kTd = dscr("kTd", [512, TPX], BF16); uTd = dscr("uTd", [512, TPX], BF16); qTd = dscr("qTd", [512, TO + 32], BF16)
    vSd = dscr("vSd", [TPX, 512], BF16); cumFd = dscr("cumFd", [TP, NH]); modrows = dscr("modrows", [3, 9 * D])

    S = Sched(nc)
    es = contextlib.ExitStack()
    with es:
        es.enter_context(nc.allow_low_precision("bf16 matmul operands, fp32 accumulate"))
        es.enter_context(nc.allow_non_contiguous_dma("small strided loads"))
        NW = 51200
        arena_t = es.enter_context(nc.sbuf_tensor("arena", [128, NW], F32))
        AR = Arena(arena_t, NW)
        PS = [es.enter_context(nc.psum_tensor("ps%d" % i, [128, 512], F32)) for i in range(8)]
        ps_ctr = [0]

        reserved = set()

        def psb():
            while True:
                i = ps_ctr[0] % 8
                ps_ctr[0] += 1
                if i not in reserved:
                    return PS[i], ("ps", i)

        def dma(q, out, in_, r=(), w=(), is_output=False):
            S.dma(q, lambda e: e.dma_start(out=out, in_=in_), reads=r, writes=w, is_output=is_output)

        def mm(out, lhsT, rhs, start, stop, r, w):
            S.op("pe", lambda e: e.matmul(out, lhsT=lhsT, rhs=rhs, start=start, stop=stop), reads=r, writes=w)

        def act(out, in_, func, r, w, bias=None, scale=None, accum_out=None):
            kw = {}
            if bias is not None:
                kw["bias"] = bias
            if scale is not None:
                kw["scale"] = scale
            if accum_out is not None:
                kw["accum_out"] = accum_out
            S.op("act", lambda e: e.activation(out=out, in_=in_, func=func, **kw), reads=r, writes=w)

        def tt(eng, out, in0, in1, op, r, w):
            S.op(eng, lambda e: e.tensor_tensor(out=out, in0=in0, in1=in1, op=op), reads=r, writes=w)

        def ts(eng, out, in0, s1, s2, op0, op1, r, w):
            if op1 is None:
                S.op(eng, lambda e: e.tensor_scalar(out=out, in0=in0, scalar1=s1, scalar2=s2, op0=op0), reads=r, writes=w)
            else:
                S.op(eng, lambda e: e.tensor_scalar(out=out, in0=in0, scalar1=s1, scalar2=s2, op0=op0, op1=op1), reads=r, writes=w)

        def cp(eng, out, in_, r, w):
            if eng == "act":
                S.op("act", lambda e: e.copy(out=out, in_=in_), reads=r, writes=w)
            else:
                S.op(eng, lambda e: e.tensor_copy(out=out, in_=in_), reads=r, writes=w)

        def memset(eng, ap, val, w):
            S.op(eng, lambda e: e.memset(ap, val), writes=w)

        def recip(out, in_, r, w):
            S.op("dve", lambda e: e.reciprocal(out=out, in_=in_), reads=r, writes=w)

        def rsqrt_small(out, in_, scale, r, w):
            act(out, in_, AF.Sqrt, r, w, bias=epsb[0:out.shape[0], 0:1], scale=scale)
            recip(out, out, w, w)

        ident = AR.bf16(128)
        identf = AR.f32(128)
        UT = AR.f32(128)
        onesf = AR.f32(128)
        masks = AR.bf16(4 * 512).rearrange("p (o t) -> p o t", o=4)
        modcol = AR.f32(72 * 3).rearrange("p (j s) -> p j s", s=3)
        gcol = AR.f32(8 * 3).rearrange("p (k s) -> p k s", s=3)
        gsc = [AR.f32(8 * 3).rearrange("p (k s) -> p k s", s=3) for _ in range(3)]
        ccol = AR.f32(4 * 3).rearrange("p (k s) -> p k s", s=3)
        wcol = AR.f32(4 * CW).rearrange("p (k j) -> p k j", k=4)
        flag_t = AR.f32(3 * NTO)
        bf_t = AR.f32(NH)
        gq_t = AR.f32(512)
        gk_t = AR.f32(512)
        cshift = AR.f32(1)
        epsb = AR.f32(1)
        accF = AR.f32(NH)
        AR.set_mark()

        memset("pool", identf, 0.0, ["identf"])
        S.op("pool", lambda e: e.affine_select(out=identf, in_=identf, pattern=[[-1, 128]], compare_op=ALU.not_equal,
                                               fill=1.0, base=0, channel_multiplier=1), reads=["identf"], writes=["identf"])
        cp("dve", ident, identf, ["identf"], ["ident"])
        memset("pool", UT, 1.0, ["UT"])
        S.op("pool", lambda e: e.affine_select(out=UT, in_=UT, pattern=[[1, 128]], compare_op=ALU.is_ge,
                                               fill=0.0, base=0, channel_multiplier=-1), reads=["UT"], writes=["UT"])
        memset("dve", onesf, 1.0, ["onesf"])
        memset("dve", epsb, EPS, ["epsb"])
        memset("dve", accF, 0.0, ["accF"])
        mtmp = AR.f32(512)
        for o in range(4):
            memset("pool", mtmp, 1.0, ["mtmp"])
            S.op("pool", lambda e, o=o: e.affine_select(out=mtmp, in_=mtmp, pattern=[[1, 512]], compare_op=ALU.is_ge,
                                                        fill=0.0, base=-128 * o, channel_multiplier=-1),
                 reads=["mtmp"], writes=["mtmp"])
            cp("pool", masks[:, o, :], mtmp, ["mtmp"], ["masks"])
        dma("sp", flag_t, flagc[:, :], w=["flag"])
        dma("sp", bf_t, b_f[0:1, :].partition_broadcast(128).rearrange("p a n -> p (a n)"), w=["bf_t"])
        gq64 = AR.f32(64)
        gk64 = AR.f32(64)
        dma("sp", gq64, g_q[0:1, :].partition_broadcast(128).rearrange("p a n -> p (a n)"), w=["gq64"])
        dma("sp", gk64, g_k[0:1, :].partition_broadcast(128).rearrange("p a n -> p (a n)"), w=["gk64"])
        cp("dve", gq_t.rearrange("p (h d) -> p h d", h=NH), gq64.unsqueeze(1).to_broadcast([128, NH, DH]), ["gq64"], ["gq_t"])
        cp("dve", gk_t.rearrange("p (h d) -> p h d", h=NH), gk64.unsqueeze(1).to_broadcast([128, NH, DH]), ["gk64"], ["gk_t"])
        gcolqk = AR.f32(2)
        AR.set_mark()
        wup_pre = AR.bf16(8 * 2 * DFF).rearrange("p (k n) -> p k n", k=8)
        if "A" in phases:
            wup_v0 = w_up[0].rearrange("(k p) n -> p k n", p=128)
            for gi, (f0, f1) in enumerate(WUP_GROUPS):
                for half in range(2):
                    c0, c1 = half * DFF + f0 * 128, half * DFF + f1 * 128
                    dma("pool", wup_pre[:, :, c0:c1], wup_v0[:, :, c0:c1], w=[("wup", gi)])
        grow2 = AR.f32(128)
        for j_ in range(2):
            dma("sp", grow2[0:1, 64 * j_:64 * j_ + 64], g_q[0:1, :], w=["grow2q"])
            dma("sp", grow2[32:33, 64 * j_:64 * j_ + 64], g_k[0:1, :], w=["grow2k"])
        pgq, pkgq = psb()
        mm(pgq[:, 0:1], grow2[0:1, :], identf[0:1, 0:1], True, True, ["grow2q", "identf"], [pkgq])
        mm(pgq[:, 1:2], grow2[32:33, :], onesf[32:33, 0:1], True, True, ["grow2k", "onesf"], [pkgq])
        cp("dve", gcolqk, pgq[:, 0:2], [pkgq], ["gcolqk"])
        mq = AR.f32(2)
        S.op("dve", lambda e: e.tensor_reduce(out=mq[:, 0:1], in_=gq64, axis=AX.X, op=ALU.max, apply_absolute_value=True),
             reads=["gq64"], writes=["mq"])
        S.op("dve", lambda e: e.tensor_reduce(out=mq[:, 1:2], in_=gk64, axis=AX.X, op=ALU.max, apply_absolute_value=True),
             reads=["gk64"], writes=["mq"])
        tt("dve", cshift, mq[:, 0:1], mq[:, 1:2], ALU.mult, ["mq"], ["cshift"])
        ts("dve", cshift, cshift, 8.0, None, ALU.mult, None, ["cshift"], ["cshift"])

        crow = AR.f32(D)
        cT = AR.bf16(8 * 3).rearrange("p (k s) -> p k s", s=3)
        mrow = AR.f32(9 * D)
        grow = AR.f32(D)
        c3row = AR.f32(512)
        cwrow = AR.f32(512)
        dma("sp", crow[0:3, :], cvec[:, :], w=["crow"])
        dma("sp", grow[0:3, :], g3[:, :], w=["grow"])
        dma("sp", c3row[0:3, :], cvec3[:, :], w=["c3row"])
        dma("sp", cwrow[0:CW, :], conv_w[:, :], w=["cwrow"])
        pst, pk = psb()
        for kc in range(8):
            mm(pst[:, kc * 3:kc * 3 + 3], crow[0:3, kc * 128:(kc + 1) * 128], identf[0:3, 0:3], True, True, ["crow", "identf"], [pk])
        act(cT, pst[:, 0:24].rearrange("p (k s) -> p k s", s=3), AF.Silu, [pk], ["cT"])
        pst, pk = psb()
        for kc in range(8):
            mm(pst[:, kc * 3:kc * 3 + 3], grow[0:3, kc * 128:(kc + 1) * 128], identf[0:3, 0:3], True, True, ["grow", "identf"], [pk])
        cp("dve", gcol, pst[:, 0:24].rearrange("p (k s) -> p k s", s=3), [pk], ["gcol"])
        pst, pk = psb()
        for kc in range(4):
            mm(pst[:, kc * 3:kc * 3 + 3], c3row[0:3, kc * 128:(kc + 1) * 128], identf[0:3, 0:3], True, True, ["c3row", "identf"], [pk])
        cp("dve", ccol, pst[:, 0:12].rearrange("p (k s) -> p k s", s=3), [pk], ["ccol"])
        pst, pk = psb()
        for kc in range(4):
            mm(pst[:, kc * CW:(kc + 1) * CW], cwrow[0:CW, kc * 128:(kc + 1) * 128], identf[0:CW, 0:CW], True, True, ["cwrow", "identf"], [pk])
        cp("dve", wcol, pst[:, 0:4 * CW].rearrange("p (k j) -> p k j", k=4), [pk], ["wcol"])

        wa_ring = [AR.bf16(8 * 512).rearrange("p (k n) -> p k n", k=8) for _ in range(3)]
        bch = [AR.f32(512) for _ in range(2)]
        w_ada_v = w_ada.rearrange("(k p) n -> p k n", p=128)
        for n in range(18):
            wr = wa_ring[n % 3]
            dma("pool", wr, w_ada_v[:, :, n * 512:(n + 1) * 512], w=[("wa", n % 3)])
            bb = bch[n % 2]
            dma("sp", bb[0:3, :], b_ada[0:1, n * 512:(n + 1) * 512].partition_broadcast(3).rearrange("p a n -> p (a n)"), w=[("bch", n % 2)])
            pst, pk = psb()
            for kc in range(8):
                mm(pst[0:3, :], cT[:, kc, :], wr[:, kc, :], kc == 0, kc == 7, ["cT", ("wa", n % 3)], [pk])
            tt("dve", mrow[0:3, n * 512:(n + 1) * 512], pst[0:3, :], bb[0:3, :], ALU.add, [pk, ("bch", n % 2)], ["mrow"])
        dma("sp", modrows[:, :], mrow[0:3, :], r=["mrow"], w=["modrows"])
        for jj in range(0, 72, 24):
            pst, pk = psb()
            for j in range(jj, jj + 24):
                mm(pst[:, (j - jj) * 3:(j - jj) * 3 + 3], mrow[0:3, j * 128:(j + 1) * 128], identf[0:3, 0:3], True, True, ["mrow", "identf"], [pk])
            cp("dve", modcol[:, jj:jj + 24, :], pst[:, 0:72].rearrange("p (j s) -> p j s", s=3), [pk], ["modcol"])
        for n in range(3):
            ts("dve", gsc[n], modcol[:, (3 * n + 1) * 8:(3 * n + 2) * 8, :], 1.0, None, ALU.add, None, ["modcol"], [("gsc", n)])
            tt("dve", gsc[n], gsc[n], gcol[:, :, n:n + 1].to_broadcast([128, 8, 3]), ALU.mult, [("gsc", n), "gcol"], [("gsc", n)])
        S.barrier()
        AR.reset()

        def shcol(n, kc, s):
            return modcol[:, 3 * n * 8 + kc, s:s + 1]

        def load_gate(gt, n, sample):
            c0 = (3 * n + 2) * D
            if not sample:
                dma("sp", gt, modrows[0:1, c0:c0 + D].partition_broadcast(128).rearrange("p a n -> p (a n)"), r=["modrows"], w=["gate"])
            else:
                for j in range(2):
                    dma("sp", gt[16 * j:16 * j + 16, :], modrows[1 + j:2 + j, c0:c0 + D].partition_broadcast(16).rearrange("p a n -> p (a n)"),
                        r=["modrows"], w=["gate"])

        def normA(xt, xk, nb, np_, scr):
            junk, ss, xn = scr
            for b in range(nb):
                act(junk[0:np_, :], xt[0:np_, b, :], AF.Square, [xk], ["junk", "ss"], accum_out=ss[0:np_, b:b + 1])
            rsqrt_small(ss[0:np_, 0:nb], ss[0:np_, 0:nb], 1.0 / D, ["ss"], ["ss"])
            for b in range(nb):
                act(xn[0:np_, b, :], xt[0:np_, b, :], AF.Identity, [xk, "ss"], [("xn", b)], scale=ss[0:np_, b:b + 1])

        def transp(nb, np_, n, sample, hT, scr, hkey="hT"):
            junk, ss, xn = scr
            ntok = (nb - 1) * 128 + np_
            segs = [(0, ntok, 0)] if not sample else [(0, 16, 1), (16, 32, 2)]
            for kc in range(8):
                pst, pk = psb()
                for b in range(nb):
                    mm(pst[:, b * 128:b * 128 + np_], xn[0:np_, b, kc * 128:(kc + 1) * 128], ident[0:np_, 0:np_], True, True,
                       [("xn", b), "ident"], [pk])
                for (c0, c1, s) in segs:
                    if kc % 2 == 0:
                        act(hT[:, kc, c0:c1], pst[:, c0:c1], AF.Identity, [pk], [hkey], bias=shcol(n, kc, s), scale=gsc[n][:, kc, s:s + 1])
                    else:
                        ts("dve", hT[:, kc, c0:c1], pst[:, c0:c1], gsc[n][:, kc, s:s + 1], shcol(n, kc, s), ALU.mult, ALU.add, [pk], [hkey])
            return ntok

        def ffn_phase(which, tiles, final_norm):
            n = 0 if which == 0 else 2
            AR.reset()
            wup = AR.bf16(8 * 2 * DFF).rearrange("p (k n) -> p k n", k=8)
            XB = [AR.f32(4 * D).rearrange("p (b d) -> p b d", b=4) for _ in range(2)]
            hT = AR.bf16(8 * T).rearrange("p (k t) -> p k t", k=8)
            actT = AR.bf16(NF * T).rearrange("p (f t) -> p f t", f=NF)
            gate = AR.f32(D)
            wdr = [AR.bf16(D) for _ in range(4)]
            junk = AR.bf16(D)
            ss = AR.f32(4)
            xn = AR.bf16(4 * D).rearrange("p (b d) -> p b d", b=4)
            sg = [AR.f32(T) for _ in range(2)]
            tmp = [AR.f32(512) for _ in range(2)]
            gfin = AR.f32(D) if final_norm else None
            wup_v = w_up[which].rearrange("(k p) n -> p k n", p=128)
            wdv = w_dn[which]
            if which != 0:
                for gi, (f0, f1) in enumerate(WUP_GROUPS):
                    for half in range(2):
                        c0, c1 = half * DFF + f0 * 128, half * DFF + f1 * 128
                        dma("pool", wup[:, :, c0:c1], wup_v[:, :, c0:c1], w=[("wup", gi)])
            if final_norm:
                dma("sp", gfin, g_final[0:1, :].partition_broadcast(128).rearrange("p a n -> p (a n)"), w=["gfin"])

            def load_x(i):
                src, _, nb, np_, sample = tiles[i]
                xb = XB[i % 2]
                if sample:
                    dma("sp", xb[0:np_, 0, :], src, w=[("xb", i % 2)])
                else:
                    dma("sp", xb[:, 0:nb, :], src.rearrange("(b p) d -> p b d", p=128), w=[("xb", i % 2)])

            ss2 = AR.f32(4)
            scr = (junk, ss, xn)
            load_x(0)
            normA(XB[0], ("xb", 0), tiles[0][2], tiles[0][3], scr)
            transp(tiles[0][2], tiles[0][3], n, tiles[0][4], hT, scr)
            cur_gate = None
            for i, (src, dst, nb, np_, sample) in enumerate(tiles):
                xb = XB[i % 2]
                xk = ("xb", i % 2)
                ntok = (nb - 1) * 128 + np_
                if i + 1 < len(tiles):
                    load_x(i + 1)
                if cur_gate != sample:
                    load_gate(gate, n, sample)
                    ts("dve", gate, gate, 0.5, None, ALU.mult, None, ["gate"], ["gate"])
                    cur_gate = sample
                for f in range(4):
                    dma("pool", wdr[f], wdv[f * 128:(f + 1) * 128, :], w=[("wd", f)])
                for f in range(NF):
                    pa, pka = psb()
                    pb, pkb = psb()
                    wk = ("wup", [gi for gi, (f0, f1) in enumerate(WUP_GROUPS) if f0 <= f < f1][0])
                    for kc in range(8):
                        mm(pa[:, 0:ntok], wup[:, kc, f * 128:(f + 1) * 128], hT[:, kc, 0:ntok], kc == 0, kc == 7, ["hT", wk], [pka])
                    for kc in range(8):
                        mm(pb[:, 0:ntok], wup[:, kc, DFF + f * 128:DFF + (f + 1) * 128], hT[:, kc, 0:ntok], kc == 0, kc == 7, ["hT", wk], [pkb])
                    sgb = sg[f % 2]
                    act(sgb[:, 0:ntok], pa[:, 0:ntok], AF.Silu, [pka], [("sg", f % 2)])
                    tt("dve", actT[:, f, 0:ntok], sgb[:, 0:ntok], pb[:, 0:ntok], ALU.mult, [("sg", f % 2), pkb], [("actT", f)])
                    if f == 10 and i + 1 < len(tiles):
                        normA(XB[(i + 1) % 2], ("xb", (i + 1) % 2), tiles[i + 1][2], tiles[i + 1][3], scr)
                if i + 1 < len(tiles):
                    transp(tiles[i + 1][2], tiles[i + 1][3], n, tiles[i + 1][4], hT, scr)
                accs = {}
                for b in range(nb):
                    for h in range(2):
                        accs[(b, h)] = psb()
                for f in range(NF):
                    for b in range(nb):
                        for h in range(2):
                            pacc, pkk = accs[(b, h)]
                            mm(pacc[0:np_, :], actT[:, f, b * 128:b * 128 + np_], wdr[f % 4][:, h * 512:(h + 1) * 512], f == 0, f == NF - 1,
                               [("actT", f), ("wd", f % 4)], [pkk])
                    if f + 4 < NF:
                        dma("pool", wdr[f % 4], wdv[(f + 4) * 128:(f + 5) * 128, :], w=[("wd", f % 4)])
                k = 0
                for b in range(nb):
                    for h in range(2):
                        pacc, pkk = accs[(b, h)]
                        tb = tmp[k % 2]
                        tt("dve", tb[0:np_, :], pacc[0:np_, :], gate[0:np_, h * 512:(h + 1) * 512], ALU.mult, [pkk, "gate"], [("tmp", k % 2)])
                        tt("pool", xb[0:np_, b, h * 512:(h + 1) * 512], tb[0:np_, :], xb[0:np_, b, h * 512:(h + 1) * 512], ALU.add,
                           [("tmp", k % 2), xk], [xk])
                        k += 1
                if final_norm:
                    for b in range(nb):
                        act(junk[0:np_, :], xb[0:np_, b, :], AF.Square, [xk], ["junk", "ss2"], accum_out=ss2[0:np_, b:b + 1])
                    rsqrt_small(ss2[0:np_, 0:nb], ss2[0:np_, 0:nb], 1.0 / D, ["ss2"], ["ss2"])
                    for b in range(nb):
                        S.op("dve", lambda e, b=b, xb=xb, np_=np_: e.scalar_tensor_tensor(
                            out=xb[0:np_, b, :], in0=xb[0:np_, b, :], scalar=ss2[0:np_, b:b + 1], in1=gfin[0:np_, :],
                            op0=ALU.mult, op1=ALU.mult), reads=[xk, "ss2", "gfin"], writes=[xk])
                if sample:
                    dma("sp", dst, xb[0:np_, 0, :], r=[xk], w=[("dst", i)], is_output=final_norm)
                else:
                    dma("sp", dst.rearrange("(b p) d -> p b d", p=128), xb[:, 0:nb, :], r=[xk], w=[("dst", i)], is_output=final_norm)
            S.barrier()

        if "A" in phases:
            tilesA = [(xp[t * T:(t + 1) * T, :], x1d[t * T:(t + 1) * T, :], 4, 128, False) for t in range(NTA)]
            tilesA.append((xs[:, :], x1d[TP:TP + 32, :], 1, 32, True))
            ffn_phase(0, tilesA, False)

        def qknorm(ps_t, pk, np_, gtile, sq, sqi, ssh, kn, kni, kb, idx, want_f32):
            act(sq[0:np_, :], ps_t[0:np_, :], AF.Square, [pk], [("sq", sqi)])
            S.op("dve", lambda e: e.tensor_reduce(out=ssh[0:np_, :], in_=sq[0:np_, :].rearrange("p (h d) -> p h d", h=NH),
                                                  axis=AX.X, op=ALU.add), reads=[("sq", sqi)], writes=[("ssh", idx)])
            rsqrt_small(ssh[0:np_, :], ssh[0:np_, :], 1.0 / DH, [("ssh", idx)], [("ssh", idx)])
            tt("dve", kb[0:np_, :].rearrange("p (h d) -> p h d", h=NH), ps_t[0:np_, :].rearrange("p (h d) -> p h d", h=NH),
               ssh[0:np_, :].unsqueeze(2).to_broadcast([np_, NH, DH]), ALU.mult, [pk, ("ssh", idx)], [("kb", idx)])
            if want_f32:
                tt("dve", kn[0:np_, :].rearrange("p (h d) -> p h d", h=NH), ps_t[0:np_, :].rearrange("p (h d) -> p h d", h=NH),
                   ssh[0:np_, :].unsqueeze(2).to_broadcast([np_, NH, DH]), ALU.mult, [pk, ("ssh", idx)], [("kn", kni)])
                tt("pool", kn[0:np_, :], kn[0:np_, :], gtile[0:np_, :], ALU.mult, [("kn", kni), "gq_t", "gk_t"], [("kn", kni)])

        lfsd = dscr("lfsd", [32, NH])

        def mix_setup(emit=True):
            wout = AR.bf16(8 * D).rearrange("p (k n) -> p k n", k=8)
            diag = AR.bf16(4 * CW * 128).rearrange("p (c j m) -> p c j m", c=4, j=CW)
            if emit:
                mix_emit_wout(wout)
                for c in range(4):
                    mix_emit_diag(diag, c)
            return wout, diag

        def mix_emit_wout(wout):
            dma("pool", wout, w_out.rearrange("(k p) n -> p k n", p=128), w=["wout"])

        def mix_emit_diag(diag, c):
            for j in range(CW):
                if j % 2 == 0:
                    ts("dve", diag[:, c, j, :], identf, wcol[:, c, j:j + 1], None, ALU.mult, None, ["identf", "wcol"], [("diag", c)])
                else:
                    act(diag[:, c, j, :], identf, AF.Identity, ["identf", "wcol"], [("diag", c)], scale=wcol[:, c, j:j + 1])

        def proj_phase():
            AR.reset()
            stage_c = "C" in phases
            if stage_c:
                wout_c, diag_c = mix_setup(emit=False)
            win = AR.bf16(8 * DIN).rearrange("p (k n) -> p k n", k=8)
            XB = [AR.f32(4 * D).rearrange("p (b d) -> p b d", b=4) for _ in range(2)]
            hTs = [AR.bf16(8 * T).rearrange("p (k t) -> p k t", k=8) for _ in range(2)]
            junk = AR.bf16(D)
            ss = AR.f32(4)
            xn = AR.bf16(4 * D).rearrange("p (b d) -> p b d", b=4)
            sq = [AR.f32(512) for _ in range(2)]
            ssh = [AR.f32(NH) for _ in range(4)]
            kn = [AR.f32(512) for _ in range(2)]
            kb = [AR.bf16(512) for _ in range(4)]
            vf = [AR.f32(512) for _ in range(2)]
            vb = [AR.bf16(512) for _ in range(2)]
            kT_sb = AR.bf16(4 * T).rearrange("p (c t) -> p c t", c=4)
            qT_sb = AR.bf16(4 * T).rearrange("p (c t) -> p c t", c=4)
            uT_sb = AR.bf16(4 * T).rearrange("p (c t) -> p c t", c=4)
            sgm = [AR.f32(T) for _ in range(2)]
            lx = AR.f32(4 * NH)
            la = AR.f32(4 * NH)
            lm = AR.f32(4 * NH)
            lf = AR.f32(4 * NH).rearrange("p (b h) -> p b h", b=4)
            cumF_sb = AR.f32(4 * NH)
            utk = AR.f32(512)
            win_v = w_in.rearrange("(k p) n -> p k n", p=128)
            for kc in range(8):
                dma("pool", win[:, kc, :], win_v[:, kc, :], w=[("win", kc)])
            wink = [("win", kc) for kc in range(8)]
            kTd_v = kTd.rearrange("(c p) t -> p c t", p=128)
            qTd_v = qTd.rearrange("(c p) t -> p c t", p=128)
            uTd_v = uTd.rearrange("(c p) t -> p c t", p=128)
            PSF = PS[7]
            pkf = ("ps", 7)
            reserved.add(7)
            psb7 = psb

            tiles = [(t * T, 4, 128, False, (t - NTO) if t >= NTO else None) for t in range(NTA)]
            tiles.append((TP, 1, 32, True, NTO))

            def load_x(i):
                row0, nb, np_, sample, _ = tiles[i]
                xb = XB[i % 2]
                if sample:
                    dma("sp", xb[0:np_, 0, :], x1d[row0:row0 + np_, :], w=[("xb", i % 2)])
                else:
                    dma("sp", xb[:, 0:nb, :], x1d[row0:row0 + nb * 128, :].rearrange("(b p) d -> p b d", p=128), w=[("xb", i % 2)])

            scr = (junk, ss, xn)
            load_x(0)
            normA(XB[0], ("xb", 0), tiles[0][1], tiles[0][2], scr)
            transp(tiles[0][1], tiles[0][2], 1, tiles[0][3], hTs[0], scr, ("hT", 0))
            cnt = [0]
            pending_cs = []
            for i, (row0, nb, np_, sample, own) in enumerate(tiles):
                xb = XB[i % 2]
                xk = ("xb", i % 2)
                hT = hTs[i % 2]
                hk = ("hT", i % 2)
                ntok = (nb - 1) * 128 + np_
                if i + 1 < len(tiles):
                    load_x(i + 1)
                fo = 128 * (i % 2)
                pkf = ("psf", i % 2)
                for b in range(nb):
                    for kc in range(8):
                        mm(PSF[0:np_, fo + b * NH:fo + (b + 1) * NH], hT[:, kc, b * 128:b * 128 + np_], win[:, kc, 1536:1536 + NH], kc == 0, kc == 7,
                           [hk, ("win", kc)], [pkf])
                kout, vout = (ks, vs) if sample else (kp, vp)
                pending = []

                def flush():
                    for (part, b_, ix_) in pending:
                        ptt, pkt = psb7()
                        for c in range(4):
                            mm(ptt[:, c * 128:c * 128 + np_], kb[ix_][0:np_, c * 128:(c + 1) * 128], ident[0:np_, 0:np_], True, True,
                               [("kb", ix_), "ident"], [pkt])
                        dst_sb = kT_sb if part == "k" else qT_sb
                        act(dst_sb[:, :, b_ * 128:b_ * 128 + np_], ptt[:, :].rearrange("p (c t) -> p c t", c=4)[:, :, 0:np_], AF.Identity,
                            [pkt, "gcolqk"], [("T" + part,)], scale=gcolqk[:, (0 if part == "q" else 1):(1 if part == "q" else 2)])
                    del pending[:]

                for b in range(nb):
                    r0 = 0 if sample else row0 + b * 128
                    parts = ["k", "v"] + (["q"] if own is not None else [])
                    newp = []
                    for part in parts:
                        c0 = {"q": 0, "k": 512, "v": 1024}[part]
                        pt_, pk_ = psb7()
                        for kc in range(8):
                            mm(pt_[0:np_, :], hT[:, kc, b * 128:b * 128 + np_], win[:, kc, c0:c0 + 512], kc == 0, kc == 7,
                               [hk, ("win", kc)], [pk_])
                        if part == "v":
                            ix = cnt[0] % 2
                            cnt[0] += 1
                            cp("act", vf[ix][0:np_, :], pt_[0:np_, :], [pk_], [("vf", ix)])
                            dma("sp", vout[r0:r0 + np_, :], vf[ix][0:np_, :], r=[("vf", ix)], w=[("vout", i, b)], is_output=True)
                            cp("pool", vb[ix][0:np_, :], vf[ix][0:np_, :], [("vf", ix)], [("vb", ix)])
                            dma("sp", vSd[row0 + b * 128:row0 + b * 128 + np_, :], vb[ix][0:np_, :], r=[("vb", ix)], w=[("vSd", i, b)])
                            continue
                        ix = (2 * b + (1 if part == "q" else 0)) % 4
                        qknorm(pt_, pk_, np_, gq_t if part == "q" else gk_t, sq[ix % 2], ix % 2, ssh[ix], kn[b % 2], b % 2, kb[ix], ix, part == "k")
                        if part == "k":
                            dma("sp", kout[r0:r0 + np_, :], kn[b % 2][0:np_, :], r=[("kn", b % 2)], w=[("kout", i, b)], is_output=True)
                        newp.append((part, b, ix))
                    flush()
                    pending.extend(newp)
                    if b == 0:
                        while pending_cs:
                            pending_cs.pop(0)()
                    if b == 0 and i + 1 < len(tiles):
                        normA(XB[(i + 1) % 2], ("xb", (i + 1) % 2), tiles[i + 1][1], tiles[i + 1][2], scr)
                uproj_pending = True
                u0 = 0 if (own is not None) else ntok - 128
                for c in range(4):
                    pa, pka = psb7()
                    pg, pkg = psb7()
                    for kc in range(8):
                        mm(pa[:, u0:ntok], win[:, kc, 1544 + c * 128:1544 + (c + 1) * 128], hT[:, kc, u0:ntok], kc == 0, kc == 7,
                           [hk, ("win", kc)], [pka])
                    for kc in range(8):
                        mm(pg[:, u0:ntok], win[:, kc, 2056 + c * 128:2056 + (c + 1) * 128], hT[:, kc, u0:ntok], kc == 0, kc == 7,
                           [hk, ("win", kc)], [pkg])
                    act(sgm[c % 2][:, u0:ntok], pg[:, u0:ntok], AF.Sigmoid, [pkg], [("sgm", c % 2)])
                    tt("dve", uT_sb[:, c, u0:ntok], pa[:, u0:ntok], sgm[c % 2][:, u0:ntok], ALU.mult, [pka, ("sgm", c % 2)], ["uT_sb"])
                flush()
                dma("sp", kTd_v[:, :, row0:row0 + ntok], kT_sb[:, :, 0:ntok], r=[("Tk",)], w=[("kTd", i)])
                if own is not None:
                    dma("sp", qTd_v[:, :, own * T:own * T + ntok], qT_sb[:, :, 0:ntok], r=[("Tq",)], w=[("qTd", i)])
                dma("sp", uTd_v[:, :, row0 + u0:row0 + ntok], uT_sb[:, :, u0:ntok], r=["uT_sb"], w=[("uTd", i)])
                if sample or i == NTA - 1:
                    bl = nb - 1
                    pa, pka = psb7()
                    pg, pkg = psb7()
                    for kc in range(8):
                        mm(pa[0:np_, :], hT[:, kc, bl * 128:bl * 128 + np_], win[:, kc, 1544:2056], kc == 0, kc == 7, [hk, ("win", kc)], [pka])
                    for kc in range(8):
                        mm(pg[0:np_, :], hT[:, kc, bl * 128:bl * 128 + np_], win[:, kc, 2056:2568], kc == 0, kc == 7, [hk, ("win", kc)], [pkg])
                    act(sgm[0][0:np_, :], pg[0:np_, :], AF.Sigmoid, [pkg], [("sgm", 0)])
                    tt("dve", utk[0:np_, :], pa[0:np_, :], sgm[0][0:np_, :], ALU.mult, [pka, ("sgm", 0)], ["utk"])
                    if sample:
                        for j in range(2):
                            dma("sp", cvs[j, 14:30, :], utk[16 * j:16 * j + 16, :], r=["utk"], w=[("cvs", j)], is_output=True)
                            dma("sp", cvs[j, 0:14, :], sconv[j, 16:30, :], w=[("cvs0", j)], is_output=True)
                    else:
                        dma("sp", cvp[:, :], utk[98:128, :], r=["utk"], w=["cvp"], is_output=True)
                nl = nb * NH
                tt("dve", lx[0:np_, 0:nl].rearrange("p (b h) -> p b h", h=NH), PSF[0:np_, fo:fo + nl].rearrange("p (b h) -> p b h", h=NH),
                   bf_t[0:np_, :].unsqueeze(1).to_broadcast([np_, nb, NH]), ALU.add, [pkf, "bf_t"], ["lx"])
                ts("dve", lm[0:np_, 0:nl], lx[0:np_, 0:nl], -1.0, None, ALU.mult, None, ["lx"], ["lm"])
                tt("dve", la[0:np_, 0:nl], lx[0:np_, 0:nl], lm[0:np_, 0:nl], ALU.max, ["lx", "lm"], ["la"])
                act(la[0:np_, 0:nl], la[0:np_, 0:nl], AF.Exp, ["la"], ["la"], scale=-1.0)
                act(la[0:np_, 0:nl], la[0:np_, 0:nl], AF.Ln, ["la"], ["la"], bias=onesf[0:np_, 0:1])
                ts("dve", lm[0:np_, 0:nl], lx[0:np_, 0:nl], 0.0, None, ALU.min, None, ["lx"], ["lm"])
                lf2 = lf[0:np_, 0:nb, :]
                tt("dve", lf2, lm[0:np_, 0:nl].rearrange("p (b h) -> p b h", h=NH), la[0:np_, 0:nl].rearrange("p (b h) -> p b h", h=NH),
                   ALU.subtract, ["lm", "la"], ["lf"])
                if sample:
                    dma("sp", lfs[:, :], lf[0:np_, 0, :], r=["lf"], w=["lfs"], is_output=True)
                    dma("sp", lfsd[:, :], lf[0:np_, 0, :], r=["lf"], w=["lfsd"])
                else:
                    dma("sp", lfp[row0:row0 + T, :].rearrange("(b p) h -> p b h", p=128), lf[:, 0:nb, :], r=["lf"], w=[("lfp", i)], is_output=True)
                    def cumsum_job(i=i, row0=row0, fo=fo, pkf=pkf, nb=nb):
                        memset("dve", accF, 0.0, ["accF"])
                        for b in range(nb):
                            mm(PSF[:, fo + 64 + b * NH:fo + 64 + (b + 1) * NH], UT, lf[:, b, :], True, False, ["UT", "lf"], [pkf])
                            mm(PSF[:, fo + 64 + b * NH:fo + 64 + (b + 1) * NH], onesf, accF, False, True, ["onesf", "accF"], [pkf])
                            tt("dve", accF, accF, lf[:, b, :], ALU.add, ["accF", "lf"], ["accF"])
                        cp("dve", cumF_sb, PSF[:, fo + 64:fo + 64 + 4 * NH], [pkf], ["cumF_sb"])
                        dma("sp", cumFd[row0:row0 + T, :].rearrange("(b p) h -> p b h", p=128), cumF_sb.rearrange("p (b h) -> p b h", h=NH),
                            r=["cumF_sb"], w=[("cumFd", i)])
                    pending_cs.append(cumsum_job)
                if i + 1 < len(tiles):
                    transp(tiles[i + 1][1], tiles[i + 1][2], 1, tiles[i + 1][3], hTs[(i + 1) % 2], scr, ("hT", (i + 1) % 2))
                if stage_c and len(tiles) >= 6:
                    if 1 <= i <= 4:
                        mix_emit_diag(diag_c, i - 1)
                    elif i == 5:
                        mix_emit_wout(wout_c)
            while pending_cs:
                pending_cs.pop(0)()
            if stage_c and len(tiles) < 6:
                mix_emit_wout(wout_c)
                for c in range(4):
                    mix_emit_diag(diag_c, c)
            reserved.clear()
            S.barrier()

        if "B" in phases:
            proj_phase()

        def conv_a(rhs_of, ntok, segs, diag, bufs, ukey):
            yf, y2, mean, ex2, rstd, t1 = bufs
            for c in range(4):
                pc, pkc = psb()
                for si, (c0, c1) in enumerate(segs):
                    for tap in range(CW):
                        mm(pc[:, c0:c1], diag[:, c, tap, :], rhs_of(c, si, tap), tap == 0, tap == CW - 1, [("diag", c), ukey], [pkc])
                act(yf[:, c, 0:ntok], pc[:, 0:ntok], AF.Identity, [pkc], [("yf", c)], bias=ccol[:, c, 0:1])
                act(y2[:, c, 0:ntok], yf[:, c, 0:ntok], AF.Square, [("yf", c)], [("y2", c)])

        def conv_b(ntok, catT, bufs):
            yf, y2, mean, ex2, rstd, t1 = bufs
            p1, pk1 = psb()
            p2, pk2 = psb()
            for c in range(4):
                mm(p1[:, 0:ntok], onesf, yf[:, c, 0:ntok], c == 0, c == 3, ["onesf", ("yf", c)], [pk1])
            for c in range(4):
                mm(p2[:, 0:ntok], onesf, y2[:, c, 0:ntok], c == 0, c == 3, ["onesf", ("y2", c)], [pk2])
            ts("dve", mean[:, 0:ntok], p1[:, 0:ntok], 1.0 / 512, None, ALU.mult, None, [pk1], ["mean"])
            ts("dve", ex2[:, 0:ntok], p2[:, 0:ntok], 1.0 / 512, None, ALU.mult, None, [pk2], ["ex2"])
            tt("pool", rstd[:, 0:ntok], mean[:, 0:ntok], mean[:, 0:ntok], ALU.mult, ["mean"], ["rstd"])
            tt("dve", rstd[:, 0:ntok], ex2[:, 0:ntok], rstd[:, 0:ntok], ALU.subtract, ["ex2", "rstd"], ["rstd"])
            act(rstd[:, 0:ntok], rstd[:, 0:ntok], AF.Ln, ["rstd"], ["rstd"], bias=epsb[:, 0:1], scale=1.0)
            act(rstd[:, 0:ntok], rstd[:, 0:ntok], AF.Exp, ["rstd"], ["rstd"], scale=-0.5)
            for c in range(4):
                tb = t1[c % 2]
                tt("dve", tb[:, 0:ntok], yf[:, c, 0:ntok], mean[:, 0:ntok], ALU.subtract, [("yf", c), "mean"], [("t1", c % 2)])
                tt("pool", tb[:, 0:ntok], tb[:, 0:ntok], rstd[:, 0:ntok], ALU.mult, [("t1", c % 2), "rstd"], [("t1", c % 2)])
                act(catT[:, 4 + c, 0:ntok], tb[:, 0:ntok], AF.Silu, [("t1", c % 2)], [("catT", 4 + c)], bias=ccol[:, c, 2:3], scale=ccol[:, c, 1:2])

        def attn_finish_group(items, catT, ncol, col0, recs, rshs):
            geo = []
            for i, (acc, pka, hl, cc) in enumerate(items):
                o0, d0 = (0, 64) if hl % 2 == 0 else (64, 0)
                geo.append((o0, d0))
                recip(recs[i][d0:d0 + 1, 0:ncol], acc[d0:d0 + 1, 0:ncol], [pka], [("rec", i)])
            for i, (acc, pka, hl, cc) in enumerate(items):
                o0, d0 = geo[i]
                pb_, pkb_ = psb()
                mm(pb_[:, 0:ncol], onesf[d0:d0 + 1, 0:128], recs[i][d0:d0 + 1, 0:ncol], True, True, ["onesf", ("rec", i)], [pkb_])
                cp("act", rshs[i][o0:o0 + 64, 0:ncol], pb_[o0:o0 + 64, 0:ncol], [pkb_], [("rsh", i)])
            for i, (acc, pka, hl, cc) in enumerate(items):
                o0, d0 = geo[i]
                tt("dve", catT[o0:o0 + 64, cc, col0:col0 + ncol], acc[o0:o0 + 64, 0:ncol], rshs[i][o0:o0 + 64, 0:ncol], ALU.mult,
                   [pka, ("rsh", i)], [("catT", cc)])

        def attn_finish_start(items, catT, accsb):
            jobs = []
            geo = []
            for i, (acc, pka, hl, cc) in enumerate(items):
                o0, d0 = (0, 64) if hl % 2 == 0 else (64, 0)
                geo.append((o0, d0))
                cp("act" if i % 2 == 0 else "dve", accsb[i], acc[:, 0:T], [pka], [("accsb", i)])
            for i, (acc, pka, hl, cc) in enumerate(items):
                o0, d0 = geo[i]
                recip(accsb[i][d0:d0 + 1, :], accsb[i][d0:d0 + 1, :], [("accsb", i)], [("accsb", i)])

                def job(i=i, o0=o0, d0=d0, cc=cc):
                    pb_, pkb_ = psb()
                    mm(pb_[:, 0:T], onesf[d0:d0 + 1, 0:128], accsb[i][d0:d0 + 1, :], True, True, ["onesf", ("accsb", i)], [pkb_])
                    tt("dve", catT[o0:o0 + 64, cc, 0:T], accsb[i][o0:o0 + 64, :], pb_[o0:o0 + 64, 0:T], ALU.mult,
                       [("accsb", i), pkb_], [("catT", cc)])
                jobs.append(job)
            return jobs

        def out_proj(catT, wout, xb, xk, nb, np_, gate, tmp):
            k = 0
            for b in range(nb):
                for h in range(2):
                    po, pko = psb()
                    for c in range(8):
                        mm(po[0:np_, :], catT[:, c, b * 128:b * 128 + np_], wout[:, c, h * 512:(h + 1) * 512], c == 0, c == 7,
                           [("catT", c), "wout"], [pko])
                    tb = tmp[k % 2]
                    tt("dve", tb[0:np_, :], po[0:np_, :], gate[0:np_, h * 512:(h + 1) * 512], ALU.mult, [pko, "gate"], [("tmp", k % 2)])
                    tt("pool", xb[0:np_, b, h * 512:(h + 1) * 512], tb[0:np_, :], xb[0:np_, b, h * 512:(h + 1) * 512], ALU.add,
                       [("tmp", k % 2), xk], [xk])
                    k += 1

        def mix_prompt():
            AR.reset()
            wout, diag = mix_setup(emit=("B" not in phases))
            NBK = NTA * 4
            Fg = AR.f32(NBK * NH).rearrange("p (j h) -> p j h", h=NH)
            Fst = AR.f32(NTO * NH).rearrange("p (m h) -> p m h", h=NH)
            Fen = AR.f32(NTO * NH).rearrange("p (m h) -> p m h", h=NH)
            Rc = AR.f32(NTO * NH).rearrange("p (m h) -> p m h", h=NH)
            gate = AR.f32(D)
            XB = [AR.f32(4 * D).rearrange("p (b d) -> p b d", b=4) for _ in range(2)]
            qT_sb = [AR.bf16(NH * T).rearrange("p (h t) -> p h t", h=NH) for _ in range(2)]
            UW = T + 32
            uh = [AR.bf16(4 * UW).rearrange("p (c t) -> p c t", c=4) for _ in range(2)]
            catT = AR.bf16(8 * T).rearrange("p (c t) -> p c t", c=8)
            halo = [AR.bf16(4 * 2 * 32).rearrange("p (c w t) -> p c w t", c=4, w=2)[:, :, :, 0:30] for _ in range(2)]
            kTs = [AR.bf16(2 * T).rearrange("p (c t) -> p c t", c=2) for _ in range(2)]
            Vst = [AR.bf16(4 * 256).rearrange("p (b f) -> p b f", b=4) for _ in range(2)]
            Vp = [AR.bf16(4 * 4 * 128).rearrange("p (b h m) -> p b h m", b=4, h=4) for _ in range(2)]
            pT = [AR.bf16(T) for _ in range(4)]
            bias = [AR.f32(4 * NH).rearrange("p (b h) -> p b h", h=NH) for _ in range(2)]
            accsb = [AR.f32(T) for _ in range(4)]
            fin_jobs = []
            yf = AR.f32(4 * T).rearrange("p (c t) -> p c t", c=4)
            y2 = AR.f32(4 * T).rearrange("p (c t) -> p c t", c=4)
            mean = AR.f32(T); ex2 = AR.f32(T); rstd = AR.f32(T)
            t1 = [AR.f32(T) for _ in range(2)]
            tmp = [AR.f32(512) for _ in range(2)]

            dma("sp", Fg, cumFd.rearrange("(j p) h -> p j h", p=128), w=["Fg"])
            own_v = cumFd[NTO * T:NTA * T, :].rearrange("(m t) h -> m t h", t=T)
            dma("sp", Fst, own_v[:, 0, :].partition_broadcast(128), w=["Fst"])
            dma("sp", Fen, own_v[:, T - 1, :].partition_broadcast(128), w=["Fen"])
            tot = AR.f32(NH)
            Lsb = AR.f32(NTA)
            rhs3 = AR.f32(NTA * NH)
            off = AR.f32(NTA * NH).rearrange("p (j h) -> p j h", h=NH)
            dma("sp", tot[0:NTA, :], cumFd.rearrange("(j t) h -> j t h", t=T)[:, T - 1, :], w=["tot"])
            dma("sp", Lsb[0:NTA, :], lmat[:, :], w=["Lsb"])
            tt("dve", rhs3[0:NTA, :].rearrange("p (j h) -> p j h", h=NH), Lsb[0:NTA, :].unsqueeze(2).to_broadcast([NTA, NTA, NH]),
               tot[0:NTA, :].unsqueeze(1).to_broadcast([NTA, NTA, NH]), ALU.mult, ["Lsb", "tot"], ["rhs3"])
            pof, pko = psb()
            mm(pof[:, 0:NTA * NH], onesf[0:NTA, :], rhs3[0:NTA, :], True, True, ["onesf", "rhs3"], [pko])
            cp("dve", off, pof[:, 0:NTA * NH].rearrange("p (j h) -> p j h", h=NH), [pko], ["off"])
            tt("dve", Fg.rearrange("p (j b) h -> p j b h", b=4), Fg.rearrange("p (j b) h -> p j b h", b=4),
               off.unsqueeze(2).to_broadcast([128, NTA, 4, NH]), ALU.add, ["Fg", "off"], ["Fg"])
            tt("dve", Fst, Fst, off[:, NTO:NTA, :], ALU.add, ["Fst", "off"], ["Fst"])
            tt("dve", Fen, Fen, off[:, NTO:NTA, :], ALU.add, ["Fen", "off"], ["Fen"])
            tt("dve", Rc, Fst, Fen, ALU.add, ["Fst", "Fen"], ["Rc"])
            ts("dve", Rc, Rc, 0.5, cshift[:, 0:1], ALU.mult, ALU.subtract, ["Rc", "cshift"], ["Rc"])
            load_gate(gate, 1, False)
            for vb_ in Vp:
                memset("pool", vb_, 1.0, ["Vp0", "Vp1"])
            for qi, qb_ in enumerate(qT_sb):
                memset("dve", qb_, 0.0, [("qT", qi)])
            qTd_h = qTd.rearrange("(c e d) t -> e d c t", e=2, d=64)
            kTd_v = kTd.rearrange("(c p) t -> p c t", p=128)
            qTd_v = qTd.rearrange("(c p) t -> p c t", p=128)
            uTd_v = uTd.rearrange("(c p) t -> p c t", p=128)
            for i in (0, 1, 2, 3):
                reserved.add(i)

            def load_tile(m):
                g = NTO + m
                dma("sp", XB[m % 2][:, :, :], x1d[g * T:(g + 1) * T, :].rearrange("(b p) d -> p b d", p=128), w=[("xb", m % 2)])
                q4 = qT_sb[m % 2].rearrange("p (c e) t -> p c e t", e=2)
                for e_ in range(2):
                    dma("sp", q4[64 * e_:64 * e_ + 64, :, e_, :], qTd_h[e_, :, :, m * T:(m + 1) * T], w=[("qT", m % 2)])
                dma("sp", uh[m % 2][:, :, 32:UW], uTd_v[:, :, g * T:(g + 1) * T], w=[("uh", m % 2)])
                po_ = (NTO + m - 1) if m >= 1 else 0
                dma("sp", halo[m % 2][:, :, 0, :], uTd_v[:, :, po_ * T + T - 30:(po_ + 1) * T], w=[("halo", m % 2)])
                dma("sp", halo[m % 2][:, :, 1, :], uTd_v[:, :, m * T + T - 30:(m + 1) * T], w=[("halo", m % 2)])

            load_tile(0)
            kvn = [0]
            ptn = [0]
            kvmap = {}
            cbufs = (yf, y2, mean, ex2, rstd, t1)

            def halo_blend(m):
                uhh = uh[m % 2]; uk = ("uh", m % 2)
                hl_ = halo[m % 2]
                ts("pool", hl_[:, :, 0, :], hl_[:, :, 0, :], flag_t[:, NTO + m:NTO + m + 1], None, ALU.mult, None, [("halo", m % 2), "flag"], [("halo", m % 2)])
                S.op("dve", lambda e: e.scalar_tensor_tensor(
                    out=uhh[:, :, 2:32], in0=hl_[:, :, 1, :], scalar=flag_t[:, 2 * NTO + m:2 * NTO + m + 1], in1=hl_[:, :, 0, :],
                    op0=ALU.mult, op1=ALU.add), reads=[("halo", m % 2), "flag"], writes=[uk])

            def conv_front(m):
                uhh = uh[m % 2]; uk = ("uh", m % 2)

                def rhs_of(c, si, tap):
                    return uhh[:, c, 2 + tap:2 + tap + T]
                conv_a(rhs_of, T, [(0, T)], diag, cbufs, uk)

            halo_blend(0)
            conv_front(0)
            for m in range(NTO):
                g = NTO + m
                xb = XB[m % 2]; xk = ("xb", m % 2)
                qt = qT_sb[m % 2]; qk = ("qT", m % 2)
                uhh = uh[m % 2]; uk = ("uh", m % 2)
                if m + 1 < NTO:
                    load_tile(m + 1)
                for hg in range(2):
                    accs = [(PS[hl], ("ps", hl)) for hl in range(4)]
                    def prep(p, hg=hg, m=m):
                        if (m, hg, p) in kvmap:
                            return
                        bi = kvn[0] % 2
                        kvn[0] += 1
                        kvmap[(m, hg, p)] = bi
                        kb_, vs_, vp_, bs_ = kTs[bi], Vst[bi], Vp[bi], bias[bi]
                        dma("sp", kb_, kTd_v[:, 2 * hg:2 * hg + 2, p * T:(p + 1) * T], w=[("kTs", bi)])
                        dma("sp", vs_, vSd[p * T:(p + 1) * T, hg * 256:(hg + 1) * 256].rearrange("(b s) f -> s b f", s=128), w=[("Vst", bi)])
                        vs5 = vs_.rearrange("p b (c e d) -> p b c e d", c=2, e=2)
                        vp5 = vp_.rearrange("p b (c e) m -> p b c e m", c=2)
                        cp("pool", vp5[:, :, :, 0, 0:64], vs5[:, :, :, 0, :], [("Vst", bi)], ["Vp%d" % bi])
                        cp("dve", vp5[:, :, :, 1, 64:128], vs5[:, :, :, 1, :], [("Vst", bi)], ["Vp%d" % bi])
                        tt("dve", bs_, Rc[:, m:m + 1, :].to_broadcast([128, 4, NH]), Fg[:, p * 4:(p + 1) * 4, :], ALU.subtract, ["Rc", "Fg"], [("bias", bi)])
                        if p == m:
                            ts("dve", bs_, bs_, flag_t[:, m:m + 1], None, ALU.add, None, [("bias", bi), "flag"], [("bias", bi)])

                    plist = list(range(m + 1)) + list(range(NTO, g + 1))
                    units = [(p, b, hl) for p in plist for b in range(4) for hl in range(4)]
                    ufirst = units[0][0]
                    stb = {}

                    def qk(u, hg=hg, qt=qt, qk_=qk, m_=m):
                        p, b, hl = u
                        prep(p)
                        bi = kvmap[(m_, hg, p)]
                        cc = hl // 2
                        r0 = (hl % 2) * 64
                        st, pks = psb()
                        stb[u] = (st, pks)
                        mm(st[:, 0:T], kTs[bi][:, cc, b * 128:(b + 1) * 128], qt[:, hg * 4 + hl, :], True, True,
                           [("kTs", bi), qk_], [pks])

                    def rest(u, hg=hg, g=g, ufirst=ufirst, m_=m):
                        p, b, hl = u
                        bi = kvmap[(m_, hg, p)]
                        h = hg * 4 + hl
                        st, pks = stb.pop(u)
                        pi = ptn[0] % 4
                        ptn[0] += 1
                        act(pT[pi], st[:, 0:T], AF.Exp, [pks, ("bias", bi)], [("pT", pi)], bias=bias[bi][:, b, h:h + 1], scale=0.125)
                        if p == g:
                            tt("dve", pT[pi], pT[pi], masks[:, b, :], ALU.mult, [("pT", pi), "masks"], [("pT", pi)])
                        acc, pka = accs[hl]
                        mm(acc[:, 0:T], Vp[bi][:, b, hl, :], pT[pi], (p == ufirst and b == 0), (p == g and b == 3),
                           ["Vp%d" % bi, ("pT", pi)], [pka])

                    LOOK = 3
                    for idx in range(min(LOOK, len(units))):
                        qk(units[idx])
                    for idx, u in enumerate(units):
                        rest(u)
                        if fin_jobs and idx in (6, 12, 18, 24):
                            fin_jobs.pop(0)()
                        if idx + LOOK < len(units):
                            qk(units[idx + LOOK])
                    while fin_jobs:
                        fin_jobs.pop(0)()
                    if hg == 0:
                        prep(0, hg=1, m=m)
                        if m + 1 < NTO:
                            halo_blend(m + 1)
                    elif m + 1 < NTO:
                        prep(0, hg=0, m=m + 1)
                    fin_jobs.extend(attn_finish_start([(accs[hl][0], accs[hl][1], hl, hg * 2 + hl // 2) for hl in range(4)], catT, accsb))
                conv_b(T, catT, cbufs)
                if m + 1 < NTO:
                    conv_front(m + 1)
                while fin_jobs:
                    fin_jobs.pop(0)()
                out_proj(catT, wout, xb, xk, 4, 128, gate, tmp)
                dma("sp", x2d[m * T:(m + 1) * T, :].rearrange("(b p) d -> p b d", p=128), xb[:, :, :], r=[xk], w=[("x2d", m)])
            reserved.clear()
            S.barrier()

        if "C" in phases:
            mix_prompt()

        def mix_sample():
            AR.reset()
            wout, diag = mix_setup(emit=("C" not in phases))
            gate = AR.f32(D)
            xb = AR.f32(D).rearrange("p (b d) -> p b d", b=1)
            qs = AR.bf16(4 * 32).rearrange("p (c t) -> p c t", c=4)
            kn_ = AR.bf16(4 * 32).rearrange("p (c t) -> p c t", c=4)
            us = AR.bf16(4 * 32).rearrange("p (c t) -> p c t", c=4)
            vnew = AR.bf16(512)
            VpN = AR.bf16(NH * 128).rearrange("p (h m) -> p h m", h=NH)
            uhs = AR.bf16(4 * 2 * 48).rearrange("p (c j t) -> p c j t", c=4, j=2)
            scf = AR.f32(512)
            scb = AR.bf16(512)
            catT = AR.bf16(8 * 32).rearrange("p (c t) -> p c t", c=8)
            ckbs = [AR.bf16(NPB * 512).rearrange("p (b f) -> p b f", b=NPB) for _ in range(2)]
            cvbs = [AR.bf16(NPB * 512).rearrange("p (b f) -> p b f", b=NPB) for _ in range(2)]
            lfcs = [AR.f32(NPB * NH).rearrange("p (b h) -> p b h", h=NH) for _ in range(2)]
            for j in range(2):
                dma("pool", ckbs[j], ck[j, :, :].rearrange("(b p) f -> p b f", p=128), w=[("ckb", j)])
                dma("pool", cvbs[j], cv[j, :, :].rearrange("(b p) f -> p b f", p=128), w=[("cvb", j)])
                dma("sp", lfcs[j], clf[j, :, :].rearrange("(b p) h -> p b h", p=128), w=[("lfc", j)])
            kTc = AR.bf16(4 * PAST).rearrange("p (c t) -> p c t", c=4)
            VpC = AR.bf16(NPB * NH * 128).rearrange("p (b h m) -> p b h m", b=NPB, h=NH)
            Fc = AR.f32(NPB * NH).rearrange("p (b h) -> p b h", h=NH)
            accS = AR.f32(NH)
            Rcs = AR.f32(NH)
            lfn = AR.f32(NH)
            Fn = AR.f32(NH)
            biasC = AR.f32(NPB * NH).rearrange("p (b h) -> p b h", h=NH)
            biasN = AR.f32(NH)
            pT = [AR.bf16(16) for _ in range(4)]
            rec = [AR.f32(16) for _ in range(2)]
            rsh = [AR.f32(16) for _ in range(2)]
            yf = AR.f32(4 * 32).rearrange("p (c t) -> p c t", c=4)
            y2 = AR.f32(4 * 32).rearrange("p (c t) -> p c t", c=4)
            mean = AR.f32(32); ex2 = AR.f32(32); rstd = AR.f32(32)
            t1 = [AR.f32(32) for _ in range(2)]
            tmp = [AR.f32(512) for _ in range(2)]
            kTd_v = kTd.rearrange("(c p) t -> p c t", p=128)
            qTd_v = qTd.rearrange("(c p) t -> p c t", p=128)
            uTd_v = uTd.rearrange("(c p) t -> p c t", p=128)

            load_gate(gate, 1, True)
            dma("sp", xb[0:32, 0, :], x1d[TP:TP + 32, :], w=["xb"])
            dma("sp", qs, qTd_v[:, :, TO:TO + 32], w=["qs"])
            dma("sp", kn_, kTd_v[:, :, TP:TP + 32], w=["kn_"])
            dma("sp", us, uTd_v[:, :, TP:TP + 32], w=["us"])
            memset("pool", VpC, 1.0, ["VpC"])
            for j in range(2):
                dma("sp", scf[0:30, :], sconv[j, :, :], w=["scf"])
                cp("dve", scb[0:30, :], scf[0:30, :], ["scf"], ["scb"])
                pst, pk = psb()
                for c in range(4):
                    mm(pst[:, c * 32:c * 32 + 30], scb[0:30, c * 128:(c + 1) * 128], ident[0:30, 0:30], True, True, ["scb", "ident"], [pk])
                cp("act", uhs[:, :, j, 0:30], pst[:, 0:128].rearrange("p (c t) -> p c t", c=4)[:, :, 0:30], [pk], ["uhs"])
                cp("dve", uhs[:, :, j, 30:46], us[:, :, 16 * j:16 * j + 16], ["us"], ["uhs"])

            def rhs_of(c, si, tap):
                return uhs[:, c, si, tap:tap + 16]
            conv_a(rhs_of, 32, [(0, 16), (16, 32)], diag, (yf, y2, mean, ex2, rstd, t1), "uhs")
            conv_b(32, catT, (yf, y2, mean, ex2, rstd, t1))
            for i in (0, 1):
                reserved.add(i)
            for j in range(2):
                ckb, cvb, lfc = ckbs[j], cvbs[j], lfcs[j]
                dma("sp", lfn[0:16, :], lfsd[16 * j:16 * j + 16, :], r=["lfsd"], w=["lfn"])
                dma("sp", vnew[0:16, :], vSd[TP + 16 * j:TP + 16 * j + 16, :], w=["vnew"])
                for blk in range(NPB):
                    pst, pk = psb()
                    for c in range(4):
                        mm(pst[:, c * 128:(c + 1) * 128], ckb[:, blk, c * 128:(c + 1) * 128], ident, True, True, [("ckb", j), "ident"], [pk])
                    cp("act" if blk % 2 == 0 else "dve", kTc[:, :, blk * 128:(blk + 1) * 128], pst[:, :].rearrange("p (c t) -> p c t", c=4), [pk], ["kTc"])
                    cv4 = cvb[:, blk, :].rearrange("p (c e d) -> p c e d", e=2, d=64)
                    vp4 = VpC[:, blk, :, :].rearrange("p (c e) m -> p c e m", e=2)
                    cp("pool", vp4[:, :, 0, 0:64], cv4[:, :, 0, :], [("cvb", j), "VpC"], ["VpC"])
                    cp("dve", vp4[:, :, 1, 64:128], cv4[:, :, 1, :], [("cvb", j), "VpC"], ["VpC"])
                memset("pool", VpN, 1.0, ["VpN"])
                for h in range(NH):
                    e0 = (h % 2) * 64
                    cp("pool", VpN[0:16, h, e0:e0 + 64], vnew[0:16, h * 64:(h + 1) * 64], ["vnew", "VpN"], ["VpN"])
                memset("dve", accS, 0.0, ["accS"])
                pcs, pkc = psb()
                for blk in range(NPB):
                    mm(pcs[:, blk * NH:(blk + 1) * NH], UT, lfc[:, blk, :], True, False, ["UT", ("lfc", j)], [pkc])
                    mm(pcs[:, blk * NH:(blk + 1) * NH], onesf, accS, False, True, ["onesf", "accS"], [pkc])
                    tt("dve", accS, accS, lfc[:, blk, :], ALU.add, ["accS", ("lfc", j)], ["accS"])
                cp("dve", Fc, pcs[:, 0:NPB * NH].rearrange("p (b h) -> p b h", h=NH), [pkc], ["Fc"])
                pr, pkr = psb()
                mm(pr[:, 0:NH], onesf, accS, True, True, ["onesf", "accS"], [pkr])
                mm(pr[0:16, NH:2 * NH], UT[0:16, 0:16], lfn[0:16, :], True, False, ["UT", "lfn"], [pkr])
                mm(pr[0:16, NH:2 * NH], onesf[:, 0:16], accS, False, True, ["onesf", "accS"], [pkr])
                ts("dve", Rcs, pr[:, 0:NH], cshift[:, 0:1], None, ALU.subtract, None, [pkr, "cshift"], ["Rcs"])
                cp("dve", Fn[0:16, :], pr[0:16, NH:2 * NH], [pkr], ["Fn"])
                tt("dve", biasC, Rcs.unsqueeze(1).to_broadcast([128, NPB, NH]), Fc, ALU.subtract, ["Rcs", "Fc"], ["biasC"])
                tt("dve", biasN[0:16, :], Rcs[0:16, :], Fn[0:16, :], ALU.subtract, ["Rcs", "Fn"], ["biasN"])
                pn = [0]
                units = [(h, blk) for h in range(NH) for blk in range(NPB + 1)]
                stb = {}

                def qk(u, j=j):
                    h, blk = u
                    cc = h // 2
                    r0 = (h % 2) * 64
                    qcol = qs[r0:r0 + 64, cc, 16 * j:16 * j + 16]
                    st, pks = psb()
                    stb[u] = (st, pks)
                    if blk < NPB:
                        mm(st[:, 0:16], kTc[r0:r0 + 64, cc, blk * 128:(blk + 1) * 128], qcol, True, True, ["kTc", "qs"], [pks])
                    else:
                        mm(st[0:16, 0:16], kn_[r0:r0 + 64, cc, 16 * j:16 * j + 16], qcol, True, True, ["kn_", "qs"], [pks])

                def rest(u, j=j):
                    h, blk = u
                    cc = h // 2
                    acc, pka = PS[h % 2], ("ps", h % 2)
                    st, pks = stb.pop(u)
                    pi = pn[0] % 4
                    pn[0] += 1
                    if blk < NPB:
                        act(pT[pi], st[:, 0:16], AF.Exp, [pks, "biasC"], [("pT", pi)], bias=biasC[:, blk, h:h + 1], scale=0.125)
                        mm(acc[:, 0:16], VpC[:, blk, h, :], pT[pi], blk == 0, False, ["VpC", ("pT", pi)], [pka])
                    else:
                        act(pT[pi][0:16, :], st[0:16, 0:16], AF.Exp, [pks, "biasN"], [("pT", pi)], bias=biasN[0:16, h:h + 1], scale=0.125)
                        tt("pool", pT[pi][0:16, :], pT[pi][0:16, :], masks[0:16, 0, 0:16], ALU.mult, [("pT", pi), "masks"], [("pT", pi)])
                        mm(acc[:, 0:16], VpN[0:16, h, :], pT[pi][0:16, :], False, True, ["VpN", ("pT", pi)], [pka])
                        attn_finish_group([(acc, pka, h, cc)], catT, 16, 16 * j, [rec[h % 2]], [rsh[h % 2]])

                LOOK = 3
                for idx in range(LOOK):
                    qk(units[idx])
                for idx, u in enumerate(units):
                    rest(u)
                    if idx + LOOK < len(units):
                        qk(units[idx + LOOK])
            reserved.clear()
            out_proj(catT, wout, xb, "xb", 1, 32, gate, tmp)
            dma("sp", x2d[TO:TO + 32, :], xb[0:32, 0, :], r=["xb"], w=["x2ds"])
            S.barrier()

        if "C" in phases:
            mix_sample()

        if "D" in phases:
            tilesD = [(x2d[m * T:(m + 1) * T, :], yp[m * T:(m + 1) * T, :], 4, 128, False) for m in range(NTO)]
            tilesD.append((x2d[TO:TO + 32, :], ys[:, :], 1, 32, True))
            ffn_phase(1, tilesD, True)

        S.emit()
    return nc


_CACHE = {}


def kernel(x_prompt, x_sample, c_prompt, c_sample, cache_k, cache_v, cache_logf, state_conv,
           w_ada, b_ada, g_ffn1, w_up1, w_down1, g_mix, w_in, b_f, g_q, g_k,
           conv_w, conv_b, conv_ln_g, conv_ln_b, w_out, g_ffn2, w_up2, w_down2, g_final, _phases="0ABCD"):
    f = lambda a: np.ascontiguousarray(np.asarray(a, dtype=np.float32))
    x_prompt = f(x_prompt); x_sample = f(x_sample)
    B, SEQ, _ = x_prompt.shape
    SB, SS, _ = x_sample.shape
    PAST = cache_k.shape[2]
    assert B * 2 == 8 and SB == 16 and SS == 16 and SEQ % (2 * T) == 0
    HALF = SEQ // 2
    NTO = HALF // T
    key = (NTO, PAST, _phases)
    if key not in _CACHE:
        _CACHE[key] = build(NTO, PAST, _phases)
    nc = _CACHE[key]
    shared = {
        "w_ada": f(w_ada)[0], "b_ada": f(b_ada), "g3": np.stack([f(g_ffn1)[0], f(g_mix)[0], f(g_ffn2)[0]]),
        "w_up1": f(w_up1)[0], "w_up2": f(w_up2)[0], "w_down1": f(w_down1)[0], "w_down2": f(w_down2)[0],
        "w_in": f(w_in)[0], "b_f": f(b_f), "g_q": f(g_q), "g_k": f(g_k), "conv_w": f(conv_w)[0],
        "cvec3": np.stack([f(conv_b)[0], f(conv_ln_g)[0], f(conv_ln_b)[0]]), "w_out": f(w_out)[0], "g_final": f(g_final),
    }
    cs_ = f(c_sample); cp_ = f(c_prompt)
    ck_ = f(cache_k)[0].reshape(SB, PAST, 512); cv_ = f(cache_v)[0].reshape(SB, PAST, 512)
    clf_ = f(cache_logf)[0]; sc_ = f(state_conv)[0]
    in_maps = []
    NTA = 2 * NTO
    own_t = {0: [t for t in range(NTA) if t % 4 in (0, 3)], 1: [t for t in range(NTA) if t % 4 in (1, 2)]}
    storage = {r: own_t[1 - r] + own_t[r] for r in (0, 1)}
    for core in range(8):
        b, r = core // 2, core % 2
        xt_ = x_prompt[b].reshape(NTA, T, D)
        m = dict(shared)
        m["xp"] = np.ascontiguousarray(xt_[storage[r]].reshape(NTA * T, D))
        glob = storage[r]
        lm = np.zeros((NTA, NTA), np.float32)
        for a_ in range(NTA):
            for c_ in range(NTA):
                lm[a_, c_] = 1.0 if glob[a_] < glob[c_] else 0.0
        m["lmat"] = lm
        m["xs"] = np.ascontiguousarray(x_sample[2 * core:2 * core + 2].reshape(32, D))
        m["cvec"] = np.ascontiguousarray(np.stack([cp_[b], cs_[2 * core], cs_[2 * core + 1]]))
        m["ck"] = np.ascontiguousarray(ck_[2 * core:2 * core + 2]); m["cv"] = np.ascontiguousarray(cv_[2 * core:2 * core + 2])
        m["clf"] = np.ascontiguousarray(clf_[2 * core:2 * core + 2]); m["sconv"] = np.ascontiguousarray(sc_[2 * core:2 * core + 2])
        fl = np.zeros((128, 3 * NTO), np.float32)
        for mm_ in range(NTO):
            o_, x_ = own_t[r][mm_], own_t[1 - r][mm_]
            fl[:, mm_] = NEGBIG if x_ > o_ else 0.0
            pred = o_ - 1
            if pred >= 0:
                if mm_ >= 1 and own_t[r][mm_ - 1] == pred:
                    fl[:, NTO + mm_] = 1.0
                else:
                    assert x_ == pred, (r, mm_, pred)
                    fl[:, 2 * NTO + mm_] = 1.0
        m["flagc"] = fl
        in_maps.append(m)
    res = run_bass_kernel_spmd(nc, in_maps, core_ids=list(range(8)))
    R = res.results
    y_prompt = np.zeros((B, SEQ, D), np.float32)
    k_prompt = np.zeros((1, B, SEQ, NH, DH), np.float32); v_prompt = np.zeros_like(k_prompt)
    logf_prompt = np.zeros((1, B, SEQ, NH), np.float32); conv_prompt = np.zeros((1, B, 30, 512), np.float32)
    y_sample = np.zeros((SB, SS, D), np.float32)
    k_sample = np.zeros((1, SB, SS, NH, DH), np.float32); v_sample = np.zeros_like(k_sample)
    logf_sample = np.zeros((1, SB, SS, NH), np.float32); conv_sample = np.zeros((1, SB, 30, 512), np.float32)
    for core in range(8):
        b, r = core // 2, core % 2
        o = R[core]
        yv = y_prompt[b].reshape(NTA, T, D)
        yv[own_t[r]] = o["yp"].reshape(NTO, T, D)
        if r == 1:
            inv = storage[r]
            k_prompt[0, b].reshape(NTA, T, NH, DH)[inv] = o["kp"].reshape(NTA, T, NH, DH)
            v_prompt[0, b].reshape(NTA, T, NH, DH)[inv] = o["vp"].reshape(NTA, T, NH, DH)
            logf_prompt[0, b].reshape(NTA, T, NH)[inv] = o["lfp"].reshape(NTA, T, NH)
        if own_t[r][-1] == NTA - 1:
            conv_prompt[0, b] = o["cvp"]
        sl = slice(2 * core, 2 * core + 2)
        y_sample[sl] = o["ys"].reshape(2, SS, D)
        k_sample[0, sl] = o["ks"].reshape(2, SS, NH, DH); v_sample[0, sl] = o["vs"].reshape(2, SS, NH, DH)
        logf_sample[0, sl] = o["lfs"].reshape(2, SS, NH); conv_sample[0, sl] = o["cvs"]
    return (y_prompt, y_sample, k_prompt, v_prompt, logf_prompt, conv_prompt, k_sample, v_sample, logf_sample, conv_sample)
```

```python
import contextlib
import numpy as np
import ml_dtypes
import concourse.bass as bass
import concourse.mybir as mybir
from concourse.bass_utils import run_bass_kernel_spmd

F32 = mybir.dt.float32
BF16 = mybir.dt.bfloat16
AF = mybir.ActivationFunctionType
ALU = mybir.AluOpType
AX = mybir.AxisListType

COMPUTE = ("pe", "act", "dve", "pool")
NDMA_SEMS = 32
NSW_SEMS = 8

D = 1024
DFF = 2816
NF = DFF // 128
NH = 8
DH = 64
DIN = 2568
T = 512
CW = 31
EPS = 1e-6
NEGBIG = -30000.0
WUP_GROUPS = ((0, 4), (4, 10), (10, 16), (16, 22))


class Sched:
    def __init__(self, nc, same_engine_sync=True):
        self.nc = nc
        self.same_engine_sync = same_engine_sync
        self.queues = {e: [] for e in ("pe", "act", "dve", "pool", "sp")}
        self.count = {e: 0 for e in COMPUTE}
        self.dma_n = 0
        self.dma_sw = 0
        self.dma_slot_last = [0] * NDMA_SEMS
        self.last_write = {}
        self.reads = {}
        self.waited = {e: {} for e in self.queues}
        self.out_dma = []
        self.needed = {e: set() for e in COMPUTE}
        self.pending = {e: {} for e in self.queues}

    def _deps(self, eng, reads, writes):
        need = dict(self.pending[eng])
        self.pending[eng] = {}

        def add(tok):
            if tok is None:
                return
            k, v = tok
            if need.get(k, 0) < v:
                need[k] = v

        for r in reads:
            add(self.last_write.get(r))
        for w in writes:
            add(self.last_write.get(w))
            for t in self.reads.get(w, ()):
                add(t)
        out = {}
        for k, v in need.items():
            if k == eng and (eng == "pe" or not self.same_engine_sync):
                continue
            if self.waited[eng].get(k, 0) >= v:
                continue
            self.waited[eng][k] = v
            out[k] = v
            if not isinstance(k, tuple):
                self.needed[k].add(v)
        return out

    def _commit(self, tok, reads, writes):
        for r in reads:
            self.reads.setdefault(r, []).append(tok)
        for w in writes:
            self.last_write[w] = tok
            self.reads[w] = []

    def op(self, eng, fn, reads=(), writes=()):
        waits = self._deps(eng, reads, writes)
        self.count[eng] += 1
        tok = (eng, self.count[eng])
        self.queues[eng].append((waits, fn, tok))
        self._commit(tok, reads, writes)
        return tok

    def dma(self, q, fn, reads=(), writes=(), is_output=False):
        if q == "pool":
            slot = NDMA_SEMS - NSW_SEMS + self.dma_sw % NSW_SEMS
            self.dma_sw += 1
        else:
            slot = self.dma_n % (NDMA_SEMS - NSW_SEMS)
            self.dma_n += 1
        key = ("dma", slot)
        waits = self._deps(q, reads, writes)
        prev = self.dma_slot_last[slot]
        if prev and self.waited[q].get(key, 0) < prev:
            waits[key] = max(waits.get(key, 0), prev)
            self.waited[q][key] = prev
        val = prev + 16
        self.dma_slot_last[slot] = val
        tok = (key, val)
        self.queues[q].append((waits, fn, tok))
        self._commit(tok, reads, writes)
        if is_output:
            self.out_dma.append(tok)
        return tok

    def barrier(self):
        allw = {}
        for e in COMPUTE:
            if self.count[e]:
                allw[e] = self.count[e]
        for i in range(NDMA_SEMS):
            if self.dma_slot_last[i]:
                allw[("dma", i)] = self.dma_slot_last[i]
        for e in self.queues:
            p = self.pending[e]
            for k, v in allw.items():
                if p.get(k, 0) < v:
                    p[k] = v
        self.last_write = {}
        self.reads = {}

    def emit(self):
        nc = self.nc
        with contextlib.ExitStack() as es:
            sems = {}
            for e in COMPUTE:
                sems[e] = es.enter_context(nc.semaphore("s_" + e))
            for i in range(NDMA_SEMS):
                sems[("dma", i)] = es.enter_context(nc.semaphore("s_dma%d" % i))
            final_waits = {}
            for i in range(NDMA_SEMS):
                if self.dma_slot_last[i]:
                    final_waits[("dma", i)] = self.dma_slot_last[i]
            for e in COMPUTE:
                if self.count[e]:
                    self.needed[e].add(self.count[e])
                    final_waits[e] = self.count[e]
            rank = {}
            for e in COMPUTE:
                for i, v in enumerate(sorted(self.needed[e])):
                    rank[(e, v)] = i + 1
            block = es.enter_context(nc.Block())
            queues = self.queues

            def semval(k, v):
                return v if isinstance(k, tuple) else rank[(k, v)]

            def run(engobj, q, extra_final=False):
                for waits, fn, tok in queues[q]:
                    for k, v in waits.items():
                        engobj.wait_ge(sems[k], semval(k, v))
                    ins = fn(engobj)
                    k, v = tok
                    if isinstance(k, tuple):
                        ins.then_inc(sems[k], 16)
                    elif (k, v) in rank:
                        ins.then_inc(sems[k], 1)
                if extra_final:
                    for k, v in final_waits.items():
                        engobj.wait_ge(sems[k], semval(k, v))

            @block.sync
            def _(e):
                run(e, "sp", extra_final=True)

            @block.tensor
            def _(e):
                run(e, "pe")

            @block.scalar
            def _(e):
                run(e, "act")

            @block.vector
            def _(e):
                run(e, "dve")

            @block.gpsimd
            def _(e):
                run(e, "pool")


class Arena:
    def __init__(self, ap, nwords):
        self.ap = ap
        self.n = nwords
        self.off = 0
        self.mark = 0

    def set_mark(self):
        self.mark = self.off

    def reset(self):
        self.off = self.mark

    def f32(self, n):
        assert self.off + n <= self.n, ("arena overflow", self.off, n, self.n)
        a = self.ap[:, self.off:self.off + n]
        self.off += n
        return a

    def bf16(self, n):
        w = (n + 1) // 2
        assert self.off + w <= self.n, ("arena overflow", self.off, w, self.n)
        a = self.ap[:, self.off:self.off + w].bitcast(BF16)
        self.off += w
        return a[:, 0:n]


def build(NTO, PAST, phases="0ABCD"):
    NTA = 2 * NTO
    TP = NTA * T
    TO = NTO * T
    NPB = PAST // 128
    nc = bass.Bass("TRN2", target_bir_lowering=False)

    def din(name, shape, dt=F32):
        return nc.dram_tensor(name, list(shape), dt, kind="ExternalInput").ap()

    def dout(name, shape, dt=F32):
        return nc.dram_tensor(name, list(shape), dt, kind="ExternalOutput").ap()

    def dscr(name, shape, dt=F32):
        return nc.dram_tensor(name, list(shape), dt).ap()

    xp = din("xp", [TP, D]); xs = din("xs", [32, D]); cvec = din("cvec", [3, D])
    ck = din("ck", [2, PAST, 512]); cv = din("cv", [2, PAST, 512]); clf = din("clf", [2, PAST, NH])
    sconv = din("sconv", [2, 30, 512]); flagc = din("flagc", [128, 3 * NTO]); lmat = din("lmat", [NTA, NTA])
    w_ada = din("w_ada", [D, 9 * D]); b_ada = din("b_ada", [1, 9 * D])
    g3 = din("g3", [3, D])
    w_up = [din("w_up1", [D, 2 * DFF]), din("w_up2", [D, 2 * DFF])]
    w_dn = [din("w_down1", [DFF, D]), din("w_down2", [DFF, D])]
    w_in = din("w_in", [D, DIN]); b_f = din("b_f", [1, NH]); g_q = din("g_q", [1, DH]); g_k = din("g_k", [1, DH])
    conv_w = din("conv_w", [CW, 512]); cvec3 = din("cvec3", [3, 512])
    w_out = din("w_out", [D, D]); g_final = din("g_final", [1, D])

    yp = dout("yp", [TO, D]); ys = dout("ys", [32, D])
    kp = dout("kp", [TP, 512]); vp = dout("vp", [TP, 512]); lfp = dout("lfp", [TP, NH]); cvp = dout("cvp", [30, 512])
    ks = dout("ks", [32, 512]); vs = dout("vs", [32, 512]); lfs = dout("lfs", [32, NH]); cvs = dout("cvs", [2, 30, 512])

    TPX = TP + 32
    x1d = dscr("x1d", [TPX, D]); x2d = dscr("x2d", [TO + 32, D])
    kTd = dscr("kTd", [512, TPX], BF16); uTd = dscr("uTd", [512, TPX], BF16); qTd = dscr("qTd", [512, TO + 32], BF16)
    vSd = dscr("vSd", [TPX, 512], BF16); cumFd = dscr("cumFd", [TP, NH]); modrows = dscr("modrows", [3, 9 * D])

    S = Sched(nc)
    es = contextlib.ExitStack()
    with es:
        es.enter_context(nc.allow_low_precision("bf16 matmul operands, fp32 accumulate"))
        es.enter_context(nc.allow_non_contiguous_dma("small strided loads"))
        NW = 51200
        arena_t = es.enter_context(nc.sbuf_tensor("arena", [128, NW], F32))
        AR = Arena(arena_t, NW)
        PS = [es.enter_context(nc.psum_tensor("ps%d" % i, [128, 512], F32)) for i in range(8)]
        ps_ctr = [0]

        reserved = set()

        def psb():
            while True:
                i = ps_ctr[0] % 8
                ps_ctr[0] += 1
                if i not in reserved:
                    return PS[i], ("ps", i)

        def dma(q, out, in_, r=(), w=(), is_output=False):
            S.dma(q, lambda e: e.dma_start(out=out, in_=in_), reads=r, writes=w, is_output=is_output)

        def mm(out, lhsT, rhs, start, stop, r, w):
            S.op("pe", lambda e: e.matmul(out, lhsT=lhsT, rhs=rhs, start=start, stop=stop), reads=r, writes=w)

        def act(out, in_, func, r, w, bias=None, scale=None, accum_out=None):
            kw = {}
            if bias is not None:
                kw["bias"] = bias
            if scale is not None:
                kw["scale"] = scale
            if accum_out is not None:
                kw["accum_out"] = accum_out
            S.op("act", lambda e: e.activation(out=out, in_=in_, func=func, **kw), reads=r, writes=w)

        def tt(eng, out, in0, in1, op, r, w):
            S.op(eng, lambda e: e.tensor_tensor(out=out, in0=in0, in1=in1, op=op), reads=r, writes=w)

        def ts(eng, out, in0, s1, s2, op0, op1, r, w):
            if op1 is None:
                S.op(eng, lambda e: e.tensor_scalar(out=out, in0=in0, scalar1=s1, scalar2=s2, op0=op0), reads=r, writes=w)
            else:
                S.op(eng, lambda e: e.tensor_scalar(out=out, in0=in0, scalar1=s1, scalar2=s2, op0=op0, op1=op1), reads=r, writes=w)

        def cp(eng, out, in_, r, w):
            if eng == "act":
                S.op("act", lambda e: e.copy(out=out, in_=in_), reads=r, writes=w)
            else:
                S.op(eng, lambda e: e.tensor_copy(out=out, in_=in_), reads=r, writes=w)

        def memset(eng, ap, val, w):
            S.op(eng, lambda e: e.memset(ap, val), writes=w)

        def recip(out, in_, r, w):
            S.op("dve", lambda e: e.reciprocal(out=out, in_=in_), reads=r, writes=w)

        def rsqrt_small(out, in_, scale, r, w):
            act(out, in_, AF.Sqrt, r, w, bias=epsb[0:out.shape[0], 0:1], scale=scale)
            recip(out, out, w, w)

        ident = AR.bf16(128)
        identf = AR.f32(128)
        UT = AR.f32(128)
        onesf = AR.f32(128)
        masks = AR.bf16(4 * 512).rearrange("p (o t) -> p o t", o=4)
        modcol = AR.f32(72 * 3).rearrange("p (j s) -> p j s", s=3)
        gcol = AR.f32(8 * 3).rearrange("p (k s) -> p k s", s=3)
        gsc = [AR.f32(8 * 3).rearrange("p (k s) -> p k s", s=3) for _ in range(3)]
        ccol = AR.f32(4 * 3).rearrange("p (k s) -> p k s", s=3)
        wcol = AR.f32(4 * CW).rearrange("p (k j) -> p k j", k=4)
        flag_t = AR.f32(3 * NTO)
        bf_t = AR.f32(NH)
        gq_t = AR.f32(512)
        gk_t = AR.f32(512)
        cshift = AR.f32(1)
        epsb = AR.f32(1)
        accF = AR.f32(NH)
        AR.set_mark()

        memset("pool", identf, 0.0, ["identf"])
        S.op("pool", lambda e: e.affine_select(out=identf, in_=identf, pattern=[[-1, 128]], compare_op=ALU.not_equal,
                                               fill=1.0, base=0, channel_multiplier=1), reads=["identf"], writes=["identf"])
        cp("dve", ident, identf, ["identf"], ["ident"])
        memset("pool", UT, 1.0, ["UT"])
        S.op("pool", lambda e: e.affine_select(out=UT, in_=UT, pattern=[[1, 128]], compare_op=ALU.is_ge,
                                               fill=0.0, base=0, channel_multiplier=-1), reads=["UT"], writes=["UT"])
        memset("dve", onesf, 1.0, ["onesf"])
        memset("dve", epsb, EPS, ["epsb"])
        memset("dve", accF, 0.0, ["accF"])
        mtmp = AR.f32(512)
        for o in range(4):
            memset("pool", mtmp, 1.0, ["mtmp"])
            S.op("pool", lambda e, o=o: e.affine_select(out=mtmp, in_=mtmp, pattern=[[1, 512]], compare_op=ALU.is_ge,
                                                        fill=0.0, base=-128 * o, channel_multiplier=-1),
                 reads=["mtmp"], writes=["mtmp"])
            cp("pool", masks[:, o, :], mtmp, ["mtmp"], ["masks"])
        dma("sp", flag_t, flagc[:, :], w=["flag"])
        dma("sp", bf_t, b_f[0:1, :].partition_broadcast(128).rearrange("p a n -> p (a n)"), w=["bf_t"])
        gq64 = AR.f32(64)
        gk64 = AR.f32(64)
        dma("sp", gq64, g_q[0:1, :].partition_broadcast(128).rearrange("p a n -> p (a n)"), w=["gq64"])
        dma("sp", gk64, g_k[0:1, :].partition_broadcast(128).rearrange("p a n -> p (a n)"), w=["gk64"])
        cp("dve", gq_t.rearrange("p (h d) -> p h d", h=NH), gq64.unsqueeze(1).to_broadcast([128, NH, DH]), ["gq64"], ["gq_t"])
        cp("dve", gk_t.rearrange("p (h d) -> p h d", h=NH), gk64.unsqueeze(1).to_broadcast([128, NH, DH]), ["gk64"], ["gk_t"])
        gcolqk = AR.f32(2)
        AR.set_mark()
        wup_pre = AR.bf16(8 * 2 * DFF).rearrange("p (k n) -> p k n", k=8)
        if "A" in phases:
            wup_v0 = w_up[0].rearrange("(k p) n -> p k n", p=128)
            for gi, (f0, f1) in enumerate(WUP_GROUPS):
                for half in range(2):
                    c0, c1 = half * DFF + f0 * 128, half * DFF + f1 * 128
                    dma("pool", wup_pre[:, :, c0:c1], wup_v0[:, :, c0:c1], w=[("wup", gi)])
        grow2 = AR.f32(128)
        for j_ in range(2):
            dma("sp", grow2[0:1, 64 * j_:64 * j_ + 64], g_q[0:1, :], w=["grow2q"])
            dma("sp", grow2[32:33, 64 * j_:64 * j_ + 64], g_k[0:1, :], w=["grow2k"])
        pgq, pkgq = psb()
        mm(pgq[:, 0:1], grow2[0:1, :], identf[0:1, 0:1], True, True, ["grow2q", "identf"], [pkgq])
        mm(pgq[:, 1:2], grow2[32:33, :], onesf[32:33, 0:1], True, True, ["grow2k", "onesf"], [pkgq])
        cp("dve", gcolqk, pgq[:, 0:2], [pkgq], ["gcolqk"])
        mq = AR.f32(2)
        S.op("dve", lambda e: e.tensor_reduce(out=mq[:, 0:1], in_=gq64, axis=AX.X, op=ALU.max, apply_absolute_value=True),
             reads=["gq64"], writes=["mq"])
        S.op("dve", lambda e: e.tensor_reduce(out=mq[:, 1:2], in_=gk64, axis=AX.X, op=ALU.max, apply_absolute_value=True),
             reads=["gk64"], writes=["mq"])
        tt("dve", cshift, mq[:, 0:1], mq[:, 1:2], ALU.mult, ["mq"], ["cshift"])
        ts("dve", cshift, cshift, 8.0, None, ALU.mult, None, ["cshift"], ["cshift"])

        crow = AR.f32(D)
        cT = AR.bf16(8 * 3).rearrange("p (k s) -> p k s", s=3)
        mrow = AR.f32(9 * D)
        grow = AR.f32(D)
        c3row = AR.f32(512)
        cwrow = AR.f32(512)
        dma("sp", crow[0:3, :], cvec[:, :], w=["crow"])
        dma("sp", grow[0:3, :], g3[:, :], w=["grow"])
        dma("sp", c3row[0:3, :], cvec3[:, :], w=["c3row"])
        dma("sp", cwrow[0:CW, :], conv_w[:, :], w=["cwrow"])
        pst, pk = psb()
        for kc in range(8):
            mm(pst[:, kc * 3:kc * 3 + 3], crow[0:3, kc * 128:(kc + 1) * 128], identf[0:3, 0:3], True, True, ["crow", "identf"], [pk])
        act(cT, pst[:, 0:24].rearrange("p (k s) -> p k s", s=3), AF.Silu, [pk], ["cT"])
        pst, pk = psb()
        for kc in range(8):
            mm(pst[:, kc * 3:kc * 3 + 3], grow[0:3, kc * 128:(kc + 1) * 128], identf[0:3, 0:3], True, True, ["grow", "identf"], [pk])
        cp("dve", gcol, pst[:, 0:24].rearrange("p (k s) -> p k s", s=3), [pk], ["gcol"])
        pst, pk = psb()
        for kc in range(4):
            mm(pst[:, kc * 3:kc * 3 + 3], c3row[0:3, kc * 128:(kc + 1) * 128], identf[0:3, 0:3], True, True, ["c3row", "identf"], [pk])
        cp("dve", ccol, pst[:, 0:12].rearrange("p (k s) -> p k s", s=3), [pk], ["ccol"])
        pst, pk = psb()
        for kc in range(4):
            mm(pst[:, kc * CW:(kc + 1) * CW], cwrow[0:CW, kc * 128:(kc + 1) * 128], identf[0:CW, 0:CW], True, True, ["cwrow", "identf"], [pk])
        cp("dve", wcol, pst[:, 0:4 * CW].rearrange("p (k j) -> p k j", k=4), [pk], ["wcol"])

        wa_ring = [AR.bf16(8 * 512).rearrange("p (k n) -> p k n", k=8) for _ in range(3)]
        bch = [AR.f32(512) for _ in range(2)]
        w_ada_v = w_ada.rearrange("(k p) n -> p k n", p=128)
        for n in range(18):
            wr = wa_ring[n % 3]
            dma("pool", wr, w_ada_v[:, :, n * 512:(n + 1) * 512], w=[("wa", n % 3)])
            bb = bch[n % 2]
            dma("sp", bb[0:3, :], b_ada[0:1, n * 512:(n + 1) * 512].partition_broadcast(3).rearrange("p a n -> p (a n)"), w=[("bch", n % 2)])
            pst, pk = psb()
            for kc in range(8):
                mm(pst[0:3, :], cT[:, kc, :], wr[:, kc, :], kc == 0, kc == 7, ["cT", ("wa", n % 3)], [pk])
            tt("dve", mrow[0:3, n * 512:(n + 1) * 512], pst[0:3, :], bb[0:3, :], ALU.add, [pk, ("bch", n % 2)], ["mrow"])
        dma("sp", modrows[:, :], mrow[0:3, :], r=["mrow"], w=["modrows"])
        for jj in range(0, 72, 24):
            pst, pk = psb()
            for j in range(jj, jj + 24):
                mm(pst[:, (j - jj) * 3:(j - jj) * 3 + 3], mrow[0:3, j * 128:(j + 1) * 128], identf[0:3, 0:3], True, True, ["mrow", "identf"], [pk])
            cp("dve", modcol[:, jj:jj + 24, :], pst[:, 0:72].rearrange("p (j s) -> p j s", s=3), [pk], ["modcol"])
        for n in range(3):
            ts("dve", gsc[n], modcol[:, (3 * n + 1) * 8:(3 * n + 2) * 8, :], 1.0, None, ALU.add, None, ["modcol"], [("gsc", n)])
            tt("dve", gsc[n], gsc[n], gcol[:, :, n:n + 1].to_broadcast([128, 8, 3]), ALU.mult, [("gsc", n), "gcol"], [("gsc", n)])
        S.barrier()
        AR.reset()

        def shcol(n, kc, s):
            return modcol[:, 3 * n * 8 + kc, s:s + 1]

        def load_gate(gt, n, sample):
            c0 = (3 * n + 2) * D
            if not sample:
                dma("sp", gt, modrows[0:1, c0:c0 + D].partition_broadcast(128).rearrange("p a n -> p (a n)"), r=["modrows"], w=["gate"])
            else:
                for j in range(2):
                    dma("sp", gt[16 * j:16 * j + 16, :], modrows[1 + j:2 + j, c0:c0 + D].partition_broadcast(16).rearrange("p a n -> p (a n)"),
                        r=["modrows"], w=["gate"])

        def normA(xt, xk, nb, np_, scr):
            junk, ss, xn = scr
            for b in range(nb):
                act(junk[0:np_, :], xt[0:np_, b, :], AF.Square, [xk], ["junk", "ss"], accum_out=ss[0:np_, b:b + 1])
            rsqrt_small(ss[0:np_, 0:nb], ss[0:np_, 0:nb], 1.0 / D, ["ss"], ["ss"])
            for b in range(nb):
                act(xn[0:np_, b, :], xt[0:np_, b, :], AF.Identity, [xk, "ss"], [("xn", b)], scale=ss[0:np_, b:b + 1])

        def transp(nb, np_, n, sample, hT, scr, hkey="hT"):
            junk, ss, xn = scr
            ntok = (nb - 1) * 128 + np_
            segs = [(0, ntok, 0)] if not sample else [(0, 16, 1), (16, 32, 2)]
            for kc in range(8):
                pst, pk = psb()
                for b in range(nb):
                    mm(pst[:, b * 128:b * 128 + np_], xn[0:np_, b, kc * 128:(kc + 1) * 128], ident[0:np_, 0:np_], True, True,
                       [("xn", b), "ident"], [pk])
                for (c0, c1, s) in segs:
                    if kc % 2 == 0:
                        act(hT[:, kc, c0:c1], pst[:, c0:c1], AF.Identity, [pk], [hkey], bias=shcol(n, kc, s), scale=gsc[n][:, kc, s:s + 1])
                    else:
                        ts("dve", hT[:, kc, c0:c1], pst[:, c0:c1], gsc[n][:, kc, s:s + 1], shcol(n, kc, s), ALU.mult, ALU.add, [pk], [hkey])
            return ntok

        def ffn_phase(which, tiles, final_norm):
            n = 0 if which == 0 else 2
            AR.reset()
            wup = AR.bf16(8 * 2 * DFF).rearrange("p (k n) -> p k n", k=8)
            XB = [AR.f32(4 * D).rearrange("p (b d) -> p b d", b=4) for _ in range(2)]
            hT = AR.bf16(8 * T).rearrange("p (k t) -> p k t", k=8)
            actT = AR.bf16(NF * T).rearrange("p (f t) -> p f t", f=NF)
            gate = AR.f32(D)
            wdr = [AR.bf16(D) for _ in range(4)]
            junk = AR.bf16(D)
            ss = AR.f32(4)
            xn = AR.bf16(4 * D).rearrange("p (b d) -> p b d", b=4)
            sg = [AR.f32(T) for _ in range(2)]
            tmp = [AR.f32(512) for _ in range(2)]
            gfin = AR.f32(D) if final_norm else None
            wup_v = w_up[which].rearrange("(k p) n -> p k n", p=128)
            wdv = w_dn[which]
            if which != 0:
                for gi, (f0, f1) in enumerate(WUP_GROUPS):
                    for half in range(2):
                        c0, c1 = half * DFF + f0 * 128, half * DFF + f1 * 128
                        dma("pool", wup[:, :, c0:c1], wup_v[:, :, c0:c1], w=[("wup", gi)])
            if final_norm:
                dma("sp", gfin, g_final[0:1, :].partition_broadcast(128).rearrange("p a n -> p (a n)"), w=["gfin"])

            def load_x(i):
                src, _, nb, np_, sample = tiles[i]
                xb = XB[i % 2]
                if sample:
                    dma("sp", xb[0:np_, 0, :], src, w=[("xb", i % 2)])
                else:
                    dma("sp", xb[:, 0:nb, :], src.rearrange("(b p) d -> p b d", p=128), w=[("xb", i % 2)])

            ss2 = AR.f32(4)
            scr = (junk, ss, xn)
            load_x(0)
            normA(XB[0], ("xb", 0), tiles[0][2], tiles[0][3], scr)
            transp(tiles[0][2], tiles[0][3], n, tiles[0][4], hT, scr)
            cur_gate = None
            for i, (src, dst, nb, np_, sample) in enumerate(tiles):
                xb = XB[i % 2]
                xk = ("xb", i % 2)
                ntok = (nb - 1) * 128 + np_
                if i + 1 < len(tiles):
                    load_x(i + 1)
                if cur_gate != sample:
                    load_gate(gate, n, sample)
                    ts("dve", gate, gate, 0.5, None, ALU.mult, None, ["gate"], ["gate"])
                    cur_gate = sample
                for f in range(4):
                    dma("pool", wdr[f], wdv[f * 128:(f + 1) * 128, :], w=[("wd", f)])
                for f in range(NF):
                    pa, pka = psb()
                    pb, pkb = psb()
                    wk = ("wup", [gi for gi, (f0, f1) in enumerate(WUP_GROUPS) if f0 <= f < f1][0])
                    for kc in range(8):
                        mm(pa[:, 0:ntok], wup[:, kc, f * 128:(f + 1) * 128], hT[:, kc, 0:ntok], kc == 0, kc == 7, ["hT", wk], [pka])
                    for kc in range(8):
                        mm(pb[:, 0:ntok], wup[:, kc, DFF + f * 128:DFF + (f + 1) * 128], hT[:, kc, 0:ntok], kc == 0, kc == 7, ["hT", wk], [pkb])
                    sgb = sg[f % 2]
                    act(sgb[:, 0:ntok], pa[:, 0:ntok], AF.Silu, [pka], [("sg", f % 2)])
                    tt("dve", actT[:, f, 0:ntok], sgb[:, 0:ntok], pb[:, 0:ntok], ALU.mult, [("sg", f % 2), pkb], [("actT", f)])
                    if f == 10 and i + 1 < len(tiles):
                        normA(XB[(i + 1) % 2], ("xb", (i + 1) % 2), tiles[i + 1][2], tiles[i + 1][3], scr)
                if i + 1 < len(tiles):
                    transp(tiles[i + 1][2], tiles[i + 1][3], n, tiles[i + 1][4], hT, scr)
                accs = {}
                for b in range(nb):
                    for h in range(2):
                        accs[(b, h)] = psb()
                for f in range(NF):
                    for b in range(nb):
                        for h in range(2):
                            pacc, pkk = accs[(b, h)]
                            mm(pacc[0:np_, :], actT[:, f, b * 128:b * 128 + np_], wdr[f % 4][:, h * 512:(h + 1) * 512], f == 0, f == NF - 1,
                               [("actT", f), ("wd", f % 4)], [pkk])
                    if f + 4 < NF:
                        dma("pool", wdr[f % 4], wdv[(f + 4) * 128:(f + 5) * 128, :], w=[("wd", f % 4)])
                k = 0
                for b in range(nb):
                    for h in range(2):
                        pacc, pkk = accs[(b, h)]
                        tb = tmp[k % 2]
                        tt("dve", tb[0:np_, :], pacc[0:np_, :], gate[0:np_, h * 512:(h + 1) * 512], ALU.mult, [pkk, "gate"], [("tmp", k % 2)])
                        tt("pool", xb[0:np_, b, h * 512:(h + 1) * 512], tb[0:np_, :], xb[0:np_, b, h * 512:(h + 1) * 512], ALU.add,
                           [("tmp", k % 2), xk], [xk])
                        k += 1
                if final_norm:
                    for b in range(nb):
                        act(junk[0:np_, :], xb[0:np_, b, :], AF.Square, [xk], ["junk", "ss2"], accum_out=ss2[0:np_, b:b + 1])
                    rsqrt_small(ss2[0:np_, 0:nb], ss2[0:np_, 0:nb], 1.0 / D, ["ss2"], ["ss2"])
                    for b in range(nb):
                        S.op("dve", lambda e, b=b, xb=xb, np_=np_: e.scalar_tensor_tensor(
                            out=xb[0:np_, b, :], in0=xb[0:np_, b, :], scalar=ss2[0:np_, b:b + 1], in1=gfin[0:np_, :],
                            op0=ALU.mult, op1=ALU.mult), reads=[xk, "ss2", "gfin"], writes=[xk])
                if sample:
                    dma("sp", dst, xb[0:np_, 0, :], r=[xk], w=[("dst", i)], is_output=final_norm)
                else:
                    dma("sp", dst.rearrange("(b p) d -> p b d", p=128), xb[:, 0:nb, :], r=[xk], w=[("dst", i)], is_output=final_norm)
            S.barrier()

        if "A" in phases:
            tilesA = [(xp[t * T:(t + 1) * T, :], x1d[t * T:(t + 1) * T, :], 4, 128, False) for t in range(NTA)]
            tilesA.append((xs[:, :], x1d[TP:TP + 32, :], 1, 32, True))
            ffn_phase(0, tilesA, False)

        def qknorm(ps_t, pk, np_, gtile, sq, sqi, ssh, kn, kni, kb, idx, want_f32):
            act(sq[0:np_, :], ps_t[0:np_, :], AF.Square, [pk], [("sq", sqi)])
            S.op("dve", lambda e: e.tensor_reduce(out=ssh[0:np_, :], in_=sq[0:np_, :].rearrange("p (h d) -> p h d", h=NH),
                                                  axis=AX.X, op=ALU.add), reads=[("sq", sqi)], writes=[("ssh", idx)])
            rsqrt_small(ssh[0:np_, :], ssh[0:np_, :], 1.0 / DH, [("ssh", idx)], [("ssh", idx)])
            tt("dve", kb[0:np_, :].rearrange("p (h d) -> p h d", h=NH), ps_t[0:np_, :].rearrange("p (h d) -> p h d", h=NH),
               ssh[0:np_, :].unsqueeze(2).to_broadcast([np_, NH, DH]), ALU.mult, [pk, ("ssh", idx)], [("kb", idx)])
            if want_f32:
                tt("dve", kn[0:np_, :].rearrange("p (h d) -> p h d", h=NH), ps_t[0:np_, :].rearrange("p (h d) -> p h d", h=NH),
                   ssh[0:np_, :].unsqueeze(2).to_broadcast([np_, NH, DH]), ALU.mult, [pk, ("ssh", idx)], [("kn", kni)])
                tt("pool", kn[0:np_, :], kn[0:np_, :], gtile[0:np_, :], ALU.mult, [("kn", kni), "gq_t", "gk_t"], [("kn", kni)])

        lfsd = dscr("lfsd", [32, NH])

        def mix_setup(emit=True):
            wout = AR.bf16(8 * D).rearrange("p (k n) -> p k n", k=8)
            diag = AR.bf16(4 * CW * 128).rearrange("p (c j m) -> p c j m", c=4, j=CW)
            if emit:
                mix_emit_wout(wout)
                for c in range(4):
                    mix_emit_diag(diag, c)
            return wout, diag

        def mix_emit_wout(wout):
            dma("pool", wout, w_out.rearrange("(k p) n -> p k n", p=128), w=["wout"])

        def mix_emit_diag(diag, c):
            for j in range(CW):
                if j % 2 == 0:
                    ts("dve", diag[:, c, j, :], identf, wcol[:, c, j:j + 1], None, ALU.mult, None, ["identf", "wcol"], [("diag", c)])
                else:
                    act(diag[:, c, j, :], identf, AF.Identity, ["identf", "wcol"], [("diag", c)], scale=wcol[:, c, j:j + 1])

        def proj_phase():
            AR.reset()
            stage_c = "C" in phases
            if stage_c:
                wout_c, diag_c = mix_setup(emit=False)
            win = AR.bf16(8 * DIN).rearrange("p (k n) -> p k n", k=8)
            XB = [AR.f32(4 * D).rearrange("p (b d) -> p b d", b=4) for _ in range(2)]
            hTs = [AR.bf16(8 * T).rearrange("p (k t) -> p k t", k=8) for _ in range(2)]
            junk = AR.bf16(D)
            ss = AR.f32(4)
            xn = AR.bf16(4 * D).rearrange("p (b d) -> p b d", b=4)
            sq = [AR.f32(512) for _ in range(2)]
            ssh = [AR.f32(NH) for _ in range(4)]
            kn = [AR.f32(512) for _ in range(2)]
            kb = [AR.bf16(512) for _ in range(4)]
            vf = [AR.f32(512) for _ in range(2)]
            vb = [AR.bf16(512) for _ in range(2)]
            kT_sb = AR.bf16(4 * T).rearrange("p (c t) -> p c t", c=4)
            qT_sb = AR.bf16(4 * T).rearrange("p (c t) -> p c t", c=4)
            uT_sb = AR.bf16(4 * T).rearrange("p (c t) -> p c t", c=4)
            sgm = [AR.f32(T) for _ in range(2)]
            lx = AR.f32(4 * NH)
            la = AR.f32(4 * NH)
            lm = AR.f32(4 * NH)
            lf = AR.f32(4 * NH).rearrange("p (b h) -> p b h", b=4)
            cumF_sb = AR.f32(4 * NH)
            utk = AR.f32(512)
            win_v = w_in.rearrange("(k p) n -> p k n", p=128)
            for kc in range(8):
                dma("pool", win[:, kc, :], win_v[:, kc, :], w=[("win", kc)])
            wink = [("win", kc) for kc in range(8)]
            kTd_v = kTd.rearrange("(c p) t -> p c t", p=128)
            qTd_v = qTd.rearrange("(c p) t -> p c t", p=128)
            uTd_v = uTd.rearrange("(c p) t -> p c t", p=128)
            PSF = PS[7]
            pkf = ("ps", 7)
            reserved.add(7)
            psb7 = psb

            tiles = [(t * T, 4, 128, False, (t - NTO) if t >= NTO else None) for t in range(NTA)]
            tiles.append((TP, 1, 32, True, NTO))

            def load_x(i):
                row0, nb, np_, sample, _ = tiles[i]
                xb = XB[i % 2]
                if sample:
                    dma("sp", xb[0:np_, 0, :], x1d[row0:row0 + np_, :], w=[("xb", i % 2)])
                else:
                    dma("sp", xb[:, 0:nb, :], x1d[row0:row0 + nb * 128, :].rearrange("(b p) d -> p b d", p=128), w=[("xb", i % 2)])

            scr = (junk, ss, xn)
            load_x(0)
            normA(XB[0], ("xb", 0), tiles[0][1], tiles[0][2], scr)
            transp(tiles[0][1], tiles[0][2], 1, tiles[0][3], hTs[0], scr, ("hT", 0))
            cnt = [0]
            pending_cs = []
            for i, (row0, nb, np_, sample, own) in enumerate(tiles):
                xb = XB[i % 2]
                xk = ("xb", i % 2)
                hT = hTs[i % 2]
                hk = ("hT", i % 2)
                ntok = (nb - 1) * 128 + np_
                if i + 1 < len(tiles):
                    load_x(i + 1)
                fo = 128 * (i % 2)
                pkf = ("psf", i % 2)
                for b in range(nb):
                    for kc in range(8):
                        mm(PSF[0:np_, fo + b * NH:fo + (b + 1) * NH], hT[:, kc, b * 128:b * 128 + np_], win[:, kc, 1536:1536 + NH], kc == 0, kc == 7,
                           [hk, ("win", kc)], [pkf])
                kout, vout = (ks, vs) if sample else (kp, vp)
                pending = []

                def flush():
                    for (part, b_, ix_) in pending:
                        ptt, pkt = psb7()
                        for c in range(4):
                            mm(ptt[:, c * 128:c * 128 + np_], kb[ix_][0:np_, c * 128:(c + 1) * 128], ident[0:np_, 0:np_], True, True,
                               [("kb", ix_), "ident"], [pkt])
                        dst_sb = kT_sb if part == "k" else qT_sb
                        act(dst_sb[:, :, b_ * 128:b_ * 128 + np_], ptt[:, :].rearrange("p (c t) -> p c t", c=4)[:, :, 0:np_], AF.Identity,
                            [pkt, "gcolqk"], [("T" + part,)], scale=gcolqk[:, (0 if part == "q" else 1):(1 if part == "q" else 2)])
                    del pending[:]

                for b in range(nb):
                    r0 = 0 if sample else row0 + b * 128
                    parts = ["k", "v"] + (["q"] if own is not None else [])
                    newp = []
                    for part in parts:
                        c0 = {"q": 0, "k": 512, "v": 1024}[part]
                        pt_, pk_ = psb7()
                        for kc in range(8):
                            mm(pt_[0:np_, :], hT[:, kc, b * 128:b * 128 + np_], win[:, kc, c0:c0 + 512], kc == 0, kc == 7,
                               [hk, ("win", kc)], [pk_])
                        if part == "v":
                            ix = cnt[0] % 2
                            cnt[0] += 1
                            cp("act", vf[ix][0:np_, :], pt_[0:np_, :], [pk_], [("vf", ix)])
                            dma("sp", vout[r0:r0 + np_, :], vf[ix][0:np_, :], r=[("vf", ix)], w=[("vout", i, b)], is_output=True)
                            cp("pool", vb[ix][0:np_, :], vf[ix][0:np_, :], [("vf", ix)], [("vb", ix)])
                            dma("sp", vSd[row0 + b * 128:row0 + b * 128 + np_, :], vb[ix][0:np_, :], r=[("vb", ix)], w=[("vSd", i, b)])
                            continue
                        ix = (2 * b + (1 if part == "q" else 0)) % 4
                        qknorm(pt_, pk_, np_, gq_t if part == "q" else gk_t, sq[ix % 2], ix % 2, ssh[ix], kn[b % 2], b % 2, kb[ix], ix, part == "k")
                        if part == "k":
                            dma("sp", kout[r0:r0 + np_, :], kn[b % 2][0:np_, :], r=[("kn", b % 2)], w=[("kout", i, b)], is_output=True)
                        newp.append((part, b, ix))
                    flush()
                    pending.extend(newp)
                    if b == 0:
                        while pending_cs:
                            pending_cs.pop(0)()
                    if b == 0 and i + 1 < len(tiles):
                        normA(XB[(i + 1) % 2], ("xb", (i + 1) % 2), tiles[i + 1][1], tiles[i + 1][2], scr)
                uproj_pending = True
                u0 = 0 if (own is not None) else ntok - 128
                for c in range(4):
                    pa, pka = psb7()
                    pg, pkg = psb7()
                    for kc in range(8):
                        mm(pa[:, u0:ntok], win[:, kc, 1544 + c * 128:1544 + (c + 1) * 128], hT[:, kc, u0:ntok], kc == 0, kc == 7,
                           [hk, ("win", kc)], [pka])
                    for kc in range(8):
                        mm(pg[:, u0:ntok], win[:, kc, 2056 + c * 128:2056 + (c + 1) * 128], hT[:, kc, u0:ntok], kc == 0, kc == 7,
                           [hk, ("win", kc)], [pkg])
                    act(sgm[c % 2][:, u0:ntok], pg[:, u0:ntok], AF.Sigmoid, [pkg], [("sgm", c % 2)])
                    tt("dve", uT_sb[:, c, u0:ntok], pa[:, u0:ntok], sgm[c % 2][:, u0:ntok], ALU.mult, [pka, ("sgm", c % 2)], ["uT_sb"])
                flush()
                dma("sp", kTd_v[:, :, row0:row0 + ntok], kT_sb[:, :, 0:ntok], r=[("Tk",)], w=[("kTd", i)])
                if own is not None:
                    dma("sp", qTd_v[:, :, own * T:own * T + ntok], qT_sb[:, :, 0:ntok], r=[("Tq",)], w=[("qTd", i)])
                dma("sp", uTd_v[:, :, row0 + u0:row0 + ntok], uT_sb[:, :, u0:ntok], r=["uT_sb"], w=[("uTd", i)])
                if sample or i == NTA - 1:
                    bl = nb - 1
                    pa, pka = psb7()
                    pg, pkg = psb7()
                    for kc in range(8):
                        mm(pa[0:np_, :], hT[:, kc, bl * 128:bl * 128 + np_], win[:, kc, 1544:2056], kc == 0, kc == 7, [hk, ("win", kc)], [pka])
                    for kc in range(8):
                        mm(pg[0:np_, :], hT[:, kc, bl * 128:bl * 128 + np_], win[:, kc, 2056:2568], kc == 0, kc == 7, [hk, ("win", kc)], [pkg])
                    act(sgm[0][0:np_, :], pg[0:np_, :], AF.Sigmoid, [pkg], [("sgm", 0)])
                    tt("dve", utk[0:np_, :], pa[0:np_, :], sgm[0][0:np_, :], ALU.mult, [pka, ("sgm", 0)], ["utk"])
                    if sample:
                        for j in range(2):
                            dma("sp", cvs[j, 14:30, :], utk[16 * j:16 * j + 16, :], r=["utk"], w=[("cvs", j)], is_output=True)
                            dma("sp", cvs[j, 0:14, :], sconv[j, 16:30, :], w=[("cvs0", j)], is_output=True)
                    else:
                        dma("sp", cvp[:, :], utk[98:128, :], r=["utk"], w=["cvp"], is_output=True)
                nl = nb * NH
                tt("dve", lx[0:np_, 0:nl].rearrange("p (b h) -> p b h", h=NH), PSF[0:np_, fo:fo + nl].rearrange("p (b h) -> p b h", h=NH),
                   bf_t[0:np_, :].unsqueeze(1).to_broadcast([np_, nb, NH]), ALU.add, [pkf, "bf_t"], ["lx"])
                ts("dve", lm[0:np_, 0:nl], lx[0:np_, 0:nl], -1.0, None, ALU.mult, None, ["lx"], ["lm"])
                tt("dve", la[0:np_, 0:nl], lx[0:np_, 0:nl], lm[0:np_, 0:nl], ALU.max, ["lx", "lm"], ["la"])
                act(la[0:np_, 0:nl], la[0:np_, 0:nl], AF.Exp, ["la"], ["la"], scale=-1.0)
                act(la[0:np_, 0:nl], la[0:np_, 0:nl], AF.Ln, ["la"], ["la"], bias=onesf[0:np_, 0:1])
                ts("dve", lm[0:np_, 0:nl], lx[0:np_, 0:nl], 0.0, None, ALU.min, None, ["lx"], ["lm"])
                lf2 = lf[0:np_, 0:nb, :]
                tt("dve", lf2, lm[0:np_, 0:nl].rearrange("p (b h) -> p b h", h=NH), la[0:np_, 0:nl].rearrange("p (b h) -> p b h", h=NH),
                   ALU.subtract, ["lm", "la"], ["lf"])
                if sample:
                    dma("sp", lfs[:, :], lf[0:np_, 0, :], r=["lf"], w=["lfs"], is_output=True)
                    dma("sp", lfsd[:, :], lf[0:np_, 0, :], r=["lf"], w=["lfsd"])
                else:
                    dma("sp", lfp[row0:row0 + T, :].rearrange("(b p) h -> p b h", p=128), lf[:, 0:nb, :], r=["lf"], w=[("lfp", i)], is_output=True)
                    def cumsum_job(i=i, row0=row0, fo=fo, pkf=pkf, nb=nb):
                        memset("dve", accF, 0.0, ["accF"])
                        for b in range(nb):
                            mm(PSF[:, fo + 64 + b * NH:fo + 64 + (b + 1) * NH], UT, lf[:, b, :], True, False, ["UT", "lf"], [pkf])
                            mm(PSF[:, fo + 64 + b * NH:fo + 64 + (b + 1) * NH], onesf, accF, False, True, ["onesf", "accF"], [pkf])
                            tt("dve", accF, accF, lf[:, b, :], ALU.add, ["accF", "lf"], ["accF"])
                        cp("dve", cumF_sb, PSF[:, fo + 64:fo + 64 + 4 * NH], [pkf], ["cumF_sb"])
                        dma("sp", cumFd[row0:row0 + T, :].rearrange("(b p) h -> p b h", p=128), cumF_sb.rearrange("p (b h) -> p b h", h=NH),
                            r=["cumF_sb"], w=[("cumFd", i)])
                    pending_cs.append(cumsum_job)
                if i + 1 < len(tiles):
                    transp(tiles[i + 1][1], tiles[i + 1][2], 1, tiles[i + 1][3], hTs[(i + 1) % 2], scr, ("hT", (i + 1) % 2))
                if stage_c and len(tiles) >= 6:
                    if 1 <= i <= 4:
                        mix_emit_diag(diag_c, i - 1)
                    elif i == 5:
                        mix_emit_wout(wout_c)
            while pending_cs:
                pending_cs.pop(0)()
            if stage_c and len(tiles) < 6:
                mix_emit_wout(wout_c)
                for c in range(4):
                    mix_emit_diag(diag_c, c)
            reserved.clear()
            S.barrier()

        if "B" in phases:
            proj_phase()

        def conv_a(rhs_of, ntok, segs, diag, bufs, ukey):
            yf, y2, mean, ex2, rstd, t1 = bufs
            for c in range(4):
                pc, pkc = psb()
                for si, (c0, c1) in enumerate(segs):
                    for tap in range(CW):
                        mm(pc[:, c0:c1], diag[:, c, tap, :], rhs_of(c, si, tap), tap == 0, tap == CW - 1, [("diag", c), ukey], [pkc])
                act(yf[:, c, 0:ntok], pc[:, 0:ntok], AF.Identity, [pkc], [("yf", c)], bias=ccol[:, c, 0:1])
                act(y2[:, c, 0:ntok], yf[:, c, 0:ntok], AF.Square, [("yf", c)], [("y2", c)])

        def conv_b(ntok, catT, bufs):
            yf, y2, mean, ex2, rstd, t1 = bufs
            p1, pk1 = psb()
            p2, pk2 = psb()
            for c in range(4):
                mm(p1[:, 0:ntok], onesf, yf[:, c, 0:ntok], c == 0, c == 3, ["onesf", ("yf", c)], [pk1])
            for c in range(4):
                mm(p2[:, 0:ntok], onesf, y2[:, c, 0:ntok], c == 0, c == 3, ["onesf", ("y2", c)], [pk2])
            ts("dve", mean[:, 0:ntok], p1[:, 0:ntok], 1.0 / 512, None, ALU.mult, None, [pk1], ["mean"])
            ts("dve", ex2[:, 0:ntok], p2[:, 0:ntok], 1.0 / 512, None, ALU.mult, None, [pk2], ["ex2"])
            tt("pool", rstd[:, 0:ntok], mean[:, 0:ntok], mean[:, 0:ntok], ALU.mult, ["mean"], ["rstd"])
            tt("dve", rstd[:, 0:ntok], ex2[:, 0:ntok], rstd[:, 0:ntok], ALU.subtract, ["ex2", "rstd"], ["rstd"])
            act(rstd[:, 0:ntok], rstd[:, 0:ntok], AF.Ln, ["rstd"], ["rstd"], bias=epsb[:, 0:1], scale=1.0)
            act(rstd[:, 0:ntok], rstd[:, 0:ntok], AF.Exp, ["rstd"], ["rstd"], scale=-0.5)
            for c in range(4):
                tb = t1[c % 2]
                tt("dve", tb[:, 0:ntok], yf[:, c, 0:ntok], mean[:, 0:ntok], ALU.subtract, [("yf", c), "mean"], [("t1", c % 2)])
                tt("pool", tb[:, 0:ntok], tb[:, 0:ntok], rstd[:, 0:ntok], ALU.mult, [("t1", c % 2), "rstd"], [("t1", c % 2)])
                act(catT[:, 4 + c, 0:ntok], tb[:, 0:ntok], AF.Silu, [("t1", c % 2)], [("catT", 4 + c)], bias=ccol[:, c, 2:3], scale=ccol[:, c, 1:2])

        def attn_finish_group(items, catT, ncol, col0, recs, rshs):
            geo = []
            for i, (acc, pka, hl, cc) in enumerate(items):
                o0, d0 = (0, 64) if hl % 2 == 0 else (64, 0)
                geo.append((o0, d0))
                recip(recs[i][d0:d0 + 1, 0:ncol], acc[d0:d0 + 1, 0:ncol], [pka], [("rec", i)])
            for i, (acc, pka, hl, cc) in enumerate(items):
                o0, d0 = geo[i]
                pb_, pkb_ = psb()
                mm(pb_[:, 0:ncol], onesf[d0:d0 + 1, 0:128], recs[i][d0:d0 + 1, 0:ncol], True, True, ["onesf", ("rec", i)], [pkb_])
                cp("act", rshs[i][o0:o0 + 64, 0:ncol], pb_[o0:o0 + 64, 0:ncol], [pkb_], [("rsh", i)])
            for i, (acc, pka, hl, cc) in enumerate(items):
                o0, d0 = geo[i]
                tt("dve", catT[o0:o0 + 64, cc, col0:col0 + ncol], acc[o0:o0 + 64, 0:ncol], rshs[i][o0:o0 + 64, 0:ncol], ALU.mult,
                   [pka, ("rsh", i)], [("catT", cc)])

        def attn_finish_start(items, catT, accsb):
            jobs = []
            geo = []
            for i, (acc, pka, hl, cc) in enumerate(items):
                o0, d0 = (0, 64) if hl % 2 == 0 else (64, 0)
                geo.append((o0, d0))
                cp("act" if i % 2 == 0 else "dve", accsb[i], acc[:, 0:T], [pka], [("accsb", i)])
            for i, (acc, pka, hl, cc) in enumerate(items):
                o0, d0 = geo[i]
                recip(accsb[i][d0:d0 + 1, :], accsb[i][d0:d0 + 1, :], [("accsb", i)], [("accsb", i)])

                def job(i=i, o0=o0, d0=d0, cc=cc):
                    pb_, pkb_ = psb()
                    mm(pb_[:, 0:T], onesf[d0:d0 + 1, 0:128], accsb[i][d0:d0 + 1, :], True, True, ["onesf", ("accsb", i)], [pkb_])
                    tt("dve", catT[o0:o0 + 64, cc, 0:T], accsb[i][o0:o0 + 64, :], pb_[o0:o0 + 64, 0:T], ALU.mult,
                       [("accsb", i), pkb_], [("catT", cc)])
                jobs.append(job)
            return jobs

        def out_proj(catT, wout, xb, xk, nb, np_, gate, tmp):
            k = 0
            for b in range(nb):
                for h in range(2):
                    po, pko = psb()
                    for c in range(8):
                        mm(po[0:np_, :], catT[:, c, b * 128:b * 128 + np_], wout[:, c, h * 512:(h + 1) * 512], c == 0, c == 7,
                           [("catT", c), "wout"], [pko])
                    tb = tmp[k % 2]
                    tt("dve", tb[0:np_, :], po[0:np_, :], gate[0:np_, h * 512:(h + 1) * 512], ALU.mult, [pko, "gate"], [("tmp", k % 2)])
                    tt("pool", xb[0:np_, b, h * 512:(h + 1) * 512], tb[0:np_, :], xb[0:np_, b, h * 512:(h + 1) * 512], ALU.add,
                       [("tmp", k % 2), xk], [xk])
                    k += 1

        def mix_prompt():
            AR.reset()
            wout, diag = mix_setup(emit=("B" not in phases))
            NBK = NTA * 4
            Fg = AR.f32(NBK * NH).rearrange("p (j h) -> p j h", h=NH)
            Fst = AR.f32(NTO * NH).rearrange("p (m h) -> p m h", h=NH)
            Fen = AR.f32(NTO * NH).rearrange("p (m h) -> p m h", h=NH)
            Rc = AR.f32(NTO * NH).rearrange("p (m h) -> p m h", h=NH)
            gate = AR.f32(D)
            XB = [AR.f32(4 * D).rearrange("p (b d) -> p b d", b=4) for _ in range(2)]
            qT_sb = [AR.bf16(NH * T).rearrange("p (h t) -> p h t", h=NH) for _ in range(2)]
            UW = T + 32
            uh = [AR.bf16(4 * UW).rearrange("p (c t) -> p c t", c=4) for _ in range(2)]
            catT = AR.bf16(8 * T).rearrange("p (c t) -> p c t", c=8)
            halo = [AR.bf16(4 * 2 * 32).rearrange("p (c w t) -> p c w t", c=4, w=2)[:, :, :, 0:30] for _ in range(2)]
            kTs = [AR.bf16(2 * T).rearrange("p (c t) -> p c t", c=2) for _ in range(2)]
            Vst = [AR.bf16(4 * 256).rearrange("p (b f) -> p b f", b=4) for _ in range(2)]
            Vp = [AR.bf16(4 * 4 * 128).rearrange("p (b h m) -> p b h m", b=4, h=4) for _ in range(2)]
            pT = [AR.bf16(T) for _ in range(4)]
            bias = [AR.f32(4 * NH).rearrange("p (b h) -> p b h", h=NH) for _ in range(2)]
            accsb = [AR.f32(T) for _ in range(4)]
            fin_jobs = []
            yf = AR.f32(4 * T).rearrange("p (c t) -> p c t", c=4)
            y2 = AR.f32(4 * T).rearrange("p (c t) -> p c t", c=4)
            mean = AR.f32(T); ex2 = AR.f32(T); rstd = AR.f32(T)
            t1 = [AR.f32(T) for _ in range(2)]
            tmp = [AR.f32(512) for _ in range(2)]

            dma("sp", Fg, cumFd.rearrange("(j p) h -> p j h", p=128), w=["Fg"])
            own_v = cumFd[NTO * T:NTA * T, :].rearrange("(m t) h -> m t h", t=T)
            dma("sp", Fst, own_v[:, 0, :].partition_broadcast(128), w=["Fst"])
            dma("sp", Fen, own_v[:, T - 1, :].partition_broadcast(128), w=["Fen"])
            tot = AR.f32(NH)
            Lsb = AR.f32(NTA)
            rhs3 = AR.f32(NTA * NH)
            off = AR.f32(NTA * NH).rearrange("p (j h) -> p j h", h=NH)
            dma("sp", tot[0:NTA, :], cumFd.rearrange("(j t) h -> j t h", t=T)[:, T - 1, :], w=["tot"])
            dma("sp", Lsb[0:NTA, :], lmat[:, :], w=["Lsb"])
            tt("dve", rhs3[0:NTA, :].rearrange("p (j h) -> p j h", h=NH), Lsb[0:NTA, :].unsqueeze(2).to_broadcast([NTA, NTA, NH]),
               tot[0:NTA, :].unsqueeze(1).to_broadcast([NTA, NTA, NH]), ALU.mult, ["Lsb", "tot"], ["rhs3"])
            pof, pko = psb()
            mm(pof[:, 0:NTA * NH], onesf[0:NTA, :], rhs3[0:NTA, :], True, True, ["onesf", "rhs3"], [pko])
            cp("dve", off, pof[:, 0:NTA * NH].rearrange("p (j h) -> p j h", h=NH), [pko], ["off"])
            tt("dve", Fg.rearrange("p (j b) h -> p j b h", b=4), Fg.rearrange("p (j b) h -> p j b h", b=4),
               off.unsqueeze(2).to_broadcast([128, NTA, 4, NH]), ALU.add, ["Fg", "off"], ["Fg"])
            tt("dve", Fst, Fst, off[:, NTO:NTA, :], ALU.add, ["Fst", "off"], ["Fst"])
            tt("dve", Fen, Fen, off[:, NTO:NTA, :], ALU.add, ["Fen", "off"], ["Fen"])
            tt("dve", Rc, Fst, Fen, ALU.add, ["Fst", "Fen"], ["Rc"])
            ts("dve", Rc, Rc, 0.5, cshift[:, 0:1], ALU.mult, ALU.subtract, ["Rc", "cshift"], ["Rc"])
            load_gate(gate, 1, False)
            for vb_ in Vp:
                memset("pool", vb_, 1.0, ["Vp0", "Vp1"])
            for qi, qb_ in enumerate(qT_sb):
                memset("dve", qb_, 0.0, [("qT", qi)])
            qTd_h = qTd.rearrange("(c e d) t -> e d c t", e=2, d=64)
            kTd_v = kTd.rearrange("(c p) t -> p c t", p=128)
            qTd_v = qTd.rearrange("(c p) t -> p c t", p=128)
            uTd_v = uTd.rearrange("(c p) t -> p c t", p=128)
            for i in (0, 1, 2, 3):
                reserved.add(i)

            def load_tile(m):
                g = NTO + m
                dma("sp", XB[m % 2][:, :, :], x1d[g * T:(g + 1) * T, :].rearrange("(b p) d -> p b d", p=128), w=[("xb", m % 2)])
                q4 = qT_sb[m % 2].rearrange("p (c e) t -> p c e t", e=2)
                for e_ in range(2):
                    dma("sp", q4[64 * e_:64 * e_ + 64, :, e_, :], qTd_h[e_, :, :, m * T:(m + 1) * T], w=[("qT", m % 2)])
                dma("sp", uh[m % 2][:, :, 32:UW], uTd_v[:, :, g * T:(g + 1) * T], w=[("uh", m % 2)])
                po_ = (NTO + m - 1) if m >= 1 else 0
                dma("sp", halo[m % 2][:, :, 0, :], uTd_v[:, :, po_ * T + T - 30:(po_ + 1) * T], w=[("halo", m % 2)])
                dma("sp", halo[m % 2][:, :, 1, :], uTd_v[:, :, m * T + T - 30:(m + 1) * T], w=[("halo", m % 2)])

            load_tile(0)
            kvn = [0]
            ptn = [0]
            kvmap = {}
            cbufs = (yf, y2, mean, ex2, rstd, t1)

            def halo_blend(m):
                uhh = uh[m % 2]; uk = ("uh", m % 2)
                hl_ = halo[m % 2]
                ts("pool", hl_[:, :, 0, :], hl_[:, :, 0, :], flag_t[:, NTO + m:NTO + m + 1], None, ALU.mult, None, [("halo", m % 2), "flag"], [("halo", m % 2)])
                S.op("dve", lambda e: e.scalar_tensor_tensor(
                    out=uhh[:, :, 2:32], in0=hl_[:, :, 1, :], scalar=flag_t[:, 2 * NTO + m:2 * NTO + m + 1], in1=hl_[:, :, 0, :],
                    op0=ALU.mult, op1=ALU.add), reads=[("halo", m % 2), "flag"], writes=[uk])

            def conv_front(m):
                uhh = uh[m % 2]; uk = ("uh", m % 2)

                def rhs_of(c, si, tap):
                    return uhh[:, c, 2 + tap:2 + tap + T]
                conv_a(rhs_of, T, [(0, T)], diag, cbufs, uk)

            halo_blend(0)
            conv_front(0)
            for m in range(NTO):
                g = NTO + m
                xb = XB[m % 2]; xk = ("xb", m % 2)
                qt = qT_sb[m % 2]; qk = ("qT", m % 2)
                uhh = uh[m % 2]; uk = ("uh", m % 2)
                if m + 1 < NTO:
                    load_tile(m + 1)
                for hg in range(2):
                    accs = [(PS[hl], ("ps", hl)) for hl in range(4)]
                    def prep(p, hg=hg, m=m):
                        if (m, hg, p) in kvmap:
                            return
                        bi = kvn[0] % 2
                        kvn[0] += 1
                        kvmap[(m, hg, p)] = bi
                        kb_, vs_, vp_, bs_ = kTs[bi], Vst[bi], Vp[bi], bias[bi]
                        dma("sp", kb_, kTd_v[:, 2 * hg:2 * hg + 2, p * T:(p + 1) * T], w=[("kTs", bi)])
                        dma("sp", vs_, vSd[p * T:(p + 1) * T, hg * 256:(hg + 1) * 256].rearrange("(b s) f -> s b f", s=128), w=[("Vst", bi)])
                        vs5 = vs_.rearrange("p b (c e d) -> p b c e d", c=2, e=2)
                        vp5 = vp_.rearrange("p b (c e) m -> p b c e m", c=2)
                        cp("pool", vp5[:, :, :, 0, 0:64], vs5[:, :, :, 0, :], [("Vst", bi)], ["Vp%d" % bi])
                        tt("pool", bs_, Rc[:, m:m + 1, :].to_broadcast([128, 4, NH]), Fg[:, p * 4:(p + 1) * 4, :], ALU.subtract, ["Rc", "Fg"], [("bias", bi)])
                        if p == m:
                            ts("pool", bs_, bs_, flag_t[:, m:m + 1], None, ALU.add, None, [("bias", bi), "flag"], [("bias", bi)])
                        cp("pool", vp5[:, :, :, 1, 64:128], vs5[:, :, :, 1, :], [("Vst", bi)], ["Vp%d" % bi])

                    plist = list(range(m + 1)) + list(range(NTO, g + 1))
                    units = [(p, b, hl) for p in plist for b in range(4) for hl in range(4)]
                    ufirst = units[0][0]
                    stb = {}

                    def qk(u, hg=hg, qt=qt, qk_=qk, m_=m):
                        p, b, hl = u
                        prep(p)
                        bi = kvmap[(m_, hg, p)]
                        cc = hl // 2
                        r0 = (hl % 2) * 64
                        st, pks = psb()
                        stb[u] = (st, pks)
                        mm(st[:, 0:T], kTs[bi][:, cc, b * 128:(b + 1) * 128], qt[:, hg * 4 + hl, :], True, True,
                           [("kTs", bi), qk_], [pks])

                    def rest(u, hg=hg, g=g, ufirst=ufirst, m_=m):
                        p, b, hl = u
                        bi = kvmap[(m_, hg, p)]
                        h = hg * 4 + hl
                        st, pks = stb.pop(u)
                        pi = ptn[0] % 4
                        ptn[0] += 1
                        act(pT[pi], st[:, 0:T], AF.Exp, [pks, ("bias", bi)], [("pT", pi)], bias=bias[bi][:, b, h:h + 1], scale=0.125)
                        if p == g:
                            tt("dve", pT[pi], pT[pi], masks[:, b, :], ALU.mult, [("pT", pi), "masks"], [("pT", pi)])
                        acc, pka = accs[hl]
                        mm(acc[:, 0:T], Vp[bi][:, b, hl, :], pT[pi], (p == ufirst and b == 0), (p == g and b == 3),
                           ["Vp%d" % bi, ("pT", pi)], [pka])

                    LOOK = 3
                    for idx in range(min(LOOK, len(units))):
                        qk(units[idx])
                    for idx, u in enumerate(units):
                        rest(u)
                        if fin_jobs and idx in (6, 12, 18, 24):
                            fin_jobs.pop(0)()
                        if idx + LOOK < len(units):
                            qk(units[idx + LOOK])
                    while fin_jobs:
                        fin_jobs.pop(0)()
                    if hg == 0:
                        prep(0, hg=1, m=m)
                        if m + 1 < NTO:
                            halo_blend(m + 1)
                    elif m + 1 < NTO:
                        prep(0, hg=0, m=m + 1)
                    fin_jobs.extend(attn_finish_start([(accs[hl][0], accs[hl][1], hl, hg * 2 + hl // 2) for hl in range(4)], catT, accsb))
                conv_b(T, catT, cbufs)
                if m + 1 < NTO:
                    conv_front(m + 1)
                while fin_jobs:
                    fin_jobs.pop(0)()
                out_proj(catT, wout, xb, xk, 4, 128, gate, tmp)
                dma("sp", x2d[m * T:(m + 1) * T, :].rearrange("(b p) d -> p b d", p=128), xb[:, :, :], r=[xk], w=[("x2d", m)])
            reserved.clear()
            S.barrier()

        if "C" in phases:
            mix_prompt()

        def mix_sample():
            AR.reset()
            wout, diag = mix_setup(emit=("C" not in phases))
            gate = AR.f32(D)
            xb = AR.f32(D).rearrange("p (b d) -> p b d", b=1)
            qs = AR.bf16(4 * 32).rearrange("p (c t) -> p c t", c=4)
            kn_ = AR.bf16(4 * 32).rearrange("p (c t) -> p c t", c=4)
            us = AR.bf16(4 * 32).rearrange("p (c t) -> p c t", c=4)
            vnew = AR.bf16(512)
            VpN = AR.bf16(NH * 128).rearrange("p (h m) -> p h m", h=NH)
            uhs = AR.bf16(4 * 2 * 48).rearrange("p (c j t) -> p c j t", c=4, j=2)
            scf = AR.f32(512)
            scb = AR.bf16(512)
            catT = AR.bf16(8 * 32).rearrange("p (c t) -> p c t", c=8)
            ckbs = [AR.bf16(NPB * 512).rearrange("p (b f) -> p b f", b=NPB) for _ in range(2)]
            cvbs = [AR.bf16(NPB * 512).rearrange("p (b f) -> p b f", b=NPB) for _ in range(2)]
            lfcs = [AR.f32(NPB * NH).rearrange("p (b h) -> p b h", h=NH) for _ in range(2)]
            for j in range(2):
                dma("pool", ckbs[j], ck[j, :, :].rearrange("(b p) f -> p b f", p=128), w=[("ckb", j)])
                dma("pool", cvbs[j], cv[j, :, :].rearrange("(b p) f -> p b f", p=128), w=[("cvb", j)])
                dma("sp", lfcs[j], clf[j, :, :].rearrange("(b p) h -> p b h", p=128), w=[("lfc", j)])
            kTc = AR.bf16(4 * PAST).rearrange("p (c t) -> p c t", c=4)
            VpC = AR.bf16(NPB * NH * 128).rearrange("p (b h m) -> p b h m", b=NPB, h=NH)
            Fc = AR.f32(NPB * NH).rearrange("p (b h) -> p b h", h=NH)
            accS = AR.f32(NH)
            Rcs = AR.f32(NH)
            lfn = AR.f32(NH)
            Fn = AR.f32(NH)
            biasC = AR.f32(NPB * NH).rearrange("p (b h) -> p b h", h=NH)
            biasN = AR.f32(NH)
            pT = [AR.bf16(16) for _ in range(4)]
            rec = [AR.f32(16) for _ in range(2)]
            rsh = [AR.f32(16) for _ in range(2)]
            yf = AR.f32(4 * 32).rearrange("p (c t) -> p c t", c=4)
            y2 = AR.f32(4 * 32).rearrange("p (c t) -> p c t", c=4)
            mean = AR.f32(32); ex2 = AR.f32(32); rstd = AR.f32(32)
            t1 = [AR.f32(32) for _ in range(2)]
            tmp = [AR.f32(512) for _ in range(2)]
            kTd_v = kTd.rearrange("(c p) t -> p c t", p=128)
            qTd_v = qTd.rearrange("(c p) t -> p c t", p=128)
            uTd_v = uTd.rearrange("(c p) t -> p c t", p=128)

            load_gate(gate, 1, True)
            dma("sp", xb[0:32, 0, :], x1d[TP:TP + 32, :], w=["xb"])
            dma("sp", qs, qTd_v[:, :, TO:TO + 32], w=["qs"])
            dma("sp", kn_, kTd_v[:, :, TP:TP + 32], w=["kn_"])
            dma("sp", us, uTd_v[:, :, TP:TP + 32], w=["us"])
            memset("pool", VpC, 1.0, ["VpC"])
            for j in range(2):
                dma("sp", scf[0:30, :], sconv[j, :, :], w=["scf"])
                cp("dve", scb[0:30, :], scf[0:30, :], ["scf"], ["scb"])
                pst, pk = psb()
                for c in range(4):
                    mm(pst[:, c * 32:c * 32 + 30], scb[0:30, c * 128:(c + 1) * 128], ident[0:30, 0:30], True, True, ["scb", "ident"], [pk])
                cp("act", uhs[:, :, j, 0:30], pst[:, 0:128].rearrange("p (c t) -> p c t", c=4)[:, :, 0:30], [pk], ["uhs"])
                cp("dve", uhs[:, :, j, 30:46], us[:, :, 16 * j:16 * j + 16], ["us"], ["uhs"])

            def rhs_of(c, si, tap):
                return uhs[:, c, si, tap:tap + 16]
            conv_a(rhs_of, 32, [(0, 16), (16, 32)], diag, (yf, y2, mean, ex2, rstd, t1), "uhs")
            conv_b(32, catT, (yf, y2, mean, ex2, rstd, t1))
            for i in (0, 1):
                reserved.add(i)
            for j in range(2):
                ckb, cvb, lfc = ckbs[j], cvbs[j], lfcs[j]
                dma("sp", lfn[0:16, :], lfsd[16 * j:16 * j + 16, :], r=["lfsd"], w=["lfn"])
                dma("sp", vnew[0:16, :], vSd[TP + 16 * j:TP + 16 * j + 16, :], w=["vnew"])
                for blk in range(NPB):
                    pst, pk = psb()
                    for c in range(4):
                        mm(pst[:, c * 128:(c + 1) * 128], ckb[:, blk, c * 128:(c + 1) * 128], ident, True, True, [("ckb", j), "ident"], [pk])
                    cp("act" if blk % 2 == 0 else "dve", kTc[:, :, blk * 128:(blk + 1) * 128], pst[:, :].rearrange("p (c t) -> p c t", c=4), [pk], ["kTc"])
                    cv4 = cvb[:, blk, :].rearrange("p (c e d) -> p c e d", e=2, d=64)
                    vp4 = VpC[:, blk, :, :].rearrange("p (c e) m -> p c e m", e=2)
                    cp("pool", vp4[:, :, 0, 0:64], cv4[:, :, 0, :], [("cvb", j), "VpC"], ["VpC"])
                    cp("dve", vp4[:, :, 1, 64:128], cv4[:, :, 1, :], [("cvb", j), "VpC"], ["VpC"])
                memset("pool", VpN, 1.0, ["VpN"])
                for h in range(NH):
                    e0 = (h % 2) * 64
                    cp("pool", VpN[0:16, h, e0:e0 + 64], vnew[0:16, h * 64:(h + 1) * 64], ["vnew", "VpN"], ["VpN"])
                memset("dve", accS, 0.0, ["accS"])
                pcs, pkc = psb()
                for blk in range(NPB):
                    mm(pcs[:, blk * NH:(blk + 1) * NH], UT, lfc[:, blk, :], True, False, ["UT", ("lfc", j)], [pkc])
                    mm(pcs[:, blk * NH:(blk + 1) * NH], onesf, accS, False, True, ["onesf", "accS"], [pkc])
                    tt("dve", accS, accS, lfc[:, blk, :], ALU.add, ["accS", ("lfc", j)], ["accS"])
                cp("dve", Fc, pcs[:, 0:NPB * NH].rearrange("p (b h) -> p b h", h=NH), [pkc], ["Fc"])
                pr, pkr = psb()
                mm(pr[:, 0:NH], onesf, accS, True, True, ["onesf", "accS"], [pkr])
                mm(pr[0:16, NH:2 * NH], UT[0:16, 0:16], lfn[0:16, :], True, False, ["UT", "lfn"], [pkr])
                mm(pr[0:16, NH:2 * NH], onesf[:, 0:16], accS, False, True, ["onesf", "accS"], [pkr])
                ts("dve", Rcs, pr[:, 0:NH], cshift[:, 0:1], None, ALU.subtract, None, [pkr, "cshift"], ["Rcs"])
                cp("dve", Fn[0:16, :], pr[0:16, NH:2 * NH], [pkr], ["Fn"])
                tt("dve", biasC, Rcs.unsqueeze(1).to_broadcast([128, NPB, NH]), Fc, ALU.subtract, ["Rcs", "Fc"], ["biasC"])
                tt("dve", biasN[0:16, :], Rcs[0:16, :], Fn[0:16, :], ALU.subtract, ["Rcs", "Fn"], ["biasN"])
                pn = [0]
                units = [(h, blk) for h in range(NH) for blk in range(NPB + 1)]
                stb = {}

                def qk(u, j=j):
                    h, blk = u
                    cc = h // 2
                    r0 = (h % 2) * 64
                    qcol = qs[r0:r0 + 64, cc, 16 * j:16 * j + 16]
                    st, pks = psb()
                    stb[u] = (st, pks)
                    if blk < NPB:
                        mm(st[:, 0:16], kTc[r0:r0 + 64, cc, blk * 128:(blk + 1) * 128], qcol, True, True, ["kTc", "qs"], [pks])
                    else:
                        mm(st[0:16, 0:16], kn_[r0:r0 + 64, cc, 16 * j:16 * j + 16], qcol, True, True, ["kn_", "qs"], [pks])

                def rest(u, j=j):
                    h, blk = u
                    cc = h // 2
                    acc, pka = PS[h % 2], ("ps", h % 2)
                    st, pks = stb.pop(u)
                    pi = pn[0] % 4
                    pn[0] += 1
                    if blk < NPB:
                        act(pT[pi], st[:, 0:16], AF.Exp, [pks, "biasC"], [("pT", pi)], bias=biasC[:, blk, h:h + 1], scale=0.125)
                        mm(acc[:, 0:16], VpC[:, blk, h, :], pT[pi], blk == 0, False, ["VpC", ("pT", pi)], [pka])
                    else:
                        act(pT[pi][0:16, :], st[0:16, 0:16], AF.Exp, [pks, "biasN"], [("pT", pi)], bias=biasN[0:16, h:h + 1], scale=0.125)
                        tt("pool", pT[pi][0:16, :], pT[pi][0:16, :], masks[0:16, 0, 0:16], ALU.mult, [("pT", pi), "masks"], [("pT", pi)])
                        mm(acc[:, 0:16], VpN[0:16, h, :], pT[pi][0:16, :], False, True, ["VpN", ("pT", pi)], [pka])
                        attn_finish_group([(acc, pka, h, cc)], catT, 16, 16 * j, [rec[h % 2]], [rsh[h % 2]])

                LOOK = 3
                for idx in range(LOOK):
                    qk(units[idx])
                for idx, u in enumerate(units):
                    rest(u)
                    if idx + LOOK < len(units):
                        qk(units[idx + LOOK])
            reserved.clear()
            out_proj(catT, wout, xb, "xb", 1, 32, gate, tmp)
            dma("sp", x2d[TO:TO + 32, :], xb[0:32, 0, :], r=["xb"], w=["x2ds"])
            S.barrier()

        if "C" in phases:
            mix_sample()

        if "D" in phases:
            tilesD = [(x2d[m * T:(m + 1) * T, :], yp[m * T:(m + 1) * T, :], 4, 128, False) for m in range(NTO)]
            tilesD.append((x2d[TO:TO + 32, :], ys[:, :], 1, 32, True))
            ffn_phase(1, tilesD, True)

        S.emit()
    return nc


_CACHE = {}


def kernel(x_prompt, x_sample, c_prompt, c_sample, cache_k, cache_v, cache_logf, state_conv,
           w_ada, b_ada, g_ffn1, w_up1, w_down1, g_mix, w_in, b_f, g_q, g_k,
           conv_w, conv_b, conv_ln_g, conv_ln_b, w_out, g_ffn2, w_up2, w_down2, g_final, _phases="0ABCD"):
    f = lambda a: np.ascontiguousarray(np.asarray(a, dtype=np.float32))
    x_prompt = f(x_prompt); x_sample = f(x_sample)
    B, SEQ, _ = x_prompt.shape
    SB, SS, _ = x_sample.shape
    PAST = cache_k.shape[2]
    assert B * 2 == 8 and SB == 16 and SS == 16 and SEQ % (2 * T) == 0
    HALF = SEQ // 2
    NTO = HALF // T
    key = (NTO, PAST, _phases)
    if key not in _CACHE:
        _CACHE[key] = build(NTO, PAST, _phases)
    nc = _CACHE[key]
    shared = {
        "w_ada": f(w_ada)[0], "b_ada": f(b_ada), "g3": np.stack([f(g_ffn1)[0], f(g_mix)[0], f(g_ffn2)[0]]),
        "w_up1": f(w_up1)[0], "w_up2": f(w_up2)[0], "w_down1": f(w_down1)[0], "w_down2": f(w_down2)[0],
        "w_in": f(w_in)[0], "b_f": f(b_f), "g_q": f(g_q), "g_k": f(g_k), "conv_w": f(conv_w)[0],
        "cvec3": np.stack([f(conv_b)[0], f(conv_ln_g)[0], f(conv_ln_b)[0]]), "w_out": f(w_out)[0], "g_final": f(g_final),
    }
    cs_ = f(c_sample); cp_ = f(c_prompt)
    ck_ = f(cache_k)[0].reshape(SB, PAST, 512); cv_ = f(cache_v)[0].reshape(SB, PAST, 512)
    clf_ = f(cache_logf)[0]; sc_ = f(state_conv)[0]
    in_maps = []
    NTA = 2 * NTO
    own_t = {0: [t for t in range(NTA) if t % 4 in (0, 3)], 1: [t for t in range(NTA) if t % 4 in (1, 2)]}
    storage = {r: own_t[1 - r] + own_t[r] for r in (0, 1)}
    for core in range(8):
        b, r = core // 2, core % 2
        xt_ = x_prompt[b].reshape(NTA, T, D)
        m = dict(shared)
        m["xp"] = np.ascontiguousarray(xt_[storage[r]].reshape(NTA * T, D))
        glob = storage[r]
        lm = np.zeros((NTA, NTA), np.float32)
        for a_ in range(NTA):
            for c_ in range(NTA):
                lm[a_, c_] = 1.0 if glob[a_] < glob[c_] else 0.0
        m["lmat"] = lm
        m["xs"] = np.ascontiguousarray(x_sample[2 * core:2 * core + 2].reshape(32, D))
        m["cvec"] = np.ascontiguousarray(np.stack([cp_[b], cs_[2 * core], cs_[2 * core + 1]]))
        m["ck"] = np.ascontiguousarray(ck_[2 * core:2 * core + 2]); m["cv"] = np.ascontiguousarray(cv_[2 * core:2 * core + 2])
        m["clf"] = np.ascontiguousarray(clf_[2 * core:2 * core + 2]); m["sconv"] = np.ascontiguousarray(sc_[2 * core:2 * core + 2])
        fl = np.zeros((128, 3 * NTO), np.float32)
        for mm_ in range(NTO):
            o_, x_ = own_t[r][mm_], own_t[1 - r][mm_]
            fl[:, mm_] = NEGBIG if x_ > o_ else 0.0
            pred = o_ - 1
            if pred >= 0:
                if mm_ >= 1 and own_t[r][mm_ - 1] == pred:
                    fl[:, NTO + mm_] = 1.0
                else:
                    assert x_ == pred, (r, mm_, pred)
                    fl[:, 2 * NTO + mm_] = 1.0
        m["flagc"] = fl
        in_maps.append(m)
    res = run_bass_kernel_spmd(nc, in_maps, core_ids=list(range(8)))
    R = res.results
    y_prompt = np.zeros((B, SEQ, D), np.float32)
    k_prompt = np.zeros((1, B, SEQ, NH, DH), np.float32); v_prompt = np.zeros_like(k_prompt)
    logf_prompt = np.zeros((1, B, SEQ, NH), np.float32); conv_prompt = np.zeros((1, B, 30, 512), np.float32)
    y_sample = np.zeros((SB, SS, D), np.float32)
    k_sample = np.zeros((1, SB, SS, NH, DH), np.float32); v_sample = np.zeros_like(k_sample)
    logf_sample = np.zeros((1, SB, SS, NH), np.float32); conv_sample = np.zeros((1, SB, 30, 512), np.float32)
    for core in range(8):
        b, r = core // 2, core % 2
        o = R[core]
        yv = y_prompt[b].reshape(NTA, T, D)
        yv[own_t[r]] = o["yp"].reshape(NTO, T, D)
        if r == 1:
            inv = storage[r]
            k_prompt[0, b].reshape(NTA, T, NH, DH)[inv] = o["kp"].reshape(NTA, T, NH, DH)
            v_prompt[0, b].reshape(NTA, T, NH, DH)[inv] = o["vp"].reshape(NTA, T, NH, DH)
            logf_prompt[0, b].reshape(NTA, T, NH)[inv] = o["lfp"].reshape(NTA, T, NH)
        if own_t[r][-1] == NTA - 1:
            conv_prompt[0, b] = o["cvp"]
        sl = slice(2 * core, 2 * core + 2)
        y_sample[sl] = o["ys"].reshape(2, SS, D)
        k_sample[0, sl] = o["ks"].reshape(2, SS, NH, DH); v_sample[0, sl] = o["vs"].reshape(2, SS, NH, DH)
        logf_sample[0, sl] = o["lfs"].reshape(2, SS, NH); conv_sample[0, sl] = o["cvs"]
    return (y_prompt, y_sample, k_prompt, v_prompt, logf_prompt, conv_prompt, k_sample, v_sample, logf_sample, conv_sample)
```
